# Optimizing a Trainium2 kernel written in Bass

```python
import math
import jax
import jax.numpy as jnp
from jax import lax
import numpy as np

D_MODEL = 1024
BATCH = 8
SEQ = 8192
DEPTH = 4
DEC_BATCH = 4
DEC_SEQ = 4096
PAST_LEN = 128

EPS = 1e-6
D_HY = D_MODEL // 2
HY_GROUPS = 8
N_FILT = 2
SHORT_K = 3
EMB_DIM = 33
N_BANDS = (EMB_DIM - 1) // 2
FILT_HID = 64
FILT_TAP_SCALE = 0.04
DECAY_FAST = 0.3
DECAY_SLOW = 1.5
DECAY_TARGET = 1e-2
HY_COLS = (N_FILT + 1) * D_HY
N_HEADS = 8
NOPE_DIM = 64
ROPE_DIM = 32
V_DIM = 64
Q_LORA = 384
KV_LORA = 256
ROPE_THETA = 10000.0
Q_BLOCK = 128
POOL_WINDOWS = (2, 4, 8, 16)
N_POOL_GROUPS = 4
POOL_GRP = D_MODEL // N_POOL_GROUPS
D_FF = 4 * D_MODEL
IN_COLS = HY_COLS + Q_LORA + KV_LORA + ROPE_DIM
MIX_WIDTH = D_HY + N_HEADS * V_DIM
N_EVEN = (DEPTH + 1) // 2
N_ODD = DEPTH // 2

kernel_name = 'hybrid_hyena_mla_pool_encoder'


def rmsnorm(x, g):
    xf = x.astype(jnp.float32)
    y = xf * lax.rsqrt(jnp.mean(xf * xf, axis=-1, keepdims=True) + EPS)
    return (y * g.astype(jnp.float32)).astype(x.dtype)


def short_conv(u, w, b):
    up = jnp.pad(u, ((0, 0), (1, 1), (0, 0)))
    return up[:, :-2] * w[0] + up[:, 1:-1] * w[1] + up[:, 2:] * w[2] + b


def hyena_filters(L, w1, b1, freq, w2, b2, w3):
    f32 = jnp.float32
    t = jnp.linspace(0.0, 1.0, L, dtype=f32)[:, None]
    wpos = (2.0 * math.pi / L) * jnp.arange(L, dtype=f32)[:, None]
    bands = jnp.linspace(1e-4, N_BANDS - 1, N_BANDS, dtype=f32)[None, :]
    z = jnp.concatenate([t, jnp.cos(bands * wpos), -jnp.sin(bands * wpos)], axis=-1)
    fr = freq.astype(f32)
    h = jnp.sin(fr * (z @ w1.astype(f32) + b1.astype(f32)))
    h = jnp.sin(fr * (h @ w2.astype(f32) + b2.astype(f32)))
    h = (h @ w3.astype(f32)).reshape(L, 2, N_FILT, D_HY)
    max_decay = math.log(DECAY_TARGET) / DECAY_FAST
    min_decay = math.log(DECAY_TARGET) / DECAY_SLOW
    deltas = jnp.linspace(min_decay, max_decay, D_HY, dtype=f32)
    decay = jnp.exp(-t * jnp.abs(deltas))
    h = h * decay[:, None, None, :]
    h_fwd, h_bwd = h[:, 0], h[:, 1]
    k = jnp.concatenate([h_fwd, jnp.zeros((1, N_FILT, D_HY), f32), h_bwd[:0:-1]], axis=0)
    return jnp.fft.rfft(k, axis=0)


def fft_conv(u, kf, bias):
    L = u.shape[1]
    uf32 = u.astype(jnp.float32)
    uf = jnp.fft.rfft(uf32, n=2 * L, axis=1)
    y = jnp.fft.irfft(uf * kf[None], n=2 * L, axis=1)[:, :L]
    return (y + uf32 * bias.astype(jnp.float32)).astype(u.dtype)


def hyena(u_proj, conv_w, conv_b, w1, b1, freq, w2, b2, w3, hy_bias):
    L = u_proj.shape[1]
    u = short_conv(u_proj, conv_w, conv_b)
    x1, x2, v = jnp.split(u, 3, axis=-1)
    kf = hyena_filters(L, w1, b1, freq, w2, b2, w3)
    z = fft_conv(v, kf[:, 0], hy_bias[0]) * x1
    z = fft_conv(z, kf[:, 1], hy_bias[1]) * x2
    return z


def rope(x, cos, sin):
    x1, x2 = jnp.split(x, 2, axis=-1)
    return jnp.concatenate([x1 * cos - x2 * sin, x1 * sin + x2 * cos], axis=-1)


def mla(c_q, c_kv, k_r, q_norm, w_uq, kv_norm, w_ukv):
    B, L, _ = c_q.shape
    q = (rmsnorm(c_q, q_norm) @ w_uq).reshape(B, L, N_HEADS, NOPE_DIM + ROPE_DIM)
    kv = (rmsnorm(c_kv, kv_norm) @ w_ukv).reshape(B, L, N_HEADS, NOPE_DIM + V_DIM)
    q_n, q_r = q[..., :NOPE_DIM], q[..., NOPE_DIM:]
    k_n, v = kv[..., :NOPE_DIM], kv[..., NOPE_DIM:]
    inv_freq = 1.0 / (ROPE_THETA ** (jnp.arange(0, ROPE_DIM, 2, dtype=jnp.float32) / ROPE_DIM))
    ang = jnp.arange(L, dtype=jnp.float32)[:, None] * inv_freq[None, :]
    cos = jnp.cos(ang).astype(q.dtype)
    sin = jnp.sin(ang).astype(q.dtype)
    q_r = rope(q_r, cos[:, None, :], sin[:, None, :])
    k_r = rope(k_r, cos, sin)
    scale = (NOPE_DIM + ROPE_DIM) ** -0.5
    nb = L // Q_BLOCK
    qn_b = q_n.reshape(B, nb, Q_BLOCK, N_HEADS, NOPE_DIM).transpose(1, 0, 2, 3, 4)
    qr_b = q_r.reshape(B, nb, Q_BLOCK, N_HEADS, ROPE_DIM).transpose(1, 0, 2, 3, 4)

    def attend(blk):
        qn, qr = blk
        s = (jnp.einsum('bqhd,bkhd->bhqk', qn, k_n, preferred_element_type=jnp.float32)
             + jnp.einsum('bqhr,bkr->bhqk', qr, k_r, preferred_element_type=jnp.float32))
        p = jax.nn.softmax(s * scale, axis=-1)
        return jnp.einsum('bhqk,bkhd->bqhd', p.astype(v.dtype), v)

    o = lax.map(attend, (qn_b, qr_b))
    return o.transpose(1, 0, 2, 3, 4).reshape(B, L, N_HEADS * V_DIM)


def pool_mixer(u, pool_w, pool_scale):
    B, L, _ = u.shape
    uf = u.astype(jnp.float32)
    cs = jnp.pad(jnp.cumsum(uf, axis=1), ((0, 0), (1, 0), (0, 0)))
    pos = jnp.arange(L, dtype=jnp.int32)
    outs = []
    for g, w in enumerate(POOL_WINDOWS):
        sl = slice(g * POOL_GRP, (g + 1) * POOL_GRP)
        csg = cs[..., sl]
        lo = jnp.clip(pos - w // 2, 0, L)
        hi = jnp.clip(pos + w // 2, 0, L)
        mean = (jnp.take(csg, hi, axis=1) - jnp.take(csg, lo, axis=1)) / (hi - lo).astype(jnp.float32)[None, :, None]
        d = (mean - uf[..., sl]).astype(u.dtype)
        outs.append(d @ pool_w[g])
    return jnp.concatenate(outs, axis=-1) * pool_scale


def trunk(x, mix_norm, w_in, hy_conv_w, hy_conv_b, hf_w1, hf_b1, hf_freq, hf_w2, hf_b2, hf_w3, hy_bias,
          q_norm, w_uq, kv_norm, w_ukv, w_out, pool_w, pool_scale, mlp_norm, mlp_w1, mlp_w2, final_norm):
    for i in range(DEPTH):
        h = rmsnorm(x, mix_norm[i])
        if i % 2 == 0:
            e = i // 2
            p = h @ w_in[e]
            u_hy = p[..., :HY_COLS]
            c_q = p[..., HY_COLS:HY_COLS + Q_LORA]
            c_kv = p[..., HY_COLS + Q_LORA:HY_COLS + Q_LORA + KV_LORA]
            k_r = p[..., IN_COLS - ROPE_DIM:]
            y_hy = hyena(u_hy, hy_conv_w[e], hy_conv_b[e], hf_w1[e], hf_b1[e], hf_freq[e],
                         hf_w2[e], hf_b2[e], hf_w3[e], hy_bias[e])
            y_att = mla(c_q, c_kv, k_r, q_norm[e], w_uq[e], kv_norm[e], w_ukv[e])
            x = x + jnp.concatenate([y_hy, y_att], axis=-1) @ w_out[e]
        else:
            o = i // 2
            x = x + pool_mixer(h, pool_w[o], pool_scale[o])
        h = rmsnorm(x, mlp_norm[i])
        x = x + jnp.square(jax.nn.relu(h @ mlp_w1[i])) @ mlp_w2[i]
    return rmsnorm(x, final_norm)


def setup_inputs(seed: int = 0) -> dict:
    key = jax.random.key(seed)
    ks = jax.random.split(key, 24)

    def nrm(k, shape, s):
        return jax.random.normal(k, shape, jnp.float32) * s

    def gain(k, shape):
        return 1.0 + 0.1 * jax.random.normal(k, shape, jnp.float32)

    ne, no = N_EVEN, N_ODD
    return {
        'x_prompt': nrm(ks[0], (BATCH, SEQ, D_MODEL), 1.0),
        'x_sample': nrm(ks[1], (DEC_BATCH, DEC_SEQ, D_MODEL), 1.0),
        'mix_norm': gain(ks[2], (DEPTH, D_MODEL)),
        'w_in': nrm(ks[3], (ne, D_MODEL, IN_COLS), D_MODEL ** -0.5),
        'hy_conv_w': nrm(ks[4], (ne, SHORT_K, HY_COLS), SHORT_K ** -0.5),
        'hy_conv_b': nrm(ks[5], (ne, HY_COLS), 0.02),
        'hf_w1': nrm(ks[6], (ne, EMB_DIM, FILT_HID), EMB_DIM ** -0.5),
        'hf_b1': nrm(ks[7], (ne, FILT_HID), 0.1),
        'hf_freq': gain(ks[8], (ne, FILT_HID)),
        'hf_w2': nrm(ks[9], (ne, FILT_HID, FILT_HID), FILT_HID ** -0.5),
        'hf_b2': nrm(ks[10], (ne, FILT_HID), 0.1),
        'hf_w3': nrm(ks[11], (ne, FILT_HID, 2 * N_FILT * D_HY), FILT_TAP_SCALE * FILT_HID ** -0.5),
        'hy_bias': nrm(ks[12], (ne, N_FILT, D_HY), 0.3),
        'q_norm': gain(ks[13], (ne, Q_LORA)),
        'w_uq': nrm(ks[14], (ne, Q_LORA, N_HEADS * (NOPE_DIM + ROPE_DIM)), Q_LORA ** -0.5),
        'kv_norm': gain(ks[15], (ne, KV_LORA)),
        'w_ukv': nrm(ks[16], (ne, KV_LORA, N_HEADS * (NOPE_DIM + V_DIM)), KV_LORA ** -0.5),
        'w_out': nrm(ks[17], (ne, MIX_WIDTH, D_MODEL), MIX_WIDTH ** -0.5),
        'pool_w': nrm(ks[18], (no, N_POOL_GROUPS, POOL_GRP, POOL_GRP), POOL_GRP ** -0.5),
        'pool_scale': gain(ks[19], (no, D_MODEL)),
        'mlp_norm': gain(ks[20], (DEPTH, D_MODEL)),
        'mlp_w1': nrm(ks[21], (DEPTH, D_MODEL, D_FF), D_MODEL ** -0.5),
        'mlp_w2': nrm(ks[22], (DEPTH, D_FF, D_MODEL), D_FF ** -0.5),
        'final_norm': gain(ks[23], (D_MODEL,)),
    }


def reference(x_prompt, x_sample, mix_norm, w_in, hy_conv_w, hy_conv_b, hf_w1, hf_b1, hf_freq, hf_w2, hf_b2,
              hf_w3, hy_bias, q_norm, w_uq, kv_norm, w_ukv, w_out, pool_w, pool_scale, mlp_norm, mlp_w1,
              mlp_w2, final_norm):
    weights = (mix_norm, w_in, hy_conv_w, hy_conv_b, hf_w1, hf_b1, hf_freq, hf_w2, hf_b2, hf_w3, hy_bias,
               q_norm, w_uq, kv_norm, w_ukv, w_out, pool_w, pool_scale, mlp_norm, mlp_w1, mlp_w2, final_norm)
    y_prompt = trunk(x_prompt, *weights)
    y_sample = trunk(x_sample, *weights)
    return (y_prompt, y_sample)
```

```python
import math
from contextlib import ExitStack

import numpy as np
import ml_dtypes
import concourse.bass as bass
import concourse.mybir as mybir
from concourse.bass_utils import run_bass_kernel_spmd

F32 = mybir.dt.float32
BF16 = mybir.dt.bfloat16
I32 = mybir.dt.int32
AF = mybir.ActivationFunctionType
ALU = mybir.AluOpType
AX = mybir.AxisListType
NPBF = ml_dtypes.bfloat16

D = 1024
DFF = 4096
DHY = 512
HYC = 1536
QL = 384
KVL = 256
INC = 2208
NH = 8
EPS = 1e-6
ENGS = ("sp", "act", "dve", "pool", "pe")
TWO_PI = 2.0 * math.pi


class Sem:
    def __init__(self, h):
        self.h = h
        self.n = 0


class Buf:
    def __init__(self, ap=None):
        self.ap = ap
        self.w = {}
        self.r = {}


def _merge(dst, src):
    for s, v in src.items():
        if dst.get(s, 0) < v:
            dst[s] = v


class Prog:
    def __init__(self, nc, n_dma_sems=16):
        self.nc = nc
        self.stack = ExitStack()
        self.esem = {e: Sem(self.stack.enter_context(nc.semaphore("es_" + e))) for e in ("act", "dve", "pool", "pe")}
        self.dsem = {e: [Sem(self.stack.enter_context(nc.semaphore("ds_%s%d" % (e, i)))) for i in range(n_dma_sems)]
                     for e in ("sp", "act", "pool")}
        self.dcnt = {e: 0 for e in self.dsem}
        self.q = {e: [] for e in ENGS}
        self.pending = {e: {} for e in ENGS}
        self.pend_reads = {e: [] for e in ENGS}
        self.seen = {e: {} for e in ENGS}
        self.uid = 0

    def name(self, p):
        self.uid += 1
        return "%s_%d" % (p, self.uid)

    def all_sems(self):
        out = list(self.esem.values())
        for l in self.dsem.values():
            out.extend(l)
        return out

    def op(self, eng, fn, reads=(), writes=(), dma=False, inc=True):
        waits = dict(self.pending[eng])
        self.pending[eng] = {}
        for b in reads:
            _merge(waits, b.w)
        for b in writes:
            _merge(waits, b.w)
            _merge(waits, b.r)
        tok = None
        incinfo = None
        if dma:
            sems = self.dsem[eng]
            s = sems[self.dcnt[eng] % len(sems)]
            self.dcnt[eng] += 1
            if s.n > 0 and waits.get(s, 0) < s.n:
                waits[s] = s.n
            s.n += 16
            tok = {s: s.n}
            incinfo = (s, 16)
        elif inc:
            s = self.esem[eng]
            s.n += 1
            tok = {s: s.n}
            incinfo = (s, 1)
        self.q[eng].append((fn, waits, incinfo))
        if tok is None:
            self.pend_reads[eng].extend(reads)
            self.pend_reads[eng].extend(writes)
        else:
            for b in list(reads) + self.pend_reads[eng]:
                _merge(b.r, tok)
            self.pend_reads[eng] = []
            for b in writes:
                b.w = dict(tok)
                b.r = {}
        return tok

    def dma(self, eng, out_ap, in_ap, reads=(), writes=()):
        return self.op(eng, lambda e: e.dma_start(out=out_ap, in_=in_ap), reads=reads, writes=writes, dma=True)

    def barrier(self):
        allv = {s: s.n for s in self.all_sems() if s.n > 0}
        for e in ENGS:
            _merge(self.pending[e], allv)

    def flush(self, final=False):
        nc = self.nc
        q = self.q
        if final:
            allv = {s: s.n for s in self.all_sems() if s.n > 0}
        with nc.Block() as block:
            def run(eng, e):
                seen = self.seen[eng]
                own = self.esem.get(eng)
                for fn, waits, incinfo in q[eng]:
                    for s, v in waits.items():
                        if eng == "pe" and s is own:
                            continue
                        if seen.get(s, 0) >= v:
                            continue
                        e.wait_ge(s.h, v)
                        seen[s] = v
                    ins = fn(e)
                    if incinfo is not None:
                        ins.then_inc(incinfo[0].h, incinfo[1])
                if final and eng == "sp":
                    for s, v in allv.items():
                        if seen.get(s, 0) < v:
                            e.wait_ge(s.h, v)
                            seen[s] = v

            @block.sync
            def _(e):
                run("sp", e)

            @block.scalar
            def _(e):
                run("act", e)

            @block.vector
            def _(e):
                run("dve", e)

            @block.gpsimd
            def _(e):
                run("pool", e)

            @block.tensor
            def _(e):
                run("pe", e)
        self.q = {e: [] for e in ENGS}


class Phase:
    def __init__(self, P):
        self.P = P
        self.nc = P.nc
        self.es = ExitStack()

    def __enter__(self):
        self.es.__enter__()
        return self

    def __exit__(self, *a):
        if a[0] is None:
            self.P.barrier()
            self.P.flush()
        return self.es.__exit__(*a)

    def sb(self, shape, dt, name="t"):
        t = self.es.enter_context(self.nc.sbuf_tensor(self.P.name(name), list(shape), dt))
        return t

    def ps(self, shape, dt=F32, name="p"):
        t = self.es.enter_context(self.nc.psum_tensor(self.P.name(name), list(shape), dt))
        return t

    def sbufs(self, n, shape, dt, name="r"):
        return [Buf(self.sb(shape, dt, name)) for _ in range(n)]

    def psbufs(self, n, shape, dt=F32, name="pr"):
        return [Buf(self.ps(shape, dt, name)) for _ in range(n)]


class Ring:
    def __init__(self, bufs):
        self.bufs = bufs
        self.i = -1

    def next(self):
        self.i += 1
        return self.bufs[self.i % len(self.bufs)]


def load_cast(P, ph, dst_ap, src_ap, shape, stg_ring, dstbuf, eng_cycle, scale_ap=None, scale_buf=None):
    stg = stg_ring.next()
    sap = stg.ap
    idx = tuple([slice(0, shape[0])] + [slice(0, s) for s in shape[1:]])
    view = sap[idx]
    P.dma("sp", view, src_ap, writes=[stg])
    eng = eng_cycle[0]
    eng_cycle.append(eng_cycle.pop(0))
    rd = [stg] + ([scale_buf] if scale_buf is not None else [])
    if eng == "act":
        if scale_ap is None:
            P.op("act", lambda e: e.activation(out=dst_ap, in_=view, func=AF.Copy), reads=rd, writes=[dstbuf])
        else:
            P.op("act", lambda e: e.activation(out=dst_ap, in_=view, func=AF.Copy, scale=scale_ap), reads=rd,
                 writes=[dstbuf])
    else:
        if scale_ap is None:
            P.op(eng, lambda e: e.tensor_copy(out=dst_ap, in_=view), reads=rd, writes=[dstbuf])
        else:
            P.op(eng, lambda e: e.tensor_scalar(out=dst_ap, in0=view, scalar1=scale_ap, scalar2=None, op0=ALU.mult),
                 reads=rd, writes=[dstbuf])


class StatRing:
    def __init__(self, ph, n=8):
        self.ss = ph.sb([128, n], F32, "ss")
        self.rs = ph.sb([128, n], F32, "rs")
        self.sb_ = [Buf(self.ss[:, i:i + 1]) for i in range(n)]
        self.rb_ = [Buf(self.rs[:, i:i + 1]) for i in range(n)]
        self.i = -1
        self.n = n

    def next(self):
        self.i += 1
        k = self.i % self.n
        return self.sb_[k], self.rb_[k]


def rms_norm_to(P, out_ap, outbuf, x_ap, xbuf, n, junk, stat, eps_b, post=None):
    ssb, rsb = stat.next()
    npart = x_ap.shape[0]
    ss = ssb.ap[0:npart, :]
    rs = rsb.ap[0:npart, :]
    P.op("act", lambda e: e.activation(out=junk.ap[0:npart, 0:n], in_=x_ap, func=AF.Square, accum_out=ss),
         reads=[xbuf], writes=[junk, ssb])
    P.op("act", lambda e: e.activation(out=rs, in_=ss, func=AF.Sqrt, scale=1.0 / n, bias=eps_b.ap[0:npart, :]),
         reads=[ssb, eps_b], writes=[rsb])
    P.op("dve", lambda e: e.reciprocal(out=rs, in_=rs), reads=[rsb], writes=[rsb])
    if post is not None:
        pap, pbuf = post
        P.op("dve", lambda e: e.scalar_tensor_tensor(out=out_ap, in0=x_ap, scalar=rs, in1=pap, op0=ALU.mult, op1=ALU.mult),
             reads=[xbuf, rsb, pbuf], writes=[outbuf])
        return
    P.op("dve", lambda e: e.tensor_scalar(out=out_ap, in0=x_ap, scalar1=rs, scalar2=None, op0=ALU.mult),
         reads=[xbuf, rsb], writes=[outbuf])


def make_eps(P, ph):
    eb = Buf(ph.sb([128, 1], F32, "eps"))
    P.op("pool", lambda e: e.memset(eb.ap[:], EPS), writes=[eb])
    return eb


def load_ident(P, ph, consts):
    idf = Buf(ph.sb([128, 128], F32, "idf"))
    idb = Buf(ph.sb([128, 128], BF16, "idb"))
    P.dma("sp", idf.ap[:], consts["ident"][:, :], writes=[idf])
    P.op("dve", lambda e: e.tensor_copy(out=idb.ap[:], in_=idf.ap[:]), reads=[idf], writes=[idb])
    return idf, idb


def mm(P, out_ap, lhsT_ap, rhs_ap, start, stop, reads, writes, inc=True):
    return P.op("pe", lambda e: e.matmul(out_ap, lhsT_ap, rhs_ap, start=start, stop=stop), reads=reads,
                writes=writes, inc=inc)


def tr(P, out_ap, in_ap, ident_ap, reads, writes, inc=True):
    return P.op("pe", lambda e: e.transpose(out_ap, in_ap, ident_ap), reads=reads, writes=writes, inc=inc)


def act(P, out_ap, in_ap, func, reads, writes, scale=None, bias=None):
    kw = {}
    if scale is not None:
        kw["scale"] = scale
    if bias is not None:
        kw["bias"] = bias
    return P.op("act", lambda e: e.activation(out=out_ap, in_=in_ap, func=func, **kw), reads=reads, writes=writes)


def cp(P, eng, out_ap, in_ap, reads, writes):
    if eng == "act":
        return act(P, out_ap, in_ap, AF.Copy, reads, writes)
    return P.op(eng, lambda e: e.tensor_copy(out=out_ap, in_=in_ap), reads=reads, writes=writes)


def tt(P, eng, out_ap, in0, in1, op, reads, writes):
    return P.op(eng, lambda e: e.tensor_tensor(out=out_ap, in0=in0, in1=in1, op=op), reads=reads, writes=writes)


def ts(P, eng, out_ap, in0, s1, s2, op0, op1, reads, writes):
    if s2 is None:
        return P.op(eng, lambda e: e.tensor_scalar(out=out_ap, in0=in0, scalar1=s1, scalar2=None, op0=op0),
                    reads=reads, writes=writes)
    return P.op(eng, lambda e: e.tensor_scalar(out=out_ap, in0=in0, scalar1=s1, scalar2=s2, op0=op0, op1=op1),
                reads=reads, writes=writes)


def stt(P, eng, out_ap, in0, scalar, in1, op0, op1, reads, writes):
    return P.op(eng, lambda e: e.scalar_tensor_tensor(out=out_ap, in0=in0, scalar=scalar, in1=in1, op0=op0, op1=op1),
                reads=reads, writes=writes)


def phase_mlp(P, X, T, w1, w2, gT, consts, final=None):
    TT = 256
    NS = TT // 128
    with Phase(P) as ph:
        W1s = ph.sb([128, 8, DFF], BF16, "w1s")
        W2s = ph.sb([128, 32, D], BF16, "w2s")
        W1b = [Buf() for _ in range(8)]
        W2b = [Buf() for _ in range(16)]
        gtb = Buf(ph.sb([128, 8], F32, "gT"))
        P.dma("sp", gtb.ap[:], gT, writes=[gtb])
        with Phase(P) as ph2:
            stg = Ring(ph2.sbufs(3, [128, 2048], F32, "stg"))
            cyc = ["act", "dve"]
            for kc in range(8):
                for hf in range(2):
                    load_cast(P, ph2, W1s[:, kc, hf * 2048:(hf + 1) * 2048], w1[kc * 128:(kc + 1) * 128, hf * 2048:(hf + 1) * 2048],
                              [128, 2048], stg, W1b[kc], cyc, scale_ap=gtb.ap[:, kc:kc + 1], scale_buf=gtb)
            w2v = w2.rearrange("(fc p) d -> p fc d", p=128)
            for f2 in range(16):
                st = stg.next()
                v = st.ap[:].rearrange("p (a d) -> p a d", a=2)
                P.dma("sp", v, w2v[:, 2 * f2:2 * f2 + 2, :], writes=[st])
                eng = cyc[0]
                cyc.append(cyc.pop(0))
                cp(P, eng, W2s[:, 2 * f2:2 * f2 + 2, :], v, [st], [W2b[f2]])
        idf, idb = load_ident(P, ph, consts)
        epsb = make_eps(P, ph)
        xt_r = Ring(ph.sbufs(3 * NS, [128, D], F32, "xt"))
        hb_r = Ring(ph.sbufs(2, [128, D], BF16, "hb"))
        junk = Buf(ph.sb([128, D], F32, "junk"))
        stat = StatRing(ph)
        hT_r = Ring(ph.sbufs(2, [128, 8, TT], BF16, "hT"))
        aT = ph.sb([128, 32, TT], BF16, "aT")
        aTb = [Buf() for _ in range(16)]
        tmp_r = Ring(ph.sbufs(3, [128, 2, TT], BF16, "rl"))
        xo_r = Ring(ph.sbufs(2, [128, D], F32, "xo"))
        tp_r = Ring(ph.psbufs(2, [128, 8, 128], BF16, "tp"))
        up_r = Ring(ph.psbufs(2, [128, 2, TT], F32, "up"))
        dn_r = Ring(ph.psbufs(4, [128, 512], F32, "dn"))
        if final is not None:
            gfb = Buf(ph.sb([128, D], F32, "gfin"))
            P.dma("sp", gfb.ap[:], final[0], writes=[gfb])
            yo_r = Ring(ph.sbufs(1, [128, D], F32, "yo"))
        ntile = T // TT
        xts = {}

        def issue_load(j):
            for s in range(NS):
                xb = xt_r.next()
                t0 = j * TT + s * 128
                P.dma("sp", xb.ap[:], X[t0:t0 + 128, :], writes=[xb])
                xts[(j, s)] = xb

        hTs = {}
        hbs = {}

        def prep_norm(j):
            for s in range(NS):
                xb = xts[(j, s)]
                hb = hb_r.next()
                rms_norm_to(P, hb.ap[:], hb, xb.ap[:], xb, D, junk, stat, epsb)
                hbs[(j, s)] = hb

        def prep_tr(j):
            hT = hT_r.next()
            hTs[j] = hT
            for s in range(NS):
                hb = hbs.pop((j, s))
                tp = tp_r.next()
                for kc in range(8):
                    tr(P, tp.ap[:, kc, :], hb.ap[:, kc * 128:(kc + 1) * 128], idb.ap[:], [hb, idb], [tp], inc=(kc == 7))
                cp(P, "act" if s == 0 else "dve", hT.ap[:, :, s * 128:(s + 1) * 128], tp.ap[:], [tp], [hT])

        def prep(j):
            prep_norm(j)
            prep_tr(j)

        issue_load(0)
        if ntile > 1:
            issue_load(1)
        prep(0)
        for j in range(ntile):
            hT = hTs.pop(j)
            for mp in range(16):
                up = up_r.next()
                for mi in range(2):
                    m = 2 * mp + mi
                    for kc in range(8):
                        mm(P, up.ap[:, mi, :], W1s[:, kc, m * 128:(m + 1) * 128], hT.ap[:, kc, :], kc == 0, kc == 7,
                           [W1b[kc], hT], [up], inc=(kc == 7 and mi == 1))
                tmp = tmp_r.next()
                act(P, tmp.ap[:], up.ap[:], AF.Relu, [up], [tmp])
                tt(P, "pool", aT[:, 2 * mp:2 * mp + 2, :], tmp.ap[:], tmp.ap[:], ALU.mult, [tmp], [aTb[mp]])
            if j + 2 < ntile:
                issue_load(j + 2)
            if j + 1 < ntile:
                prep_norm(j + 1)
            for s in range(NS):
                if s == 1 and j + 1 < ntile:
                    prep_tr(j + 1)
                xb = xts.pop((j, s))
                xo = xo_r.next()
                t0 = j * TT + s * 128
                for cc in range(2):
                    dn = dn_r.next()
                    for fc in range(32):
                        mm(P, dn.ap[:], aT[:, fc, s * 128:(s + 1) * 128], W2s[:, fc, cc * 512:(cc + 1) * 512],
                           fc == 0, fc == 31, [aTb[fc // 2], W2b[fc // 2]], [dn], inc=(fc == 31))
                    tt(P, "dve", xo.ap[:, cc * 512:(cc + 1) * 512], dn.ap[:], xb.ap[:, cc * 512:(cc + 1) * 512], ALU.add,
                       [dn, xb], [xo])
                if final is None:
                    P.dma("pool", X[t0:t0 + 128, :], xo.ap[:], reads=[xo])
                else:
                    yo = yo_r.next()
                    rms_norm_to(P, yo.ap[:], yo, xo.ap[:], xo, D, junk, stat, epsb, post=(gfb.ap[:], gfb))
                    for (r0, nr, oap) in final[1]:
                        if r0 <= t0 < r0 + nr:
                            P.dma("pool", oap[t0 - r0:t0 - r0 + 128, :], yo.ap[:], reads=[yo])


def dram_ap(base, offset_elems, dims):
    return bass.AP(tensor=base.tensor, offset=base.offset + offset_elems, ap=[list(d) for d in dims])


def phase_pool(P, X, seqs, pw, gT, psb, consts):
    with Phase(P) as ph:
        Wp = ph.sb([128, 8, 256], BF16, "wp")
        Wpb = [Buf() for _ in range(8)]
        gtb = Buf(ph.sb([128, 8], F32, "gT"))
        psc = Buf(ph.sb([128, D], F32, "psc"))
        MT = Buf(ph.sb([128, 4, 5, 128], BF16, "MT"))
        P.dma("sp", gtb.ap[:], gT, writes=[gtb])
        P.dma("sp", psc.ap[:], psb, writes=[psc])
        P.dma("sp", MT.ap[:], consts["poolM"], writes=[MT])
        stg = Ring(ph.sbufs(3, [128, 256], F32, "stg"))
        for cc in range(8):
            g, hf = cc // 2, cc % 2
            st = stg.next()
            P.dma("sp", st.ap[:], pw[g, hf * 128:(hf + 1) * 128, :], writes=[st])
            stt(P, "dve", Wp[:, cc, :], st.ap[:], gtb.ap[:, cc:cc + 1], psc.ap[:, g * 256:(g + 1) * 256], ALU.mult, ALU.mult,
                [st, gtb, psc], [Wpb[cc]])
        epsb = make_eps(P, ph)
        xt_r = Ring(ph.sbufs(4, [128, D], F32, "xt"))
        xr_r = Ring(ph.sbufs(4, [128, D], BF16, "xr"))
        junk = Buf(ph.sb([128, D], F32, "junk"))
        stat = StatRing(ph)
        dTs_t = [ph.sb([128, 8, 128], BF16, "dTs") for _ in range(2)]
        dTs_b = [[Buf(), Buf()] for _ in range(2)]
        xo_r = Ring(ph.sbufs(2, [128, D], F32, "xo"))
        dT_r = Ring(ph.psbufs(4, [128, 4, 128], F32, "dT"))
        po_r = Ring(ph.psbufs(4, [128, 512], F32, "po"))
        it = 0
        for (s0, L) in seqs:
            nt = L // 128
            xts, xrs = {}, {}
            for i in range(nt + 1):
                if i < nt:
                    xb = xt_r.next()
                    P.dma("sp", xb.ap[:], X[s0 + i * 128:s0 + (i + 1) * 128, :], writes=[xb])
                    xr = xr_r.next()
                    rms_norm_to(P, xr.ap[:], xr, xb.ap[:], xb, D, junk, stat, epsb)
                    xts[i], xrs[i] = xb, xr
                if i >= 1:
                    o = i - 1
                    srcs = []
                    if o >= 1:
                        srcs.append((xrs[o - 1], 0))
                    srcs.append((xrs[o], 3 if o == 0 else (4 if o == nt - 1 else 1)))
                    if o + 1 < nt:
                        srcs.append((xrs[o + 1], 2))
                    k = it % 2
                    it += 1
                    dTs = dTs_t[k]
                    for b in range(2):
                        dT = dT_r.next()
                        for ci in range(4):
                            cc = b * 4 + ci
                            g = cc // 2
                            for si, (xr, var) in enumerate(srcs):
                                mm(P, dT.ap[:, ci, :], xr.ap[:, cc * 128:(cc + 1) * 128], MT.ap[:, g, var, :], si == 0,
                                   si == len(srcs) - 1, [xr, MT], [dT], inc=(ci == 3 and si == len(srcs) - 1))
                        cp(P, "act" if b == 0 else "dve", dTs[:, b * 4:(b + 1) * 4, :], dT.ap[:], [dT], [dTs_b[k][b]])
                    xo = xo_r.next()
                    xb = xts.pop(o)
                    for pb in range(2):
                        po = po_r.next()
                        for gi in range(2):
                            g = pb * 2 + gi
                            for hf in range(2):
                                cc = g * 2 + hf
                                mm(P, po.ap[:, gi * 256:(gi + 1) * 256], dTs[:, cc, :], Wp[:, cc, :], hf == 0, hf == 1,
                                   [dTs_b[k][cc // 4], Wpb[cc]], [po], inc=(gi == 1 and hf == 1))
                        tt(P, "dve", xo.ap[:, pb * 512:(pb + 1) * 512], po.ap[:], xb.ap[:, pb * 512:(pb + 1) * 512], ALU.add,
                           [po, xb], [xo])
                    P.dma("pool", X[s0 + o * 128:s0 + (o + 1) * 128, :], xo.ap[:], reads=[xo])
                    if o >= 1:
                        xrs.pop(o - 1)


def phase_wout(P, X, T, ZH, OT, wout, consts):
    TT = 512
    with Phase(P) as ph:
        Wo = ph.sb([128, 8, D], BF16, "wo")
        Wob = [Buf() for _ in range(8)]
        with Phase(P) as ph2:
            stg = Ring(ph2.sbufs(3, [128, D], F32, "stg"))
            cyc = ["act", "dve", "pool"]
            for kc in range(8):
                load_cast(P, ph2, Wo[:, kc, :], wout[kc * 128:(kc + 1) * 128, :], [128, D], stg, Wob[kc], cyc)
        idf, idb = load_ident(P, ph, consts)
        mix_r = Ring(ph.sbufs(2, [128, 8, TT], BF16, "mixT"))
        zh_r = Ring(ph.sbufs(3, [128, 512], BF16, "zh"))
        xt_r = Ring(ph.sbufs(3, [128, D], F32, "xt"))
        xo_r = Ring(ph.sbufs(2, [128, D], F32, "xo"))
        tp_r = Ring(ph.psbufs(2, [128, 4, 128], BF16, "tp"))
        dn_r = Ring(ph.psbufs(4, [128, 512], F32, "dn"))
        OTv = OT.rearrange("(kc p) t -> p kc t", p=128)
        for j in range(T // TT):
            t0 = j * TT
            mix = mix_r.next()
            P.dma("sp", mix.ap[:, 4:8, :], OTv[:, :, t0:t0 + TT], writes=[mix])
            for s in range(4):
                zh = zh_r.next()
                P.dma("sp", zh.ap[:], ZH[t0 + s * 128:t0 + (s + 1) * 128, :], writes=[zh])
                tp = tp_r.next()
                for kc in range(4):
                    tr(P, tp.ap[:, kc, :], zh.ap[:, kc * 128:(kc + 1) * 128], idb.ap[:], [zh, idb], [tp], inc=(kc == 3))
                cp(P, "act", mix.ap[:, 0:4, s * 128:(s + 1) * 128], tp.ap[:], [tp], [mix])
            for s in range(4):
                r0 = t0 + s * 128
                xb = xt_r.next()
                P.dma("sp", xb.ap[:], X[r0:r0 + 128, :], writes=[xb])
                xo = xo_r.next()
                for cc in range(2):
                    dn = dn_r.next()
                    for kc in range(8):
                        mm(P, dn.ap[:], mix.ap[:, kc, s * 128:(s + 1) * 128], Wo[:, kc, cc * 512:(cc + 1) * 512], kc == 0,
                           kc == 7, [mix, Wob[kc]], [dn], inc=(kc == 7))
                    tt(P, "dve", xo.ap[:, cc * 512:(cc + 1) * 512], dn.ap[:], xb.ap[:, cc * 512:(cc + 1) * 512], ALU.add,
                       [dn, xb], [xo])
                P.dma("pool", X[r0:r0 + 128, :], xo.ap[:], reads=[xo])


def phase_attn(P, seqs, QT, KT, KR, V, OT, RCD, side=None):
    rcd_r = Ring([Buf(RCD[i:i + 1, :]) for i in range(RCD.shape[0])])
    for qi_, (s0, L) in enumerate(seqs):
        nkb = L // 128
        npair = nkb // 2
        with Phase(P) as ph:
            gen = side(ph) if (side is not None and qi_ == 0) else None
            VCH = min(8, nkb)
            Vall = ph.sb([128, nkb, NH * 65], BF16, "vall")
            Vb = [Buf() for _ in range((nkb + VCH - 1) // VCH)]
            Vv = V[s0:s0 + L, :].rearrange("(kb p) c -> p kb c", p=128)
            for i in range(len(Vb)):
                P.dma("sp", Vall[:, i * VCH:(i + 1) * VCH, :], Vv[:, i * VCH:(i + 1) * VCH, :], writes=[Vb[i]])
            kt_r = Ring(ph.sbufs(2, [96, L], BF16, "kth"))
            q_r = Ring(ph.sbufs(3, [96, 512], BF16, "qt"))
            pt_r = Ring(ph.sbufs(3, [128, 2, 512], BF16, "pt"))
            rc_r = Ring(ph.sbufs(2, [128, 512], F32, "rc"))
            bcs_r = Ring(ph.sbufs(2, [64, 512], F32, "bcs"))
            o_r = Ring(ph.sbufs(2, [64, 512], BF16, "oT"))
            ones = Buf(ph.sb([128, 64], F32, "ones"))
            P.op("dve", lambda e: e.memset(ones.ap[:], 1.0), writes=[ones])
            st_r = Ring(ph.psbufs(3, [128, 2, 512], F32, "st"))
            ot_r = Ring(ph.psbufs(2, [128, 512], F32, "ot"))
            steps = []
            tiles = []
            for h in range(NH):
                for qi in range(L // 512):
                    tiles.append((h, qi))
            state = {}

            def tile_ctx(ti):
                if ti not in state:
                    h, qi = tiles[ti]
                    if qi == 0:
                        kt = kt_r.next()
                        P.dma("sp", kt.ap[0:64, :], KT[h, 0:64, s0:s0 + L], writes=[kt])
                        P.dma("sp", kt.ap[64:96, :], KR[:, s0:s0 + L], writes=[kt])
                        state[("kt", h)] = kt
                    q = q_r.next()
                    P.dma("sp", q.ap[:], QT[h, :, s0 + qi * 512:s0 + (qi + 1) * 512], writes=[q])
                    state[ti] = (state[("kt", h)], q)
                return state[ti]

            nst = len(tiles) * npair
            sts = {}

            def emit_s(g):
                ti, i = divmod(g, npair)
                kt, q = tile_ctx(ti)
                st = st_r.next()
                for u in range(2):
                    kb = 2 * i + u
                    mm(P, st.ap[:, u, :], kt.ap[:, kb * 128:(kb + 1) * 128], q.ap[:], True, True, [kt, q], [st], inc=(u == 1))
                sts[g] = st

            AHEAD = 2
            for g in range(min(AHEAD, nst)):
                emit_s(g)
            ot = None
            for g in range(nst):
                if gen is not None and g % 40 == 20:
                    if next(gen, "done") == "done":
                        gen = None
                ti, i = divmod(g, npair)
                h, qi = tiles[ti]
                if i == 0:
                    ot = ot_r.next()
                if g + AHEAD < nst:
                    emit_s(g + AHEAD)
                st = sts.pop(g)
                pt = pt_r.next()
                act(P, pt.ap[:], st.ap[:], AF.Exp, [st], [pt])
                for u in range(2):
                    kb = 2 * i + u
                    mm(P, ot.ap[0:65, :], Vall[:, kb, h * 65:(h + 1) * 65], pt.ap[:, u, :], kb == 0, kb == nkb - 1,
                       [Vb[kb // VCH], pt], [ot], inc=(u == 1))
                if i == npair - 1:
                    rc = rc_r.next()
                    P.op("dve", (lambda rc=rc, ot=ot: lambda e: e.reciprocal(out=rc.ap[64:65, :], in_=ot.ap[64:65, :]))(),
                         reads=[ot], writes=[rc])
                    bcs = bcs_r.next()
                    rd = rcd_r.next()
                    P.dma("pool", rd.ap, rc.ap[64:65, :], reads=[rc], writes=[rd])
                    P.dma("pool", bcs.ap[:], rd.ap.partition_broadcast(64), reads=[rd], writes=[bcs])
                    oT = o_r.next()
                    tt(P, "dve", oT.ap[:], ot.ap[0:64, :], bcs.ap[:], ALU.mult, [ot, bcs], [oT])
                    P.dma("pool", OT[h * 64:(h + 1) * 64, s0 + qi * 512:s0 + (qi + 1) * 512], oT.ap[:], reads=[oT])
                    state.pop(ti, None)
            if gen is not None:
                for _ in gen:
                    pass


def phase_proj(P, X, seqs, w_in, w_uq, w_ukv, gT, qnT, kvnT, PHY, QT, KT, KR, V, consts):
    SC = 96.0 ** -0.5
    with Phase(P) as ph:
        Win = ph.sb([128, 8, 2176], BF16, "win")
        Winb = [Buf() for _ in range(8)]
        Wkr = Buf(ph.sb([128, 8, 2, 96], BF16, "wkr"))
        Wuq = Buf(ph.sb([128, 3, 2, 768], BF16, "wuq"))
        Wkn = Buf(ph.sb([128, 2, 8, 64], BF16, "wkn"))
        Wv = Buf(ph.sb([128, 2, 8, 64], BF16, "wv"))
        gtb = Buf(ph.sb([128, 8], F32, "gT"))
        qnb = Buf(ph.sb([128, 3], F32, "qn"))
        kvnb = Buf(ph.sb([128, 2], F32, "kvn"))
        P.dma("sp", gtb.ap[:], gT, writes=[gtb])
        P.dma("sp", qnb.ap[:], qnT, writes=[qnb])
        P.dma("sp", kvnb.ap[:], kvnT, writes=[kvnb])
        P.op("pool", lambda e: e.memset(Wkr.ap[:], 0.0), writes=[Wkr])
        P.op("pool", lambda e: e.memset(Wuq.ap[:, :, 1, :], 0.0), writes=[Wuq])
        with Phase(P) as ph2:
            stg = Ring(ph2.sbufs(3, [128, 1104], F32, "stg"))
            for kc in range(8):
                g = gtb.ap[:, kc:kc + 1]
                st = stg.next()
                P.dma("sp", st.ap[:], w_in[kc * 128:(kc + 1) * 128, 0:1104], writes=[st])
                ts(P, "dve", Win[:, kc, 0:1104], st.ap[:], g, None, ALU.mult, None, [st, gtb], [Winb[kc]])
                st = stg.next()
                P.dma("sp", st.ap[:], w_in[kc * 128:(kc + 1) * 128, 1104:2208], writes=[st])
                act(P, Win[:, kc, 1104:2176], st.ap[:, 0:1072], AF.Copy, [st, gtb], [Winb[kc]], scale=g)
                ts(P, "dve", Wkr.ap[:, kc, 0, 64:96], st.ap[:, 1072:1104], g, None, ALU.mult, None, [st, gtb], [Wkr])
                ts(P, "dve", Wkr.ap[:, kc, 1, 64:80], st.ap[:, 1088:1104], g, None, ALU.mult, None, [st, gtb], [Wkr])
                ts(P, "dve", Wkr.ap[:, kc, 1, 80:96], st.ap[:, 1072:1088], g, None, ALU.mult, None, [st, gtb], [Wkr])
            for kc in range(3):
                g = qnb.ap[:, kc:kc + 1]
                st = stg.next()
                P.dma("sp", st.ap[:, 0:768], w_uq[kc * 128:(kc + 1) * 128, :], writes=[st])
                sv = st.ap[:, 0:768].rearrange("p (h c) -> p h c", c=96)
                dv = Wuq.ap[:, kc, 1, :].rearrange("p (h c) -> p h c", c=96)
                act(P, Wuq.ap[:, kc, 0, :], st.ap[:, 0:768], AF.Copy, [st, qnb], [Wuq], scale=g)
                ts(P, "dve", dv[:, :, 64:80], sv[:, :, 80:96], g, None, ALU.mult, None, [st, qnb], [Wuq])
                ts(P, "dve", dv[:, :, 80:96], sv[:, :, 64:80], g, None, ALU.mult, None, [st, qnb], [Wuq])
            for kc in range(2):
                g = kvnb.ap[:, kc:kc + 1]
                st = stg.next()
                P.dma("sp", st.ap[:, 0:1024], w_ukv[kc * 128:(kc + 1) * 128, :], writes=[st])
                sv = st.ap[:, 0:1024].rearrange("p (h c) -> p h c", c=128)
                ts(P, "dve", Wkn.ap[:, kc, :, :], sv[:, :, 0:64], g, None, ALU.mult, None, [st, kvnb], [Wkn])
                ts(P, "pool", Wv.ap[:, kc, :, :], sv[:, :, 64:128], g, None, ALU.mult, None, [st, kvnb], [Wv])
        idf, idb = load_ident(P, ph, consts)
        zero = Buf(ph.sb([1, HYC], F32, "zero"))
        P.op("pool", lambda e: e.memset(zero.ap[:], 0.0), writes=[zero])
        for si, (s0, L) in enumerate(seqs):
            P.dma("sp", PHY[si][0:1, :], zero.ap[:], reads=[zero])
            P.dma("sp", PHY[si][L + 1:L + 2, :], zero.ap[:], reads=[zero])
        epsb = make_eps(P, ph)
        xt_r = Ring(ph.sbufs(5, [128, D], F32, "xt"))
        hb_r = Ring(ph.sbufs(3, [128, D], BF16, "hb"))
        junk = Buf(ph.sb([128, D], F32, "junk"))
        stat = StatRing(ph, 12)
        hT_r = Ring(ph.sbufs(3, [128, 8, 512], BF16, "hT"))
        phy_r = Ring(ph.sbufs(2, [128, HYC], F32, "phy"))
        cl_r = Ring(ph.sbufs(2, [128, 640], F32, "cl"))
        cln_r = Ring(ph.sbufs(2, [128, 640], BF16, "cln"))
        clT_r = Ring(ph.sbufs(3, [128, 5, 512], BF16, "clT"))
        vt_t = [ph.sb([128, 8, 65], BF16, "vt") for _ in range(2)]
        vt_b = [Buf(t) for t in vt_t]
        for t in vt_b:
            P.op("pool", (lambda t=t: lambda e: e.memset(t.ap[:], 1.0))(), writes=[t])
        vt_r = Ring(vt_b)
        rt_r = Ring(ph.sbufs(3, [96, 4, 512], F32, "rope"))
        tA_r = Ring(ph.sbufs(2, [96, 512], F32, "tA"))
        tB_r = Ring(ph.sbufs(2, [96, 512], F32, "tB"))
        kro_r = Ring(ph.sbufs(2, [96, 512], BF16, "krot"))
        qts_r = Ring(ph.sbufs(2, [96, 2, 512], BF16, "qts"))
        kts_r = Ring(ph.sbufs(2, [64, 2, 512], BF16, "kts"))
        tp_r = Ring(ph.psbufs(2, [128, 8, 128], BF16, "tp"))
        pj_r = Ring(ph.psbufs(2, [128, 512], F32, "pj"))
        ctp_r = Ring(ph.psbufs(1, [128, 5, 128], BF16, "ctp"))
        fm_r = Ring(ph.psbufs(3, [128, 512], F32, "fm"))
        rope = consts["rope"]
        ecyc = ["act", "dve"]

        def rope_mix(dst_ap, dstbuf, a, b, rt, ic, isn):
            tA = tA_r.next()
            tB = tB_r.next()
            tt(P, "dve", tA.ap[64:96, :], a.ap[64:96, :], rt.ap[64:96, ic, :], ALU.mult, [a, rt], [tA])
            tt(P, "dve", tB.ap[64:96, :], b.ap[64:96, :], rt.ap[64:96, isn, :], ALU.mult, [b, rt], [tB])
            tt(P, "pool", dst_ap, tA.ap[64:96, :], tB.ap[64:96, :], ALU.add, [tA, tB], [dstbuf])

        units = []
        for si, (s0, L) in enumerate(seqs):
            for j in range(L // 512):
                for s_ in range(4):
                    units.append((si, s0, L, j, s_))
        tctx = {}
        uctx = {}

        def tile_ctx(si, j):
            if (si, j) not in tctx:
                rt = rt_r.next()
                P.dma("sp", rt.ap[64:96, :, :], rope[:, :, j * 512:(j + 1) * 512].rearrange("f r t -> r f t"), writes=[rt])
                tctx[(si, j)] = (hT_r.next(), clT_r.next(), rt)
            return tctx[(si, j)]

        hbs = {}

        def prep_norm(u):
            xb = xload.pop(u)
            hb = hb_r.next()
            rms_norm_to(P, hb.ap[:], hb, xb.ap[:], xb, D, junk, stat, epsb)
            hbs[u] = hb

        def prep_tr(u):
            si, s0, L, j, s = units[u]
            hT, clT, rt = tile_ctx(si, j)
            hb = hbs.pop(u)
            tp = tp_r.next()
            for kc in range(8):
                tr(P, tp.ap[:, kc, :], hb.ap[:, kc * 128:(kc + 1) * 128], idb.ap[:], [hb, idb], [tp], inc=(kc == 7))
            cp(P, "dve", hT.ap[:, :, s * 128:(s + 1) * 128], tp.ap[:], [tp], [hT])

        xload = {}

        def load_x(u):
            si, s0, L, j, s = units[u]
            r0 = s0 + j * 512 + s * 128
            xb = xt_r.next()
            P.dma("sp", xb.ap[:], X[r0:r0 + 128, :], writes=[xb])
            xload[u] = xb

        def body(u):
            si, s0, L, j, s = units[u]
            hT, clT, rt = tile_ctx(si, j)
            p0 = j * 512
            r0 = s0 + p0 + s * 128
            phy = phy_r.next()
            cl = cl_r.next()
            for (c0, c1) in ((0, 512), (512, 1024), (1024, 1536), (1536, 2048), (2048, 2176)):
                pj = pj_r.next()
                w = c1 - c0
                for kc in range(8):
                    mm(P, pj.ap[:, 0:w], hT.ap[:, kc, s * 128:(s + 1) * 128], Win[:, kc, c0:c1], kc == 0, kc == 7,
                       [hT, Winb[kc]], [pj], inc=(kc == 7))
                eng = ecyc[0]
                ecyc.append(ecyc.pop(0))
                if c0 < 1536:
                    cp(P, eng, phy.ap[:, c0:c1], pj.ap[:, 0:w], [pj], [phy])
                else:
                    cp(P, eng, cl.ap[:, c0 - 1536:c1 - 1536], pj.ap[:, 0:w], [pj], [cl])
            P.dma("pool", PHY[si][1 + p0 + s * 128:1 + p0 + (s + 1) * 128, :], phy.ap[:], reads=[phy])
            cln = cln_r.next()
            rms_norm_to(P, cln.ap[:, 0:QL], cln, cl.ap[:, 0:QL], cl, QL, junk, stat, epsb)
            rms_norm_to(P, cln.ap[:, QL:640], cln, cl.ap[:, QL:640], cl, KVL, junk, stat, epsb)
            uctx[u] = cln

        def body_b(u):
            si, s0, L, j, s = units[u]
            hT, clT, rt = tile_ctx(si, j)
            r0 = s0 + j * 512 + s * 128
            cln = uctx.pop(u)
            ctp = ctp_r.next()
            for kc in range(5):
                tr(P, ctp.ap[:, kc, :], cln.ap[:, kc * 128:(kc + 1) * 128], idb.ap[:], [cln, idb], [ctp], inc=(kc == 4))
            cp(P, "dve", clT.ap[:, :, s * 128:(s + 1) * 128], ctp.ap[:], [ctp], [clT])
            vps = pj_r.next()
            for kc in range(2):
                mm(P, vps.ap[:], clT.ap[:, 3 + kc, s * 128:(s + 1) * 128], Wv.ap[:, kc, :, :].rearrange("p h c -> p (h c)"),
                   kc == 0, kc == 1, [clT, Wv], [vps], inc=(kc == 1))
            vt = vt_r.next()
            cp(P, "act", vt.ap[:, :, 0:64], vps.ap[:].rearrange("p (h c) -> p h c", c=64), [vps], [vt])
            P.dma("sp", V[r0:r0 + 128, :], vt.ap[:].rearrange("p h c -> p (h c)"), reads=[vt])

        def fm_part(si, s0, j, part):
            hT, clT, rt = tctx[(si, j)]
            g0 = s0 + j * 512
            if part == 0:
                kra = fm_r.next()
                for kc in range(8):
                    mm(P, kra.ap[0:96, :], Wkr.ap[:, kc, 0, :], hT.ap[:, kc, :], kc == 0, kc == 7, [Wkr, hT], [kra], inc=(kc == 7))
                krb = fm_r.next()
                for kc in range(8):
                    mm(P, krb.ap[0:96, :], Wkr.ap[:, kc, 1, :], hT.ap[:, kc, :], kc == 0, kc == 7, [Wkr, hT], [krb], inc=(kc == 7))
                kro = kro_r.next()
                rope_mix(kro.ap[64:96, :], kro, kra, krb, rt, 2, 3)
                P.dma("sp", KR[:, g0:g0 + 512], kro.ap[64:96, :], reads=[kro])
            qts = qts_r.next()
            kts = kts_r.next()
            for hi, h in enumerate((2 * part, 2 * part + 1)):
                qa = fm_r.next()
                for kc in range(3):
                    mm(P, qa.ap[0:96, :], Wuq.ap[:, kc, 0, h * 96:(h + 1) * 96], clT.ap[:, kc, :], kc == 0, kc == 2,
                       [Wuq, clT], [qa], inc=(kc == 2))
                qb = fm_r.next()
                for kc in range(3):
                    mm(P, qb.ap[0:96, :], Wuq.ap[:, kc, 1, h * 96:(h + 1) * 96], clT.ap[:, kc, :], kc == 0, kc == 2,
                       [Wuq, clT], [qb], inc=(kc == 2))
                act(P, qts.ap[0:64, hi, :], qa.ap[0:64, :], AF.Copy, [qa], [qts], scale=SC)
                rope_mix(qts.ap[64:96, hi, :], qts, qa, qb, rt, 0, 1)
                kn = fm_r.next()
                for kc in range(2):
                    mm(P, kn.ap[0:64, :], Wkn.ap[:, kc, h, :], clT.ap[:, 3 + kc, :], kc == 0, kc == 1, [Wkn, clT], [kn],
                       inc=(kc == 1))
                cp(P, "act", kts.ap[:, hi, :], kn.ap[0:64, :], [kn], [kts])
            h0 = 2 * part
            P.dma("sp", QT[h0:h0 + 2, :, g0:g0 + 512].rearrange("h r t -> r h t"), qts.ap[:], reads=[qts])
            P.dma("sp", KT[h0:h0 + 2, 0:64, g0:g0 + 512].rearrange("h r t -> r h t"), kts.ap[:], reads=[kts])
            if part == 3:
                tctx.pop((si, j))

        nu = len(units)
        for u in range(min(3, nu)):
            load_x(u)
        prep_norm(0)
        if nu > 1:
            prep_norm(1)
        prep_tr(0)
        for u in range(nu + 4):
            if u + 3 < nu:
                load_x(u + 3)
            if u + 2 < nu:
                prep_norm(u + 2)
            if u < nu:
                body(u)
            if 1 <= u <= nu:
                body_b(u - 1)
            k = u - 4
            if 0 <= k < nu:
                si, s0, L, j, s_ = units[k]
                fm_part(si, s0, j, s_)
            if u + 1 < nu:
                prep_tr(u + 1)


def shortconv_gen(P, ph, jobs, cwb, cbb):
    Rc = 4
    wt = Buf(ph.sb([128, 3, 512], F32, "cw"))
    bt = Buf(ph.sb([128, 512], F32, "cb"))
    up_r = Ring(ph.sbufs(2, [128, Rc + 2, 512], F32, "up"))
    acc_r = Ring(ph.sbufs(2, [128, Rc, 512], F32, "acc"))
    tmp_r = Ring(ph.sbufs(2, [128, Rc, 512], F32, "tmp"))
    out_r = Ring(ph.sbufs(2, [128, Rc, 512], BF16, "uc"))
    k = 0
    for g in range(3):
        P.dma("sp", wt.ap[:], cwb[:, :, g * 512:(g + 1) * 512], writes=[wt])
        P.dma("sp", bt.ap[:], cbb[:, g * 512:(g + 1) * 512], writes=[bt])
        for (PHYs, L, UC) in jobs:
            R = L // 128
            for rc in range(R // Rc):
                up = up_r.next()
                src = dram_ap(PHYs, (rc * Rc) * HYC + g * 512, [[R * HYC, 128], [HYC, Rc + 2], [1, 512]])
                P.dma("sp", up.ap[:], src, writes=[up])

                def wb(i):
                    return wt.ap[:, i, :].unsqueeze(1).broadcast_to([128, Rc, 512])

                bb = bt.ap[:].unsqueeze(1).broadcast_to([128, Rc, 512])
                e1 = "dve" if k % 3 != 2 else "pool"
                k += 1
                acc = acc_r.next()
                tmp = tmp_r.next()
                out = out_r.next()
                tt(P, e1, acc.ap[:], up.ap[:, 0:Rc, :], wb(0), ALU.mult, [up, wt], [acc])
                tt(P, e1, tmp.ap[:], up.ap[:, 1:Rc + 1, :], wb(1), ALU.mult, [up, wt], [tmp])
                tt(P, e1, acc.ap[:], acc.ap[:], tmp.ap[:], ALU.add, [acc, tmp], [acc])
                tt(P, e1, tmp.ap[:], up.ap[:, 2:Rc + 2, :], wb(2), ALU.mult, [up, wt], [tmp])
                tt(P, e1, acc.ap[:], acc.ap[:], tmp.ap[:], ALU.add, [acc, tmp], [acc])
                tt(P, e1, out.ap[:], acc.ap[:], bb, ALU.add, [acc, bt], [out])
                dst = dram_ap(UC, (rc * Rc) * HYC + g * 512, [[R * HYC, 128], [HYC, Rc], [1, 512]])
                P.dma("pool", dst, out.ap[:], reads=[out])
                yield


def hy_filter_hidden(P, L, zext, w1, f1c, b1c, w2d, f2c, b2c, H2D):
    N = 2 * L
    OFF = 16.0 * math.pi
    with Phase(P) as ph:
        w1t = Buf(ph.sb([33, 64], F32, "w1"))
        w2t = Buf(ph.sb([64, 128], F32, "w2"))
        f1 = Buf(ph.sb([64, 1], F32, "f1"))
        b1 = Buf(ph.sb([64, 1], F32, "b1"))
        f2 = Buf(ph.sb([128, 1], F32, "f2"))
        b2 = Buf(ph.sb([128, 1], F32, "b2"))
        for (b, a) in ((w1t, w1), (w2t, w2d), (f1, f1c), (b1, b1c), (f2, f2c), (b2, b2c)):
            P.dma("sp", b.ap[:], a, writes=[b])
        ts(P, "dve", b1.ap[:], b1.ap[:], f1.ap[:, 0:1], OFF, ALU.mult, ALU.add, [b1, f1], [b1])
        ts(P, "dve", b2.ap[:], b2.ap[:], f2.ap[:, 0:1], OFF, ALU.mult, ALU.add, [b2, f2], [b2])
        z_r = Ring(ph.sbufs(2, [33, 512], F32, "z"))
        a_r = Ring(ph.sbufs(2, [128, 512], F32, "arg"))
        i_r = Ring(ph.sbufs(2, [128, 512], I32, "iq"))
        q_r = Ring(ph.sbufs(2, [128, 512], F32, "fq"))
        h1_r = Ring(ph.sbufs(2, [64, 512], F32, "h1"))
        ho_r = Ring(ph.sbufs(2, [128, 512], BF16, "h2"))
        ps_r = Ring(ph.psbufs(4, [128, 512], F32, "ps"))
        def sin_layer(ps, npart, fr, bs, out_ap, outbuf):
            a = a_r.next()
            iq = i_r.next()
            fq = q_r.next()
            av, iv, fv = a.ap[0:npart, :], iq.ap[0:npart, :], fq.ap[0:npart, :]
            ts(P, "dve", av, ps.ap[0:npart, :], fr.ap[:, 0:1], bs.ap[:, 0:1], ALU.mult, ALU.add, [ps, fr, bs], [a])
            ts(P, "dve", iv, av, 1.0 / TWO_PI, None, ALU.mult, None, [a], [iq])
            cp(P, "dve", fv, iv, [iq], [fq])
            stt(P, "dve", av, fv, -TWO_PI, av, ALU.mult, ALU.add, [fq, a], [a])
            ts(P, "dve", fv, av, math.pi, TWO_PI, ALU.is_gt, ALU.mult, [a], [fq])
            tt(P, "dve", av, av, fv, ALU.subtract, [a, fq], [a])
            act(P, out_ap, av, AF.Sin, [a], [outbuf])

        s2_r = Ring(ph.sbufs(2, [128, 512], F32, "s2"))
        for ch in range(N // 512):
            c0 = ch * 512
            zt = z_r.next()
            P.dma("sp", zt.ap[:], zext[:, c0:c0 + 512], writes=[zt])
            p1 = ps_r.next()
            mm(P, p1.ap[0:64, :], w1t.ap[:], zt.ap[:], True, True, [w1t, zt], [p1])
            h1 = h1_r.next()
            sin_layer(p1, 64, f1, b1, h1.ap[:], h1)
            p2 = ps_r.next()
            mm(P, p2.ap[:], w2t.ap[:], h1.ap[:], True, True, [w2t, h1], [p2])
            s2 = s2_r.next()
            sin_layer(p2, 128, f2, b2, s2.ap[:], s2)
            ho = ho_r.next()
            lo = 0 if c0 < L else 64
            P.op("pool", (lambda ho=ho: lambda e: e.memset(ho.ap[:], 0.0))(), writes=[ho])
            cp(P, "pool", ho.ap[lo:lo + 64, :], s2.ap[lo:lo + 64, :], [s2], [ho])
            P.dma("pool", H2D[:, c0:c0 + 512], ho.ap[:], reads=[ho])


def fft_s1(P, N1, Kin, A_scrs, F1d, make_src):
    NK = N1 // 2 + 1
    ns = len(A_scrs)
    with Phase(P) as ph:
        F1t = Buf(ph.sb([128, 2, N1], BF16, "F1"))
        P.dma("sp", F1t.ap[0:Kin, :, :], F1d[0:Kin, :, :], writes=[F1t])
        chunk = make_src(ph)
        ps_r = Ring(ph.psbufs(4, [128, 512], F32, "s1p"))
        At_r = [Ring(ph.sbufs(2, [128, 2, 8, 512], BF16, "At")) for _ in range(ns)]
        ec = ["act", "dve"]
        for n2c in range(16):
            Ats = [r.next() for r in At_r]
            srcs = chunk(n2c)
            for j in range(8):
                for si in range(ns):
                    rhs, rb = srcs[j][si]
                    At = Ats[si]
                    for ri in range(2):
                        p = ps_r.next()
                        mm(P, p.ap[0:NK, :], F1t.ap[0:Kin, ri, 0:NK], rhs, True, True, [F1t, rb], [p])
                        cp(P, ec[0], At.ap[0:NK, ri, j, :], p.ap[0:NK, :], [p], [At])
                        ec.append(ec.pop(0))
            for si in range(ns):
                for ri in range(2):
                    P.dma("act" if ri == 0 else "pool", A_scrs[si][ri, n2c * 8:(n2c + 1) * 8, :, :].rearrange("n k c -> k n c"),
                          Ats[si].ap[0:NK, ri, :, :], reads=[Ats[si]])


def src_from_dram(P, U, L, c0):
    Kin = L // 128
    Uv = U.rearrange("(n1 n2) c -> n1 n2 c", n2=128)

    def make(ph):
        d_r = Ring(ph.sbufs(2, [128, 8, 512], BF16, "D"))

        def chunk(n2c):
            d = d_r.next()
            P.dma("sp", d.ap[0:Kin, :, :], Uv[:, n2c * 8:(n2c + 1) * 8, c0:c0 + 512], writes=[d])
            return [[(d.ap[0:Kin, j, :], d)] for j in range(8)]

        return chunk

    return make


def src_filter(P, L, H2D, w3s, tau, negd, hyb):
    N1 = 2 * L // 128

    def make(ph):
        H2 = Buf(ph.sb([128, 2 * L], BF16, "H2"))
        P.dma("sp", H2.ap[:], H2D, writes=[H2])
        W3 = Buf(ph.sb([128, 2, 512], BF16, "W3"))
        w3f = Buf(ph.sb([128, 2, 512], F32, "w3f"))
        P.dma("sp", w3f.ap[:], w3s, writes=[w3f])
        cp(P, "dve", W3.ap[:], w3f.ap[:], [w3f], [W3])
        taut = Buf(ph.sb([128, 128], F32, "tau"))
        P.dma("sp", taut.ap[:], tau, writes=[taut])
        ndt = Buf(ph.sb([128, 512], F32, "negd"))
        P.dma("sp", ndt.ap[:], negd, writes=[ndt])
        bia = Buf(ph.sb([1, 2, 512], F32, "hyb"))
        P.dma("sp", bia.ap[:], hyb.rearrange("(o f) c -> o f c", o=1), writes=[bia])
        kp_r = Ring(ph.psbufs(3, [128, 512], F32, "kp"))
        dec_r = Ring(ph.sbufs(3, [128, 512], F32, "dec"))
        kf_r = Ring(ph.sbufs(2, [128, 512], F32, "kf32"))
        kb_r = Ring(ph.sbufs(20, [128, 512], BF16, "kbf"))
        H2v = H2.ap[:].rearrange("p (n1 n2) -> p n2 n1", n2=128)

        def chunk(n2c):
            out = []
            for j in range(8):
                n2 = n2c * 8 + j
                dec = dec_r.next()
                act(P, dec.ap[0:N1, :], ndt.ap[0:N1, :], AF.Exp, [ndt, taut], [dec], scale=taut.ap[0:N1, n2:n2 + 1])
                row = []
                for f in range(2):
                    kp = kp_r.next()
                    mm(P, kp.ap[0:N1, :], H2v[:, n2, :], W3.ap[:, f, :], True, True, [H2, W3], [kp])
                    kb = kb_r.next()
                    if n2 == 0:
                        kf = kf_r.next()
                        tt(P, "dve", kf.ap[0:N1, :], kp.ap[0:N1, :], dec.ap[0:N1, :], ALU.mult, [kp, dec], [kf])
                        tt(P, "dve", kf.ap[0:1, :], kf.ap[0:1, :], bia.ap[:, f, :], ALU.add, [kf, bia], [kf])
                        cp(P, "dve", kb.ap[0:N1, :], kf.ap[0:N1, :], [kf], [kb])
                    else:
                        tt(P, "dve", kb.ap[0:N1, :], kp.ap[0:N1, :], dec.ap[0:N1, :], ALU.mult, [kp, dec], [kb])
                    row.append((kb.ap[0:N1, :], kb))
                out.append(row)
            return out

        return chunk

    return make


def fft_s2(P, N1, A_scr, Gd, KFf, mode, Gid=None, B_scr=None):
    KC = 4
    NK = N1 // 2 + 1
    chunks = [(k0, min(KC, NK - k0)) for k0 in range(0, NK, KC)]
    with Phase(P) as ph:
        At_r = Ring(ph.sbufs(2, [128, 2, KC, 512], BF16, "At"))
        Gt_r = Ring(ph.sbufs(2, [128, KC, 3, 128], BF16, "Gt"))
        Kt_r = Ring(ph.sbufs(2, [128, 2, KC, 512], BF16, "Kt"))
        X_r = Ring(ph.psbufs(2, [128, 2, 512], F32, "X"))
        if mode == "conv":
            Gi_r = Ring(ph.sbufs(2, [128, KC, 3, 128], BF16, "Git"))
            Yt_r = Ring(ph.sbufs(2, [128, 2, 512], BF16, "Yt"))
            Bt_r = Ring(ph.sbufs(2, [128, 2, KC, 512], BF16, "Bt"))
            tm_r = Ring(ph.sbufs(8, [128, 512], F32, "tm"))
            B_r = Ring(ph.psbufs(2, [128, 2, 512], F32, "B"))
        ctx = {}

        def chunk_ctx(ci):
            if ci not in ctx:
                k0, n = chunks[ci]
                At = At_r.next()
                for ri in range(2):
                    P.dma("sp", At.ap[:, ri, 0:n, :], A_scr[ri, :, k0:k0 + n, :], writes=[At])
                Gt = Gt_r.next()
                P.dma("sp", Gt.ap[:, 0:n, :, :], Gd[:, k0:k0 + n, :, :], writes=[Gt])
                Kt = Kt_r.next()
                Git = Bt = None
                if mode == "conv":
                    for ri in range(2):
                        P.dma("sp", Kt.ap[:, ri, 0:n, :], KFf[ri, :, k0:k0 + n, :], writes=[Kt])
                    Git = Gi_r.next()
                    P.dma("sp", Git.ap[:, 0:n, :, :], Gid[:, k0:k0 + n, :, :], writes=[Git])
                    Bt = Bt_r.next()
                ctx[ci] = (At, Gt, Kt, Git, Bt)
            return ctx[ci]

        planes = [(ci, j) for ci, (k0, n) in enumerate(chunks) for j in range(n)]
        Xs = {}

        def emit_X(g):
            ci, j = planes[g]
            At, Gt, Kt, Git, Bt = chunk_ctx(ci)
            X = X_r.next()
            rd = [Gt, At]
            mm(P, X.ap[:, 0, :], Gt.ap[:, j, 0, :], At.ap[:, 0, j, :], True, False, rd, [X], inc=False)
            mm(P, X.ap[:, 0, :], Gt.ap[:, j, 2, :], At.ap[:, 1, j, :], False, True, rd, [X], inc=False)
            mm(P, X.ap[:, 1, :], Gt.ap[:, j, 1, :], At.ap[:, 0, j, :], True, False, rd, [X], inc=False)
            mm(P, X.ap[:, 1, :], Gt.ap[:, j, 0, :], At.ap[:, 1, j, :], False, True, rd, [X])
            Xs[g] = X

        emit_X(0)
        for g in range(len(planes)):
            ci, j = planes[g]
            k0, n = chunks[ci]
            At, Gt, Kt, Git, Bt = chunk_ctx(ci)
            if g + 1 < len(planes):
                emit_X(g + 1)
            X = Xs.pop(g)
            if mode == "filter":
                cp(P, "act" if g % 2 == 0 else "dve", Kt.ap[:, :, j, :], X.ap[:], [X], [Kt])
                if j == n - 1:
                    for ri in range(2):
                        P.dma("pool", KFf[ri, :, k0:k0 + n, :], Kt.ap[:, ri, 0:n, :], reads=[Kt])
                    ctx.pop(ci)
                continue
            t1, t2, t3, t4 = tm_r.next(), tm_r.next(), tm_r.next(), tm_r.next()
            tt(P, "dve", t1.ap[:], X.ap[:, 0, :], Kt.ap[:, 0, j, :], ALU.mult, [X, Kt], [t1])
            tt(P, "dve", t2.ap[:], X.ap[:, 1, :], Kt.ap[:, 1, j, :], ALU.mult, [X, Kt], [t2])
            tt(P, "dve", t3.ap[:], X.ap[:, 0, :], Kt.ap[:, 1, j, :], ALU.mult, [X, Kt], [t3])
            tt(P, "dve", t4.ap[:], X.ap[:, 1, :], Kt.ap[:, 0, j, :], ALU.mult, [X, Kt], [t4])
            Yt = Yt_r.next()
            tt(P, "pool", Yt.ap[:, 0, :], t1.ap[:], t2.ap[:], ALU.subtract, [t1, t2], [Yt])
            tt(P, "dve", Yt.ap[:, 1, :], t3.ap[:], t4.ap[:], ALU.add, [t3, t4], [Yt])
            B = B_r.next()
            rd = [Git, Yt]
            mm(P, B.ap[:, 0, :], Git.ap[:, j, 0, :], Yt.ap[:, 0, :], True, False, rd, [B], inc=False)
            mm(P, B.ap[:, 0, :], Git.ap[:, j, 2, :], Yt.ap[:, 1, :], False, True, rd, [B], inc=False)
            mm(P, B.ap[:, 1, :], Git.ap[:, j, 1, :], Yt.ap[:, 0, :], True, False, rd, [B], inc=False)
            mm(P, B.ap[:, 1, :], Git.ap[:, j, 0, :], Yt.ap[:, 1, :], False, True, rd, [B])
            cp(P, "act", Bt.ap[:, :, j, :], B.ap[:], [B], [Bt])
            if j == n - 1:
                for ri in range(2):
                    P.dma("act", B_scr[ri, k0:k0 + n, :, :].rearrange("k n c -> n k c"), Bt.ap[:, ri, 0:n, :], reads=[Bt])
                ctx.pop(ci)


def fft_s4(P, N1, L, B_scr, F4d, gate_src, gc0, dst, dc0):
    Kh = N1 // 2
    NK = N1 // 2 + 1
    gv = gate_src.rearrange("(n1 n2) c -> n1 n2 c", n2=128)
    dv = dst.rearrange("(n1 n2) c -> n1 n2 c", n2=128)
    with Phase(P) as ph:
        F4t = Buf(ph.sb([128, 2, Kh], BF16, "F4"))
        P.dma("sp", F4t.ap[0:NK, :, :], F4d, writes=[F4t])
        Bt_r = Ring(ph.sbufs(2, [128, 2, 8, 512], BF16, "Bt4"))
        g_r = Ring(ph.sbufs(2, [128, 8, 512], BF16, "gate"))
        z_r = Ring(ph.sbufs(2, [128, 8, 512], BF16, "zt"))
        ps_r = Ring(ph.psbufs(4, [128, 512], F32, "s4p"))
        for n2c in range(16):
            Bt = Bt_r.next()
            for ri in range(2):
                P.dma("sp", Bt.ap[0:NK, ri, :, :], B_scr[ri, :, n2c * 8:(n2c + 1) * 8, :], writes=[Bt])
            g = g_r.next()
            P.dma("sp", g.ap[0:Kh, :, :], gv[:, n2c * 8:(n2c + 1) * 8, gc0:gc0 + 512], writes=[g])
            z = z_r.next()
            for j in range(8):
                p = ps_r.next()
                mm(P, p.ap[0:Kh, :], F4t.ap[0:NK, 0, :], Bt.ap[0:NK, 0, j, :], True, False, [F4t, Bt], [p], inc=False)
                mm(P, p.ap[0:Kh, :], F4t.ap[0:NK, 1, :], Bt.ap[0:NK, 1, j, :], False, True, [F4t, Bt], [p])
                tt(P, "dve", z.ap[0:Kh, j, :], p.ap[0:Kh, :], g.ap[0:Kh, j, :], ALU.mult, [p, g], [z])
            P.dma("act", dv[:, n2c * 8:(n2c + 1) * 8, dc0:dc0 + 512], z.ap[0:Kh, :, :], reads=[z])


def phase_hyena(P, s0, L, PHYs, hw, ZH, scr, cL):
    N1 = 2 * L // 128
    Kin = N1 // 2
    hy_filter_hidden(P, L, cL["zext"], hw["w1"], hw["f1c"], hw["b1c"], hw["w2d"], hw["f2c"], hw["b2c"], scr["H2D"])
    fft_s1(P, N1, N1, [scr["A"], scr["A2"]], cL["F1"], src_filter(P, L, scr["H2D"], hw["w3s"], cL["tau"], cL["negd"], hw["hyb"]))
    fft_s2(P, N1, scr["A"], cL["G"], scr["KF"][0], "filter")
    fft_s2(P, N1, scr["A2"], cL["G"], scr["KF"][1], "filter")
    fft_s1(P, N1, Kin, [scr["A"]], cL["F1"], src_from_dram(P, scr["UC"], L, 1024))
    fft_s2(P, N1, scr["A"], cL["G"], scr["KF"][0], "conv", cL["Gi"], scr["B"])
    fft_s4(P, N1, L, scr["B"], cL["F4"], scr["UC"], 0, scr["U1"], 0)
    fft_s1(P, N1, Kin, [scr["A"]], cL["F1"], src_from_dram(P, scr["U1"], L, 0))
    fft_s2(P, N1, scr["A"], cL["G"], scr["KF"][1], "conv", cL["Gi"], scr["B"])
    fft_s4(P, N1, L, scr["B"], cL["F4"], scr["UC"], 512, ZH[s0:s0 + L, :], 0)


_CONST_CACHE = {}


def make_consts(LP, LS):
    key = (LP, LS)
    if key in _CONST_CACHE:
        return _CONST_CACHE[key]
    c = {}
    c["ident"] = np.eye(128, dtype=np.float32)
    Lm = max(LP, LS)
    inv = (np.float32(1.0) / (np.float32(10000.0) ** (np.arange(0, 32, 2, dtype=np.float32) / np.float32(32)))).astype(np.float32)
    ang = (np.arange(Lm, dtype=np.float32)[:, None] * inv[None, :]).astype(np.float32)
    cos = np.cos(ang).astype(np.float32).T
    sin = np.sin(ang).astype(np.float32).T
    cos32 = np.concatenate([cos, cos], 0)
    sin32 = np.concatenate([-sin, sin], 0)
    sc = np.float32(96.0 ** -0.5)
    c["rope"] = np.ascontiguousarray(np.stack([cos32 * sc, sin32 * sc, cos32, sin32], 0).astype(np.float32))
    Lt = 384
    M = np.zeros((4, 5, 128, 128), np.float64)
    for g, w in enumerate((2, 4, 8, 16)):
        Mf = np.zeros((Lt, Lt))
        for t in range(Lt):
            lo = min(max(t - w // 2, 0), Lt)
            hi = min(max(t + w // 2, 0), Lt)
            Mf[t, lo:hi] = 1.0 / (hi - lo)
            Mf[t, t] -= 1.0
        M[g, 0] = Mf[128:256, 0:128].T
        M[g, 1] = Mf[128:256, 128:256].T
        M[g, 2] = Mf[128:256, 256:384].T
        M[g, 3] = Mf[0:128, 0:128].T
        M[g, 4] = Mf[256:384, 256:384].T
    c["poolM"] = np.ascontiguousarray(M.transpose(2, 0, 1, 3)).astype(NPBF)
    for L in sorted(set((LP, LS))):
        N = 2 * L
        N1 = N // 128
        n1 = np.arange(N1)
        th = 2 * np.pi * ((n1[:, None] * n1[None, :]) % N1) / N1
        c["F1_%d" % L] = np.stack([np.cos(th), -np.sin(th)], 1).astype(NPBF)
        NK = N1 // 2 + 1
        thi = th[:NK, :N1 // 2]
        wg = np.full((NK, 1), 2.0)
        wg[0] = 1.0
        wg[N1 // 2] = 1.0
        c["F4_%d" % L] = np.stack([wg * np.cos(thi) / N, -wg * np.sin(thi) / N], 1).astype(NPBF)
        n2 = np.arange(128)[:, None, None]
        k1 = np.arange(N1)[None, :, None]
        k2 = np.arange(128)[None, None, :]
        tg = 2 * np.pi * ((n2 * (k1 + N1 * k2)) % N) / N
        cg, sg = np.cos(tg), np.sin(tg)
        c["G_%d" % L] = np.stack([cg, -sg, sg], 2).astype(NPBF)
        cgt, sgt = cg.transpose(2, 1, 0), sg.transpose(2, 1, 0)
        c["Gi_%d" % L] = np.stack([cgt, sgt, -sgt], 2).astype(NPBF)
        t = np.linspace(0.0, 1.0, L)
        wpos = (2.0 * np.pi / L) * np.arange(L)
        bands = np.linspace(1e-4, 15.0, 16)
        z = np.concatenate([t[:, None], np.cos(bands[None, :] * wpos[:, None]), -np.sin(bands[None, :] * wpos[:, None])], -1)
        idx = np.arange(N)
        src = np.where(idx < L, idx, (N - idx) % L)
        c["zext_%d" % L] = np.ascontiguousarray(z[src].T).astype(np.float32)
        tau = np.where(idx < L, idx, N - idx).astype(np.float64)
        tau[L] = 1e9
        taut = np.zeros((128, 128), np.float32)
        taut[:N1] = tau.reshape(N1, 128)
        c["tau_%d" % L] = taut
        deltas = np.abs(np.linspace(math.log(1e-2) / 1.5, math.log(1e-2) / 0.3, 512))
        c["negd_%d" % L] = np.ascontiguousarray(np.broadcast_to((-deltas / (L - 1)).astype(np.float32), (128, 512)))
    _CONST_CACHE[key] = c
    return c


def layout_weights(w):
    f = np.float32
    o = {}

    def colT(v, k):
        v = np.asarray(v, f)
        return np.ascontiguousarray(v.reshape(v.shape[0], k, 128).transpose(0, 2, 1))

    def rep(v):
        v = np.asarray(v, f)
        return np.ascontiguousarray(np.broadcast_to(v[..., None, :], v.shape[:-1] + (128, v.shape[-1])))

    o["mix_norm_T"] = colT(w["mix_norm"], 8)
    o["mlp_norm_T"] = colT(w["mlp_norm"], 8)
    o["q_norm_T"] = colT(w["q_norm"], 3)
    o["kv_norm_T"] = colT(w["kv_norm"], 2)
    o["final_b"] = rep(np.asarray(w["final_norm"], f))
    o["pscale_b"] = rep(w["pool_scale"])
    for k in ("w_in", "w_uq", "w_ukv", "w_out", "pool_w", "mlp_w1", "mlp_w2", "hf_w1"):
        o[k] = np.ascontiguousarray(np.asarray(w[k], f))
    cw = np.asarray(w["hy_conv_w"], f)
    o["cwb"] = np.ascontiguousarray(np.broadcast_to(cw[:, None], (cw.shape[0], 128, 3, HYC)))
    o["cbb"] = rep(w["hy_conv_b"])
    o["f1c"] = np.ascontiguousarray(np.asarray(w["hf_freq"], f)[:, :, None])
    o["b1c"] = np.ascontiguousarray(np.asarray(w["hf_b1"], f)[:, :, None])
    o["f2c"] = np.ascontiguousarray(np.concatenate([np.asarray(w["hf_freq"], f)] * 2, 1)[:, :, None])
    o["b2c"] = np.ascontiguousarray(np.concatenate([np.asarray(w["hf_b2"], f)] * 2, 1)[:, :, None])
    o["w2d"] = np.ascontiguousarray(np.concatenate([np.asarray(w["hf_w2"], f)] * 2, 2))
    w3 = np.asarray(w["hf_w3"], f).reshape(-1, 64, 2, 2, 512)
    o["w3s"] = np.ascontiguousarray(w3.transpose(0, 2, 1, 3, 4).reshape(-1, 128, 2, 512))
    o["hyb"] = np.ascontiguousarray(np.asarray(w["hy_bias"], f))
    return o


WSHAPES = {
    "mix_norm_T": [4, 128, 8], "mlp_norm_T": [4, 128, 8], "q_norm_T": [2, 128, 3], "kv_norm_T": [2, 128, 2],
    "final_b": [128, D], "pscale_b": [2, 128, D], "w_in": [2, D, INC], "w_uq": [2, QL, 768], "w_ukv": [2, KVL, 1024],
    "w_out": [2, D, D], "pool_w": [2, 4, 256, 256], "mlp_w1": [4, D, DFF], "mlp_w2": [4, DFF, D], "hf_w1": [2, 33, 64],
    "cwb": [2, 128, 3, HYC], "cbb": [2, 128, HYC], "f1c": [2, 64, 1], "b1c": [2, 64, 1], "f2c": [2, 128, 1],
    "b2c": [2, 128, 1], "w2d": [2, 64, 128], "w3s": [2, 128, 2, 512], "hyb": [2, 2, 512],
}


def build(LP, LS, layers=(0, 1, 2, 3), final=True, consts_np=None):
    nc = bass.Bass("TRN2", target_bir_lowering=False)
    T = LP + LS
    seqs = [(0, LP), (LP, LS)]

    def din(name, shape, dt=F32):
        return nc.dram_tensor(name, list(shape), dt, kind="ExternalInput").ap()

    def dscr(name, shape, dt):
        return nc.dram_tensor(name, list(shape), dt, kind="Internal").ap()

    xP = din("xP", [LP, D])
    xS = din("xS", [LS, D])
    W = {k: din(k, v) for k, v in WSHAPES.items()}
    C = {}
    for k, v in consts_np.items():
        C[k] = din("c_" + k, v.shape, BF16 if v.dtype == NPBF else F32)
    yP = nc.dram_tensor("yP", [LP, D], F32, kind="ExternalOutput").ap()
    yS = nc.dram_tensor("yS", [LS, D], F32, kind="ExternalOutput").ap()
    X = dscr("X", [T, D], F32)
    PHY = [dscr("PHY%d" % i, [L + 2, HYC], F32) for i, (s0, L) in enumerate(seqs)]
    QT = dscr("QT", [NH, 96, T], BF16)
    KT = dscr("KT", [NH, 96, T], BF16)
    KR = dscr("KR", [32, T], BF16)
    V = dscr("V", [T, NH * 65], BF16)
    ZH = dscr("ZH", [T, 512], BF16)
    OT = dscr("OT", [512, T], BF16)
    RCD = dscr("RCD", [8, 512], F32)
    hscr = []
    for i, (s0, L) in enumerate(seqs):
        N1 = 2 * L // 128
        hscr.append({
            "A": dscr("hA%d" % i, [2, 128, N1 // 2 + 1, 512], BF16), "A2": dscr("hA2%d" % i, [2, 128, N1 // 2 + 1, 512], BF16),
            "B": dscr("hB%d" % i, [2, N1 // 2 + 1, 128, 512], BF16),
            "KF": [dscr("hKF%d_%d" % (i, f), [2, 128, N1 // 2 + 1, 512], BF16) for f in range(2)],
            "UC": dscr("hUC%d" % i, [L, HYC], BF16), "U1": dscr("hU1%d" % i, [L, 512], BF16),
            "H2D": dscr("hH2%d" % i, [128, 2 * L], BF16)})
    P = Prog(nc)
    with Phase(P) as ph:
        RB = 1024
        for (s0, L), src in zip(seqs, (xP, xS)):
            for r in range(0, L, RB):
                n = min(RB, L - r)
                P.dma("sp", X[s0 + r:s0 + r + n, :], src[r:r + n, :])
    outs = [(0, LP, yP), (LP, LS, yS)]
    for li, i in enumerate(layers):
        last = (li == len(layers) - 1)
        if i % 2 == 0:
            e = i // 2
            phase_proj(P, X, seqs, W["w_in"][e], W["w_uq"][e], W["w_ukv"][e], W["mix_norm_T"][i], W["q_norm_T"][e],
                       W["kv_norm_T"][e], PHY, QT, KT, KR, V, C)
            hw = {"cwb": W["cwb"][e], "cbb": W["cbb"][e], "w1": W["hf_w1"][e], "f1c": W["f1c"][e], "b1c": W["b1c"][e],
                  "w2d": W["w2d"][e], "f2c": W["f2c"][e], "b2c": W["b2c"][e], "w3s": W["w3s"][e], "hyb": W["hyb"][e]}
            jobs = [(PHY[si], L, hscr[si]["UC"]) for si, (s0, L) in enumerate(seqs)]
            phase_attn(P, seqs, QT, KT, KR, V, OT, RCD, side=lambda ph: shortconv_gen(P, ph, jobs, hw["cwb"], hw["cbb"]))
            for si, (s0, L) in enumerate(seqs):
                cL = {k: C["%s_%d" % (k, L)] for k in ("F1", "F4", "G", "Gi", "zext", "tau", "negd")}
                phase_hyena(P, s0, L, PHY[si], hw, ZH, hscr[si], cL)
            phase_wout(P, X, T, ZH, OT, W["w_out"][e], C)
        else:
            o = i // 2
            phase_pool(P, X, seqs, W["pool_w"][o], W["mix_norm_T"][i], W["pscale_b"][o], C)
        fin = (W["final_b"], outs) if (last and final) else None
        phase_mlp(P, X, T, W["mlp_w1"][i], W["mlp_w2"][i], W["mlp_norm_T"][i], C, final=fin)
    if not final:
        with Phase(P) as ph:
            for (r0, nr, oap) in outs:
                for r in range(0, nr, 1024):
                    n = min(1024, nr - r)
                    P.dma("sp", oap[r:r + n, :], X[r0 + r:r0 + r + n, :])
    P.flush(final=True)
    return nc


_NC_CACHE = {}


def kernel(**inputs):
    LP, LS = 8192, 4096
    consts = make_consts(LP, LS)
    key = (LP, LS)
    if key not in _NC_CACHE:
        _NC_CACHE[key] = build(LP, LS, consts_np=consts)
    nc = _NC_CACHE[key]
    wl = layout_weights(inputs)
    xp = np.asarray(inputs["x_prompt"], np.float32)
    xs = np.asarray(inputs["x_sample"], np.float32)
    base = dict(wl)
    for k, v in consts.items():
        base["c_" + k] = v
    in_maps = []
    for c in range(8):
        m = dict(base)
        m["xP"] = np.ascontiguousarray(xp[c])
        m["xS"] = np.ascontiguousarray(xs[c // 2])
        in_maps.append(m)
    res = run_bass_kernel_spmd(nc, in_maps, core_ids=list(range(8)))
    yp = np.stack([np.asarray(res.results[c]["yP"], np.float32) for c in range(8)], 0)
    hl = LS // 2
    ys = np.stack([np.concatenate([np.asarray(res.results[2 * j]["yS"], np.float32)[:hl],
                                   np.asarray(res.results[2 * j + 1]["yS"], np.float32)[hl:]], 0) for j in range(4)], 0)
    return (yp, ys)
```

```python
import math
from contextlib import ExitStack

import numpy as np
import ml_dtypes
import concourse.bass as bass
import concourse.mybir as mybir
from concourse.bass_utils import run_bass_kernel_spmd

F32 = mybir.dt.float32
BF16 = mybir.dt.bfloat16
I32 = mybir.dt.int32
AF = mybir.ActivationFunctionType
ALU = mybir.AluOpType
AX = mybir.AxisListType
NPBF = ml_dtypes.bfloat16

D = 1024
DFF = 4096
DHY = 512
HYC = 1536
QL = 384
KVL = 256
INC = 2208
NH = 8
EPS = 1e-6
ENGS = ("sp", "act", "dve", "pool", "pe")
TWO_PI = 2.0 * math.pi


class Sem:
    def __init__(self, h):
        self.h = h
        self.n = 0


class Buf:
    def __init__(self, ap=None):
        self.ap = ap
        self.w = {}
        self.r = {}


def _merge(dst, src):
    for s, v in src.items():
        if dst.get(s, 0) < v:
            dst[s] = v


class Prog:
    def __init__(self, nc, n_dma_sems=16):
        self.nc = nc
        self.stack = ExitStack()
        self.esem = {e: Sem(self.stack.enter_context(nc.semaphore("es_" + e))) for e in ("act", "dve", "pool", "pe")}
        self.dsem = {e: [Sem(self.stack.enter_context(nc.semaphore("ds_%s%d" % (e, i)))) for i in range(n_dma_sems)]
                     for e in ("sp", "act", "pool")}
        self.dcnt = {e: 0 for e in self.dsem}
        self.q = {e: [] for e in ENGS}
        self.pending = {e: {} for e in ENGS}
        self.pend_reads = {e: [] for e in ENGS}
        self.seen = {e: {} for e in ENGS}
        self.uid = 0

    def name(self, p):
        self.uid += 1
        return "%s_%d" % (p, self.uid)

    def all_sems(self):
        out = list(self.esem.values())
        for l in self.dsem.values():
            out.extend(l)
        return out

    def op(self, eng, fn, reads=(), writes=(), dma=False, inc=True):
        waits = dict(self.pending[eng])
        self.pending[eng] = {}
        for b in reads:
            _merge(waits, b.w)
        for b in writes:
            _merge(waits, b.w)
            _merge(waits, b.r)
        tok = None
        incinfo = None
        if dma:
            sems = self.dsem[eng]
            s = sems[self.dcnt[eng] % len(sems)]
            self.dcnt[eng] += 1
            if s.n > 0 and waits.get(s, 0) < s.n:
                waits[s] = s.n
            s.n += 16
            tok = {s: s.n}
            incinfo = (s, 16)
        elif inc:
            s = self.esem[eng]
            s.n += 1
            tok = {s: s.n}
            incinfo = (s, 1)
        self.q[eng].append((fn, waits, incinfo))
        if tok is None:
            self.pend_reads[eng].extend(reads)
            self.pend_reads[eng].extend(writes)
        else:
            for b in list(reads) + self.pend_reads[eng]:
                _merge(b.r, tok)
            self.pend_reads[eng] = []
            for b in writes:
                b.w = dict(tok)
                b.r = {}
        return tok

    def dma(self, eng, out_ap, in_ap, reads=(), writes=()):
        return self.op(eng, lambda e: e.dma_start(out=out_ap, in_=in_ap), reads=reads, writes=writes, dma=True)

    def barrier(self):
        allv = {s: s.n for s in self.all_sems() if s.n > 0}
        for e in ENGS:
            _merge(self.pending[e], allv)

    def flush(self, final=False):
        nc = self.nc
        q = self.q
        if final:
            allv = {s: s.n for s in self.all_sems() if s.n > 0}
        with nc.Block() as block:
            def run(eng, e):
                seen = self.seen[eng]
                own = self.esem.get(eng)
                for fn, waits, incinfo in q[eng]:
                    for s, v in waits.items():
                        if eng == "pe" and s is own:
                            continue
                        if seen.get(s, 0) >= v:
                            continue
                        e.wait_ge(s.h, v)
                        seen[s] = v
                    ins = fn(e)
                    if incinfo is not None:
                        ins.then_inc(incinfo[0].h, incinfo[1])
                if final and eng == "sp":
                    for s, v in allv.items():
                        if seen.get(s, 0) < v:
                            e.wait_ge(s.h, v)
                            seen[s] = v

            @block.sync
            def _(e):
                run("sp", e)

            @block.scalar
            def _(e):
                run("act", e)

            @block.vector
            def _(e):
                run("dve", e)

            @block.gpsimd
            def _(e):
                run("pool", e)

            @block.tensor
            def _(e):
                run("pe", e)
        self.q = {e: [] for e in ENGS}


class Phase:
    def __init__(self, P):
        self.P = P
        self.nc = P.nc
        self.es = ExitStack()

    def __enter__(self):
        self.es.__enter__()
        return self

    def __exit__(self, *a):
        if a[0] is None:
            self.P.barrier()
            self.P.flush()
        return self.es.__exit__(*a)

    def sb(self, shape, dt, name="t"):
        t = self.es.enter_context(self.nc.sbuf_tensor(self.P.name(name), list(shape), dt))
        return t

    def ps(self, shape, dt=F32, name="p"):
        t = self.es.enter_context(self.nc.psum_tensor(self.P.name(name), list(shape), dt))
        return t

    def sbufs(self, n, shape, dt, name="r"):
        return [Buf(self.sb(shape, dt, name)) for _ in range(n)]

    def psbufs(self, n, shape, dt=F32, name="pr"):
        return [Buf(self.ps(shape, dt, name)) for _ in range(n)]


class Ring:
    def __init__(self, bufs):
        self.bufs = bufs
        self.i = -1

    def next(self):
        self.i += 1
        return self.bufs[self.i % len(self.bufs)]


def load_cast(P, ph, dst_ap, src_ap, shape, stg_ring, dstbuf, eng_cycle, scale_ap=None, scale_buf=None):
    stg = stg_ring.next()
    sap = stg.ap
    idx = tuple([slice(0, shape[0])] + [slice(0, s) for s in shape[1:]])
    view = sap[idx]
    P.dma("sp", view, src_ap, writes=[stg])
    eng = eng_cycle[0]
    eng_cycle.append(eng_cycle.pop(0))
    rd = [stg] + ([scale_buf] if scale_buf is not None else [])
    if eng == "act":
        if scale_ap is None:
            P.op("act", lambda e: e.activation(out=dst_ap, in_=view, func=AF.Copy), reads=rd, writes=[dstbuf])
        else:
            P.op("act", lambda e: e.activation(out=dst_ap, in_=view, func=AF.Copy, scale=scale_ap), reads=rd,
                 writes=[dstbuf])
    else:
        if scale_ap is None:
            P.op(eng, lambda e: e.tensor_copy(out=dst_ap, in_=view), reads=rd, writes=[dstbuf])
        else:
            P.op(eng, lambda e: e.tensor_scalar(out=dst_ap, in0=view, scalar1=scale_ap, scalar2=None, op0=ALU.mult),
                 reads=rd, writes=[dstbuf])


class StatRing:
    def __init__(self, ph, n=8):
        self.ss = ph.sb([128, n], F32, "ss")
        self.rs = ph.sb([128, n], F32, "rs")
        self.sb_ = [Buf(self.ss[:, i:i + 1]) for i in range(n)]
        self.rb_ = [Buf(self.rs[:, i:i + 1]) for i in range(n)]
        self.i = -1
        self.n = n

    def next(self):
        self.i += 1
        k = self.i % self.n
        return self.sb_[k], self.rb_[k]


def rms_norm_to(P, out_ap, outbuf, x_ap, xbuf, n, junk, stat, eps_b, post=None):
    ssb, rsb = stat.next()
    npart = x_ap.shape[0]
    ss = ssb.ap[0:npart, :]
    rs = rsb.ap[0:npart, :]
    P.op("act", lambda e: e.activation(out=junk.ap[0:npart, 0:n], in_=x_ap, func=AF.Square, accum_out=ss),
         reads=[xbuf], writes=[junk, ssb])
    P.op("act", lambda e: e.activation(out=rs, in_=ss, func=AF.Sqrt, scale=1.0 / n, bias=eps_b.ap[0:npart, :]),
         reads=[ssb, eps_b], writes=[rsb])
    P.op("dve", lambda e: e.reciprocal(out=rs, in_=rs), reads=[rsb], writes=[rsb])
    if post is not None:
        pap, pbuf = post
        P.op("dve", lambda e: e.scalar_tensor_tensor(out=out_ap, in0=x_ap, scalar=rs, in1=pap, op0=ALU.mult, op1=ALU.mult),
             reads=[xbuf, rsb, pbuf], writes=[outbuf])
        return
    P.op("dve", lambda e: e.tensor_scalar(out=out_ap, in0=x_ap, scalar1=rs, scalar2=None, op0=ALU.mult),
         reads=[xbuf, rsb], writes=[outbuf])


def make_eps(P, ph):
    eb = Buf(ph.sb([128, 1], F32, "eps"))
    P.op("pool", lambda e: e.memset(eb.ap[:], EPS), writes=[eb])
    return eb


def load_ident(P, ph, consts):
    idf = Buf(ph.sb([128, 128], F32, "idf"))
    idb = Buf(ph.sb([128, 128], BF16, "idb"))
    P.dma("sp", idf.ap[:], consts["ident"][:, :], writes=[idf])
    P.op("dve", lambda e: e.tensor_copy(out=idb.ap[:], in_=idf.ap[:]), reads=[idf], writes=[idb])
    return idf, idb


def mm(P, out_ap, lhsT_ap, rhs_ap, start, stop, reads, writes, inc=True):
    return P.op("pe", lambda e: e.matmul(out_ap, lhsT_ap, rhs_ap, start=start, stop=stop), reads=reads,
                writes=writes, inc=inc)


def tr(P, out_ap, in_ap, ident_ap, reads, writes, inc=True):
    return P.op("pe", lambda e: e.transpose(out_ap, in_ap, ident_ap), reads=reads, writes=writes, inc=inc)


def act(P, out_ap, in_ap, func, reads, writes, scale=None, bias=None):
    kw = {}
    if scale is not None:
        kw["scale"] = scale
    if bias is not None:
        kw["bias"] = bias
    return P.op("act", lambda e: e.activation(out=out_ap, in_=in_ap, func=func, **kw), reads=reads, writes=writes)


def cp(P, eng, out_ap, in_ap, reads, writes):
    if eng == "act":
        return act(P, out_ap, in_ap, AF.Copy, reads, writes)
    return P.op(eng, lambda e: e.tensor_copy(out=out_ap, in_=in_ap), reads=reads, writes=writes)


def tt(P, eng, out_ap, in0, in1, op, reads, writes):
    return P.op(eng, lambda e: e.tensor_tensor(out=out_ap, in0=in0, in1=in1, op=op), reads=reads, writes=writes)


def ts(P, eng, out_ap, in0, s1, s2, op0, op1, reads, writes):
    if s2 is None:
        return P.op(eng, lambda e: e.tensor_scalar(out=out_ap, in0=in0, scalar1=s1, scalar2=None, op0=op0),
                    reads=reads, writes=writes)
    return P.op(eng, lambda e: e.tensor_scalar(out=out_ap, in0=in0, scalar1=s1, scalar2=s2, op0=op0, op1=op1),
                reads=reads, writes=writes)


def stt(P, eng, out_ap, in0, scalar, in1, op0, op1, reads, writes):
    return P.op(eng, lambda e: e.scalar_tensor_tensor(out=out_ap, in0=in0, scalar=scalar, in1=in1, op0=op0, op1=op1),
                reads=reads, writes=writes)


def phase_mlp(P, X, T, w1, w2, gT, consts, final=None):
    TT = 256
    NS = TT // 128
    with Phase(P) as ph:
        W1s = ph.sb([128, 8, DFF], BF16, "w1s")
        W2s = ph.sb([128, 32, D], BF16, "w2s")
        W1b = [Buf() for _ in range(8)]
        W2b = [Buf() for _ in range(16)]
        gtb = Buf(ph.sb([128, 8], F32, "gT"))
        P.dma("sp", gtb.ap[:], gT, writes=[gtb])
        with Phase(P) as ph2:
            stg = Ring(ph2.sbufs(3, [128, 2048], F32, "stg"))
            cyc = ["act", "dve"]
            for kc in range(8):
                for hf in range(2):
                    load_cast(P, ph2, W1s[:, kc, hf * 2048:(hf + 1) * 2048], w1[kc * 128:(kc + 1) * 128, hf * 2048:(hf + 1) * 2048],
                              [128, 2048], stg, W1b[kc], cyc, scale_ap=gtb.ap[:, kc:kc + 1], scale_buf=gtb)
            w2v = w2.rearrange("(fc p) d -> p fc d", p=128)
            for f2 in range(16):
                st = stg.next()
                v = st.ap[:].rearrange("p (a d) -> p a d", a=2)
                P.dma("sp", v, w2v[:, 2 * f2:2 * f2 + 2, :], writes=[st])
                eng = cyc[0]
                cyc.append(cyc.pop(0))
                cp(P, eng, W2s[:, 2 * f2:2 * f2 + 2, :], v, [st], [W2b[f2]])
        idf, idb = load_ident(P, ph, consts)
        epsb = make_eps(P, ph)
        xt_r = Ring(ph.sbufs(3 * NS, [128, D], F32, "xt"))
        hb_r = Ring(ph.sbufs(2, [128, D], BF16, "hb"))
        junk = Buf(ph.sb([128, D], F32, "junk"))
        stat = StatRing(ph)
        hT_r = Ring(ph.sbufs(2, [128, 8, TT], BF16, "hT"))
        aT = ph.sb([128, 32, TT], BF16, "aT")
        aTb = [Buf() for _ in range(16)]
        tmp_r = Ring(ph.sbufs(3, [128, 2, TT], BF16, "rl"))
        xo_r = Ring(ph.sbufs(2, [128, D], F32, "xo"))
        tp_r = Ring(ph.psbufs(2, [128, 8, 128], BF16, "tp"))
        up_r = Ring(ph.psbufs(2, [128, 2, TT], F32, "up"))
        dn_r = Ring(ph.psbufs(4, [128, 512], F32, "dn"))
        if final is not None:
            gfb = Buf(ph.sb([128, D], F32, "gfin"))
            P.dma("sp", gfb.ap[:], final[0], writes=[gfb])
            yo_r = Ring(ph.sbufs(1, [128, D], F32, "yo"))
        ntile = T // TT
        xts = {}

        def issue_load(j):
            for s in range(NS):
                xb = xt_r.next()
                t0 = j * TT + s * 128
                P.dma("sp", xb.ap[:], X[t0:t0 + 128, :], writes=[xb])
                xts[(j, s)] = xb

        hTs = {}
        hbs = {}

        def prep_norm(j):
            for s in range(NS):
                xb = xts[(j, s)]
                hb = hb_r.next()
                rms_norm_to(P, hb.ap[:], hb, xb.ap[:], xb, D, junk, stat, epsb)
                hbs[(j, s)] = hb

        def prep_tr(j):
            hT = hT_r.next()
            hTs[j] = hT
            for s in range(NS):
                hb = hbs.pop((j, s))
                tp = tp_r.next()
                for kc in range(8):
                    tr(P, tp.ap[:, kc, :], hb.ap[:, kc * 128:(kc + 1) * 128], idb.ap[:], [hb, idb], [tp], inc=(kc == 7))
                cp(P, "act" if s == 0 else "dve", hT.ap[:, :, s * 128:(s + 1) * 128], tp.ap[:], [tp], [hT])

        def prep(j):
            prep_norm(j)
            prep_tr(j)

        issue_load(0)
        if ntile > 1:
            issue_load(1)
        prep(0)
        for j in range(ntile):
            hT = hTs.pop(j)
            for mp in range(16):
                up = up_r.next()
                for mi in range(2):
                    m = 2 * mp + mi
                    for kc in range(8):
                        mm(P, up.ap[:, mi, :], W1s[:, kc, m * 128:(m + 1) * 128], hT.ap[:, kc, :], kc == 0, kc == 7,
                           [W1b[kc], hT], [up], inc=(kc == 7 and mi == 1))
                tmp = tmp_r.next()
                act(P, tmp.ap[:], up.ap[:], AF.Relu, [up], [tmp])
                tt(P, "pool", aT[:, 2 * mp:2 * mp + 2, :], tmp.ap[:], tmp.ap[:], ALU.mult, [tmp], [aTb[mp]])
            if j + 2 < ntile:
                issue_load(j + 2)
            if j + 1 < ntile:
                prep_norm(j + 1)
            for s in range(NS):
                if s == 1 and j + 1 < ntile:
                    prep_tr(j + 1)
                xb = xts.pop((j, s))
                xo = xo_r.next()
                t0 = j * TT + s * 128
                for cc in range(2):
                    dn = dn_r.next()
                    for fc in range(32):
                        mm(P, dn.ap[:], aT[:, fc, s * 128:(s + 1) * 128], W2s[:, fc, cc * 512:(cc + 1) * 512],
                           fc == 0, fc == 31, [aTb[fc // 2], W2b[fc // 2]], [dn], inc=(fc == 31))
                    tt(P, "dve", xo.ap[:, cc * 512:(cc + 1) * 512], dn.ap[:], xb.ap[:, cc * 512:(cc + 1) * 512], ALU.add,
                       [dn, xb], [xo])
                if final is None:
                    P.dma("pool", X[t0:t0 + 128, :], xo.ap[:], reads=[xo])
                else:
                    yo = yo_r.next()
                    rms_norm_to(P, yo.ap[:], yo, xo.ap[:], xo, D, junk, stat, epsb, post=(gfb.ap[:], gfb))
                    for (r0, nr, oap) in final[1]:
                        if r0 <= t0 < r0 + nr:
                            P.dma("pool", oap[t0 - r0:t0 - r0 + 128, :], yo.ap[:], reads=[yo])


def dram_ap(base, offset_elems, dims):
    return bass.AP(tensor=base.tensor, offset=base.offset + offset_elems, ap=[list(d) for d in dims])


def phase_pool(P, X, seqs, pw, gT, psb, consts):
    with Phase(P) as ph:
        Wp = ph.sb([128, 8, 256], BF16, "wp")
        Wpb = [Buf() for _ in range(8)]
        gtb = Buf(ph.sb([128, 8], F32, "gT"))
        psc = Buf(ph.sb([128, D], F32, "psc"))
        MT = Buf(ph.sb([128, 4, 5, 128], BF16, "MT"))
        P.dma("sp", gtb.ap[:], gT, writes=[gtb])
        P.dma("sp", psc.ap[:], psb, writes=[psc])
        P.dma("sp", MT.ap[:], consts["poolM"], writes=[MT])
        stg = Ring(ph.sbufs(3, [128, 256], F32, "stg"))
        for cc in range(8):
            g, hf = cc // 2, cc % 2
            st = stg.next()
            P.dma("sp", st.ap[:], pw[g, hf * 128:(hf + 1) * 128, :], writes=[st])
            stt(P, "dve", Wp[:, cc, :], st.ap[:], gtb.ap[:, cc:cc + 1], psc.ap[:, g * 256:(g + 1) * 256], ALU.mult, ALU.mult,
                [st, gtb, psc], [Wpb[cc]])
        epsb = make_eps(P, ph)
        xt_r = Ring(ph.sbufs(5, [128, D], F32, "xt"))
        xr_r = Ring(ph.sbufs(5, [128, D], BF16, "xr"))
        junk = Buf(ph.sb([128, D], F32, "junk"))
        stat = StatRing(ph)
        dTs_t = [ph.sb([128, 8, 128], BF16, "dTs") for _ in range(2)]
        dTs_b = [[Buf(), Buf()] for _ in range(2)]
        xo_r = Ring(ph.sbufs(2, [128, D], F32, "xo"))
        dT_r = Ring(ph.psbufs(4, [128, 4, 128], F32, "dT"))
        po_r = Ring(ph.psbufs(4, [128, 512], F32, "po"))
        it = 0
        for (s0, L) in seqs:
            nt = L // 128
            xts, xrs = {}, {}
            def pnorm(i):
                xb = xt_r.next()
                P.dma("sp", xb.ap[:], X[s0 + i * 128:s0 + (i + 1) * 128, :], writes=[xb])
                xr = xr_r.next()
                rms_norm_to(P, xr.ap[:], xr, xb.ap[:], xb, D, junk, stat, epsb)
                xts[i], xrs[i] = xb, xr

            pnorm(0)
            for i in range(nt + 1):
                if i + 1 < nt:
                    pnorm(i + 1)
                if i >= 1:
                    o = i - 1
                    srcs = []
                    if o >= 1:
                        srcs.append((xrs[o - 1], 0))
                    srcs.append((xrs[o], 3 if o == 0 else (4 if o == nt - 1 else 1)))
                    if o + 1 < nt:
                        srcs.append((xrs[o + 1], 2))
                    k = it % 2
                    it += 1
                    dTs = dTs_t[k]
                    for b in range(2):
                        dT = dT_r.next()
                        for ci in range(4):
                            cc = b * 4 + ci
                            g = cc // 2
                            for si, (xr, var) in enumerate(srcs):
                                mm(P, dT.ap[:, ci, :], xr.ap[:, cc * 128:(cc + 1) * 128], MT.ap[:, g, var, :], si == 0,
                                   si == len(srcs) - 1, [xr, MT], [dT], inc=(ci == 3 and si == len(srcs) - 1))
                        cp(P, "act" if b == 0 else "dve", dTs[:, b * 4:(b + 1) * 4, :], dT.ap[:], [dT], [dTs_b[k][b]])
                    xo = xo_r.next()
                    xb = xts.pop(o)
                    for pb in range(2):
                        po = po_r.next()
                        for gi in range(2):
                            g = pb * 2 + gi
                            for hf in range(2):
                                cc = g * 2 + hf
                                mm(P, po.ap[:, gi * 256:(gi + 1) * 256], dTs[:, cc, :], Wp[:, cc, :], hf == 0, hf == 1,
                                   [dTs_b[k][cc // 4], Wpb[cc]], [po], inc=(gi == 1 and hf == 1))
                        tt(P, "dve", xo.ap[:, pb * 512:(pb + 1) * 512], po.ap[:], xb.ap[:, pb * 512:(pb + 1) * 512], ALU.add,
                           [po, xb], [xo])
                    P.dma("pool", X[s0 + o * 128:s0 + (o + 1) * 128, :], xo.ap[:], reads=[xo])
                    if o >= 1:
                        xrs.pop(o - 1)


def phase_wout(P, X, T, ZH, OT, wout, consts):
    TT = 512
    with Phase(P) as ph:
        Wo = ph.sb([128, 8, D], BF16, "wo")
        Wob = [Buf() for _ in range(8)]
        with Phase(P) as ph2:
            stg = Ring(ph2.sbufs(3, [128, D], F32, "stg"))
            cyc = ["act", "dve"]
            for kc in range(8):
                load_cast(P, ph2, Wo[:, kc, :], wout[kc * 128:(kc + 1) * 128, :], [128, D], stg, Wob[kc], cyc)
        idf, idb = load_ident(P, ph, consts)
        mix_r = Ring(ph.sbufs(2, [128, 8, TT], BF16, "mixT"))
        zh_r = Ring(ph.sbufs(3, [128, 512], BF16, "zh"))
        xt_r = Ring(ph.sbufs(3, [128, D], F32, "xt"))
        xo_r = Ring(ph.sbufs(2, [128, D], F32, "xo"))
        tp_r = Ring(ph.psbufs(2, [128, 4, 128], BF16, "tp"))
        dn_r = Ring(ph.psbufs(4, [128, 512], F32, "dn"))
        OTv = OT.rearrange("(kc p) t -> p kc t", p=128)
        for j in range(T // TT):
            t0 = j * TT
            mix = mix_r.next()
            P.dma("sp", mix.ap[:, 4:8, :], OTv[:, :, t0:t0 + TT], writes=[mix])
            for s in range(4):
                zh = zh_r.next()
                P.dma("sp", zh.ap[:], ZH[t0 + s * 128:t0 + (s + 1) * 128, :], writes=[zh])
                tp = tp_r.next()
                for kc in range(4):
                    tr(P, tp.ap[:, kc, :], zh.ap[:, kc * 128:(kc + 1) * 128], idb.ap[:], [zh, idb], [tp], inc=(kc == 3))
                cp(P, "act", mix.ap[:, 0:4, s * 128:(s + 1) * 128], tp.ap[:], [tp], [mix])
            for s in range(4):
                r0 = t0 + s * 128
                xb = xt_r.next()
                P.dma("sp", xb.ap[:], X[r0:r0 + 128, :], writes=[xb])
                xo = xo_r.next()
                for cc in range(2):
                    dn = dn_r.next()
                    for kc in range(8):
                        mm(P, dn.ap[:], mix.ap[:, kc, s * 128:(s + 1) * 128], Wo[:, kc, cc * 512:(cc + 1) * 512], kc == 0,
                           kc == 7, [mix, Wob[kc]], [dn], inc=(kc == 7))
                    tt(P, "dve", xo.ap[:, cc * 512:(cc + 1) * 512], dn.ap[:], xb.ap[:, cc * 512:(cc + 1) * 512], ALU.add,
                       [dn, xb], [xo])
                P.dma("pool", X[r0:r0 + 128, :], xo.ap[:], reads=[xo])


def phase_attn(P, seqs, QT, KT, KR, V, OT, RCD, side=None):
    rcd_r = Ring([Buf(RCD[i:i + 1, :]) for i in range(RCD.shape[0])])
    for qi_, (s0, L) in enumerate(seqs):
        nkb = L // 128
        npair = nkb // 2
        with Phase(P) as ph:
            gen = side(ph) if (side is not None and qi_ == 0) else None
            VCH = min(8, nkb)
            Vall = ph.sb([128, nkb, NH * 65], BF16, "vall")
            Vb = [Buf() for _ in range((nkb + VCH - 1) // VCH)]
            Vv = V[s0:s0 + L, :].rearrange("(kb p) c -> p kb c", p=128)
            for i in range(len(Vb)):
                P.dma("sp", Vall[:, i * VCH:(i + 1) * VCH, :], Vv[:, i * VCH:(i + 1) * VCH, :], writes=[Vb[i]])
            kt_r = Ring(ph.sbufs(2, [96, L], BF16, "kth"))
            q_r = Ring(ph.sbufs(3, [96, 512], BF16, "qt"))
            pt_r = Ring(ph.sbufs(3, [128, 2, 512], BF16, "pt"))
            rc_r = Ring(ph.sbufs(2, [128, 512], F32, "rc"))
            bcs_r = Ring(ph.sbufs(2, [64, 512], F32, "bcs"))
            o_r = Ring(ph.sbufs(2, [64, 512], BF16, "oT"))
            ones = Buf(ph.sb([128, 64], F32, "ones"))
            P.op("dve", lambda e: e.memset(ones.ap[:], 1.0), writes=[ones])
            st_r = Ring(ph.psbufs(3, [128, 2, 512], F32, "st"))
            ot_r = Ring(ph.psbufs(2, [128, 512], F32, "ot"))
            steps = []
            tiles = []
            for h in range(NH):
                for qi in range(L // 512):
                    tiles.append((h, qi))
            state = {}

            def tile_ctx(ti):
                if ti not in state:
                    h, qi = tiles[ti]
                    if qi == 0:
                        kt = kt_r.next()
                        P.dma("sp", kt.ap[0:64, :], KT[h, 0:64, s0:s0 + L], writes=[kt])
                        P.dma("sp", kt.ap[64:96, :], KR[:, s0:s0 + L], writes=[kt])
                        state[("kt", h)] = kt
                    q = q_r.next()
                    P.dma("sp", q.ap[:], QT[h, :, s0 + qi * 512:s0 + (qi + 1) * 512], writes=[q])
                    state[ti] = (state[("kt", h)], q)
                return state[ti]

            nst = len(tiles) * npair
            sts = {}

            def emit_s(g):
                ti, i = divmod(g, npair)
                kt, q = tile_ctx(ti)
                st = st_r.next()
                for u in range(2):
                    kb = 2 * i + u
                    mm(P, st.ap[:, u, :], kt.ap[:, kb * 128:(kb + 1) * 128], q.ap[:], True, True, [kt, q], [st], inc=(u == 1))
                sts[g] = st

            AHEAD = 2
            for g in range(min(AHEAD, nst)):
                emit_s(g)
            ot = None
            for g in range(nst):
                if gen is not None and g % 40 == 20:
                    if next(gen, "done") == "done":
                        gen = None
                ti, i = divmod(g, npair)
                h, qi = tiles[ti]
                if i == 0:
                    ot = ot_r.next()
                if g + AHEAD < nst:
                    emit_s(g + AHEAD)
                st = sts.pop(g)
                pt = pt_r.next()
                act(P, pt.ap[:], st.ap[:], AF.Exp, [st], [pt])
                for u in range(2):
                    kb = 2 * i + u
                    mm(P, ot.ap[0:65, :], Vall[:, kb, h * 65:(h + 1) * 65], pt.ap[:, u, :], kb == 0, kb == nkb - 1,
                       [Vb[kb // VCH], pt], [ot], inc=(u == 1))
                if i == npair - 1:
                    rc = rc_r.next()
                    P.op("dve", (lambda rc=rc, ot=ot: lambda e: e.reciprocal(out=rc.ap[64:65, :], in_=ot.ap[64:65, :]))(),
                         reads=[ot], writes=[rc])
                    bcs = bcs_r.next()
                    rd = rcd_r.next()
                    P.dma("pool", rd.ap, rc.ap[64:65, :], reads=[rc], writes=[rd])
                    P.dma("pool", bcs.ap[:], rd.ap.partition_broadcast(64), reads=[rd], writes=[bcs])
                    oT = o_r.next()
                    tt(P, "dve", oT.ap[:], ot.ap[0:64, :], bcs.ap[:], ALU.mult, [ot, bcs], [oT])
                    P.dma("pool", OT[h * 64:(h + 1) * 64, s0 + qi * 512:s0 + (qi + 1) * 512], oT.ap[:], reads=[oT])
                    state.pop(ti, None)
            if gen is not None:
                for _ in gen:
                    pass


def phase_proj(P, X, seqs, w_in, w_uq, w_ukv, gT, qnT, kvnT, PHY, QT, KT, KR, V, consts):
    SC = 96.0 ** -0.5
    with Phase(P) as ph:
        Win = ph.sb([128, 8, 2176], BF16, "win")
        Winb = [Buf() for _ in range(8)]
        Wkr = Buf(ph.sb([128, 8, 2, 96], BF16, "wkr"))
        Wuq = Buf(ph.sb([128, 3, 2, 768], BF16, "wuq"))
        Wkn = Buf(ph.sb([128, 2, 8, 64], BF16, "wkn"))
        Wv = Buf(ph.sb([128, 2, 8, 64], BF16, "wv"))
        gtb = Buf(ph.sb([128, 8], F32, "gT"))
        qnb = Buf(ph.sb([128, 3], F32, "qn"))
        kvnb = Buf(ph.sb([128, 2], F32, "kvn"))
        P.dma("sp", gtb.ap[:], gT, writes=[gtb])
        P.dma("sp", qnb.ap[:], qnT, writes=[qnb])
        P.dma("sp", kvnb.ap[:], kvnT, writes=[kvnb])
        P.op("pool", lambda e: e.memset(Wkr.ap[:], 0.0), writes=[Wkr])
        P.op("pool", lambda e: e.memset(Wuq.ap[:, :, 1, :], 0.0), writes=[Wuq])
        with Phase(P) as ph2:
            stg = Ring(ph2.sbufs(3, [128, 1104], F32, "stg"))
            for kc in range(8):
                g = gtb.ap[:, kc:kc + 1]
                st = stg.next()
                P.dma("sp", st.ap[:], w_in[kc * 128:(kc + 1) * 128, 0:1104], writes=[st])
                ts(P, "dve", Win[:, kc, 0:1104], st.ap[:], g, None, ALU.mult, None, [st, gtb], [Winb[kc]])
                st = stg.next()
                P.dma("sp", st.ap[:], w_in[kc * 128:(kc + 1) * 128, 1104:2208], writes=[st])
                act(P, Win[:, kc, 1104:2176], st.ap[:, 0:1072], AF.Copy, [st, gtb], [Winb[kc]], scale=g)
                ts(P, "dve", Wkr.ap[:, kc, 0, 64:96], st.ap[:, 1072:1104], g, None, ALU.mult, None, [st, gtb], [Wkr])
                ts(P, "dve", Wkr.ap[:, kc, 1, 64:80], st.ap[:, 1088:1104], g, None, ALU.mult, None, [st, gtb], [Wkr])
                ts(P, "dve", Wkr.ap[:, kc, 1, 80:96], st.ap[:, 1072:1088], g, None, ALU.mult, None, [st, gtb], [Wkr])
            for kc in range(3):
                g = qnb.ap[:, kc:kc + 1]
                st = stg.next()
                P.dma("sp", st.ap[:, 0:768], w_uq[kc * 128:(kc + 1) * 128, :], writes=[st])
                sv = st.ap[:, 0:768].rearrange("p (h c) -> p h c", c=96)
                dv = Wuq.ap[:, kc, 1, :].rearrange("p (h c) -> p h c", c=96)
                act(P, Wuq.ap[:, kc, 0, :], st.ap[:, 0:768], AF.Copy, [st, qnb], [Wuq], scale=g)
                ts(P, "dve", dv[:, :, 64:80], sv[:, :, 80:96], g, None, ALU.mult, None, [st, qnb], [Wuq])
                ts(P, "dve", dv[:, :, 80:96], sv[:, :, 64:80], g, None, ALU.mult, None, [st, qnb], [Wuq])
            for kc in range(2):
                g = kvnb.ap[:, kc:kc + 1]
                st = stg.next()
                P.dma("sp", st.ap[:, 0:1024], w_ukv[kc * 128:(kc + 1) * 128, :], writes=[st])
                sv = st.ap[:, 0:1024].rearrange("p (h c) -> p h c", c=128)
                ts(P, "dve", Wkn.ap[:, kc, :, :], sv[:, :, 0:64], g, None, ALU.mult, None, [st, kvnb], [Wkn])
                ts(P, "pool", Wv.ap[:, kc, :, :], sv[:, :, 64:128], g, None, ALU.mult, None, [st, kvnb], [Wv])
        idf, idb = load_ident(P, ph, consts)
        zero = Buf(ph.sb([1, HYC], F32, "zero"))
        P.op("pool", lambda e: e.memset(zero.ap[:], 0.0), writes=[zero])
        for si, (s0, L) in enumerate(seqs):
            P.dma("sp", PHY[si][0:1, :], zero.ap[:], reads=[zero])
            P.dma("sp", PHY[si][L + 1:L + 2, :], zero.ap[:], reads=[zero])
        epsb = make_eps(P, ph)
        xt_r = Ring(ph.sbufs(5, [128, D], F32, "xt"))
        hb_r = Ring(ph.sbufs(3, [128, D], BF16, "hb"))
        junk = Buf(ph.sb([128, D], F32, "junk"))
        stat = StatRing(ph, 12)
        hT_r = Ring(ph.sbufs(3, [128, 8, 512], BF16, "hT"))
        phy_r = Ring(ph.sbufs(2, [128, HYC], F32, "phy"))
        cl_r = Ring(ph.sbufs(2, [128, 640], F32, "cl"))
        cln_r = Ring(ph.sbufs(2, [128, 640], BF16, "cln"))
        clT_r = Ring(ph.sbufs(3, [128, 5, 512], BF16, "clT"))
        vt_t = [ph.sb([128, 8, 65], BF16, "vt") for _ in range(2)]
        vt_b = [Buf(t) for t in vt_t]
        for t in vt_b:
            P.op("pool", (lambda t=t: lambda e: e.memset(t.ap[:], 1.0))(), writes=[t])
        vt_r = Ring(vt_b)
        rt_r = Ring(ph.sbufs(3, [96, 4, 512], F32, "rope"))
        tA_r = Ring(ph.sbufs(2, [96, 512], F32, "tA"))
        tB_r = Ring(ph.sbufs(2, [96, 512], F32, "tB"))
        kro_r = Ring(ph.sbufs(2, [96, 512], BF16, "krot"))
        qts_r = Ring(ph.sbufs(2, [96, 2, 512], BF16, "qts"))
        kts_r = Ring(ph.sbufs(2, [64, 2, 512], BF16, "kts"))
        tp_r = Ring(ph.psbufs(2, [128, 8, 128], BF16, "tp"))
        pj_r = Ring(ph.psbufs(2, [128, 512], F32, "pj"))
        ctp_r = Ring(ph.psbufs(1, [128, 5, 128], BF16, "ctp"))
        fm_r = Ring(ph.psbufs(3, [128, 512], F32, "fm"))
        rope = consts["rope"]
        ecyc = ["act", "dve"]

        def rope_mix(dst_ap, dstbuf, a, b, rt, ic, isn):
            tA = tA_r.next()
            tB = tB_r.next()
            tt(P, "dve", tA.ap[64:96, :], a.ap[64:96, :], rt.ap[64:96, ic, :], ALU.mult, [a, rt], [tA])
            tt(P, "dve", tB.ap[64:96, :], b.ap[64:96, :], rt.ap[64:96, isn, :], ALU.mult, [b, rt], [tB])
            tt(P, "pool", dst_ap, tA.ap[64:96, :], tB.ap[64:96, :], ALU.add, [tA, tB], [dstbuf])

        units = []
        for si, (s0, L) in enumerate(seqs):
            for j in range(L // 512):
                for s_ in range(4):
                    units.append((si, s0, L, j, s_))
        tctx = {}
        uctx = {}

        def tile_ctx(si, j):
            if (si, j) not in tctx:
                rt = rt_r.next()
                P.dma("sp", rt.ap[64:96, :, :], rope[:, :, j * 512:(j + 1) * 512].rearrange("f r t -> r f t"), writes=[rt])
                tctx[(si, j)] = (hT_r.next(), clT_r.next(), rt)
            return tctx[(si, j)]

        hbs = {}

        def prep_norm(u):
            xb = xload.pop(u)
            hb = hb_r.next()
            rms_norm_to(P, hb.ap[:], hb, xb.ap[:], xb, D, junk, stat, epsb)
            hbs[u] = hb

        def prep_tr(u):
            si, s0, L, j, s = units[u]
            hT, clT, rt = tile_ctx(si, j)
            hb = hbs.pop(u)
            tp = tp_r.next()
            for kc in range(8):
                tr(P, tp.ap[:, kc, :], hb.ap[:, kc * 128:(kc + 1) * 128], idb.ap[:], [hb, idb], [tp], inc=(kc == 7))
            cp(P, "dve", hT.ap[:, :, s * 128:(s + 1) * 128], tp.ap[:], [tp], [hT])

        xload = {}

        def load_x(u):
            si, s0, L, j, s = units[u]
            r0 = s0 + j * 512 + s * 128
            xb = xt_r.next()
            P.dma("sp", xb.ap[:], X[r0:r0 + 128, :], writes=[xb])
            xload[u] = xb

        def body(u):
            si, s0, L, j, s = units[u]
            hT, clT, rt = tile_ctx(si, j)
            p0 = j * 512
            r0 = s0 + p0 + s * 128
            phy = phy_r.next()
            cl = cl_r.next()
            for (c0, c1) in ((0, 512), (512, 1024), (1024, 1536), (1536, 2048), (2048, 2176)):
                pj = pj_r.next()
                w = c1 - c0
                for kc in range(8):
                    mm(P, pj.ap[:, 0:w], hT.ap[:, kc, s * 128:(s + 1) * 128], Win[:, kc, c0:c1], kc == 0, kc == 7,
                       [hT, Winb[kc]], [pj], inc=(kc == 7))
                eng = ecyc[0]
                ecyc.append(ecyc.pop(0))
                if c0 < 1536:
                    cp(P, eng, phy.ap[:, c0:c1], pj.ap[:, 0:w], [pj], [phy])
                else:
                    cp(P, eng, cl.ap[:, c0 - 1536:c1 - 1536], pj.ap[:, 0:w], [pj], [cl])
            P.dma("pool", PHY[si][1 + p0 + s * 128:1 + p0 + (s + 1) * 128, :], phy.ap[:], reads=[phy])
            cln = cln_r.next()
            rms_norm_to(P, cln.ap[:, 0:QL], cln, cl.ap[:, 0:QL], cl, QL, junk, stat, epsb)
            rms_norm_to(P, cln.ap[:, QL:640], cln, cl.ap[:, QL:640], cl, KVL, junk, stat, epsb)
            uctx[u] = cln

        def body_b(u):
            si, s0, L, j, s = units[u]
            hT, clT, rt = tile_ctx(si, j)
            r0 = s0 + j * 512 + s * 128
            cln = uctx.pop(u)
            ctp = ctp_r.next()
            for kc in range(5):
                tr(P, ctp.ap[:, kc, :], cln.ap[:, kc * 128:(kc + 1) * 128], idb.ap[:], [cln, idb], [ctp], inc=(kc == 4))
            cp(P, "dve", clT.ap[:, :, s * 128:(s + 1) * 128], ctp.ap[:], [ctp], [clT])
            vps = pj_r.next()
            for kc in range(2):
                mm(P, vps.ap[:], clT.ap[:, 3 + kc, s * 128:(s + 1) * 128], Wv.ap[:, kc, :, :].rearrange("p h c -> p (h c)"),
                   kc == 0, kc == 1, [clT, Wv], [vps], inc=(kc == 1))
            vt = vt_r.next()
            cp(P, "act", vt.ap[:, :, 0:64], vps.ap[:].rearrange("p (h c) -> p h c", c=64), [vps], [vt])
            P.dma("sp", V[r0:r0 + 128, :], vt.ap[:].rearrange("p h c -> p (h c)"), reads=[vt])

        def fm_part(si, s0, j, part):
            hT, clT, rt = tctx[(si, j)]
            g0 = s0 + j * 512
            if part == 0:
                kra = fm_r.next()
                for kc in range(8):
                    mm(P, kra.ap[0:96, :], Wkr.ap[:, kc, 0, :], hT.ap[:, kc, :], kc == 0, kc == 7, [Wkr, hT], [kra], inc=(kc == 7))
                krb = fm_r.next()
                for kc in range(8):
                    mm(P, krb.ap[0:96, :], Wkr.ap[:, kc, 1, :], hT.ap[:, kc, :], kc == 0, kc == 7, [Wkr, hT], [krb], inc=(kc == 7))
                kro = kro_r.next()
                rope_mix(kro.ap[64:96, :], kro, kra, krb, rt, 2, 3)
                P.dma("sp", KR[:, g0:g0 + 512], kro.ap[64:96, :], reads=[kro])
            qts = qts_r.next()
            kts = kts_r.next()
            for hi, h in enumerate((2 * part, 2 * part + 1)):
                qa = fm_r.next()
                for kc in range(3):
                    mm(P, qa.ap[0:96, :], Wuq.ap[:, kc, 0, h * 96:(h + 1) * 96], clT.ap[:, kc, :], kc == 0, kc == 2,
                       [Wuq, clT], [qa], inc=(kc == 2))
                qb = fm_r.next()
                for kc in range(3):
                    mm(P, qb.ap[0:96, :], Wuq.ap[:, kc, 1, h * 96:(h + 1) * 96], clT.ap[:, kc, :], kc == 0, kc == 2,
                       [Wuq, clT], [qb], inc=(kc == 2))
                act(P, qts.ap[0:64, hi, :], qa.ap[0:64, :], AF.Copy, [qa], [qts], scale=SC)
                rope_mix(qts.ap[64:96, hi, :], qts, qa, qb, rt, 0, 1)
                kn = fm_r.next()
                for kc in range(2):
                    mm(P, kn.ap[0:64, :], Wkn.ap[:, kc, h, :], clT.ap[:, 3 + kc, :], kc == 0, kc == 1, [Wkn, clT], [kn],
                       inc=(kc == 1))
                cp(P, "act", kts.ap[:, hi, :], kn.ap[0:64, :], [kn], [kts])
            h0 = 2 * part
            P.dma("sp", QT[h0:h0 + 2, :, g0:g0 + 512].rearrange("h r t -> r h t"), qts.ap[:], reads=[qts])
            P.dma("sp", KT[h0:h0 + 2, 0:64, g0:g0 + 512].rearrange("h r t -> r h t"), kts.ap[:], reads=[kts])
            if part == 3:
                tctx.pop((si, j))

        nu = len(units)
        for u in range(min(3, nu)):
            load_x(u)
        prep_norm(0)
        if nu > 1:
            prep_norm(1)
        prep_tr(0)
        for u in range(nu + 4):
            if u + 3 < nu:
                load_x(u + 3)
            if u + 2 < nu:
                prep_norm(u + 2)
            if u < nu:
                body(u)
            if 1 <= u <= nu:
                body_b(u - 1)
            k = u - 4
            if 0 <= k < nu:
                si, s0, L, j, s_ = units[k]
                fm_part(si, s0, j, s_)
            if u + 1 < nu:
                prep_tr(u + 1)


def shortconv_gen(P, ph, jobs, cwb, cbb):
    Rc = 4
    wt = Buf(ph.sb([128, 3, 512], F32, "cw"))
    bt = Buf(ph.sb([128, 512], F32, "cb"))
    up_r = Ring(ph.sbufs(2, [128, Rc + 2, 512], F32, "up"))
    acc_r = Ring(ph.sbufs(2, [128, Rc, 512], F32, "acc"))
    tmp_r = Ring(ph.sbufs(2, [128, Rc, 512], F32, "tmp"))
    out_r = Ring(ph.sbufs(2, [128, Rc, 512], BF16, "uc"))
    k = 0
    for g in range(3):
        P.dma("sp", wt.ap[:], cwb[:, :, g * 512:(g + 1) * 512], writes=[wt])
        P.dma("sp", bt.ap[:], cbb[:, g * 512:(g + 1) * 512], writes=[bt])
        for (PHYs, L, UC) in jobs:
            R = L // 128
            for rc in range(R // Rc):
                up = up_r.next()
                src = dram_ap(PHYs, (rc * Rc) * HYC + g * 512, [[R * HYC, 128], [HYC, Rc + 2], [1, 512]])
                P.dma("sp", up.ap[:], src, writes=[up])

                def wb(i):
                    return wt.ap[:, i, :].unsqueeze(1).broadcast_to([128, Rc, 512])

                bb = bt.ap[:].unsqueeze(1).broadcast_to([128, Rc, 512])
                e1 = "dve" if k % 3 != 2 else "pool"
                k += 1
                acc = acc_r.next()
                tmp = tmp_r.next()
                out = out_r.next()
                tt(P, e1, acc.ap[:], up.ap[:, 0:Rc, :], wb(0), ALU.mult, [up, wt], [acc])
                tt(P, e1, tmp.ap[:], up.ap[:, 1:Rc + 1, :], wb(1), ALU.mult, [up, wt], [tmp])
                tt(P, e1, acc.ap[:], acc.ap[:], tmp.ap[:], ALU.add, [acc, tmp], [acc])
                tt(P, e1, tmp.ap[:], up.ap[:, 2:Rc + 2, :], wb(2), ALU.mult, [up, wt], [tmp])
                tt(P, e1, acc.ap[:], acc.ap[:], tmp.ap[:], ALU.add, [acc, tmp], [acc])
                tt(P, e1, out.ap[:], acc.ap[:], bb, ALU.add, [acc, bt], [out])
                dst = dram_ap(UC, (rc * Rc) * HYC + g * 512, [[R * HYC, 128], [HYC, Rc], [1, 512]])
                P.dma("pool", dst, out.ap[:], reads=[out])
                yield


def hy_filter_hidden(P, L, zext, w1, f1c, b1c, w2d, f2c, b2c, H2D):
    N = 2 * L
    OFF = 16.0 * math.pi
    with Phase(P) as ph:
        w1t = Buf(ph.sb([33, 64], F32, "w1"))
        w2t = Buf(ph.sb([64, 128], F32, "w2"))
        f1 = Buf(ph.sb([64, 1], F32, "f1"))
        b1 = Buf(ph.sb([64, 1], F32, "b1"))
        f2 = Buf(ph.sb([128, 1], F32, "f2"))
        b2 = Buf(ph.sb([128, 1], F32, "b2"))
        for (b, a) in ((w1t, w1), (w2t, w2d), (f1, f1c), (b1, b1c), (f2, f2c), (b2, b2c)):
            P.dma("sp", b.ap[:], a, writes=[b])
        ts(P, "dve", b1.ap[:], b1.ap[:], f1.ap[:, 0:1], OFF, ALU.mult, ALU.add, [b1, f1], [b1])
        ts(P, "dve", b2.ap[:], b2.ap[:], f2.ap[:, 0:1], OFF, ALU.mult, ALU.add, [b2, f2], [b2])
        z_r = Ring(ph.sbufs(2, [33, 512], F32, "z"))
        a_r = Ring(ph.sbufs(2, [128, 512], F32, "arg"))
        i_r = Ring(ph.sbufs(2, [128, 512], I32, "iq"))
        q_r = Ring(ph.sbufs(2, [128, 512], F32, "fq"))
        h1_r = Ring(ph.sbufs(2, [64, 512], F32, "h1"))
        ho_r = Ring(ph.sbufs(2, [128, 512], BF16, "h2"))
        ps_r = Ring(ph.psbufs(4, [128, 512], F32, "ps"))
        def sin_layer(ps, npart, fr, bs, out_ap, outbuf):
            a = a_r.next()
            iq = i_r.next()
            fq = q_r.next()
            av, iv, fv = a.ap[0:npart, :], iq.ap[0:npart, :], fq.ap[0:npart, :]
            ts(P, "dve", av, ps.ap[0:npart, :], fr.ap[:, 0:1], bs.ap[:, 0:1], ALU.mult, ALU.add, [ps, fr, bs], [a])
            ts(P, "dve", iv, av, 1.0 / TWO_PI, None, ALU.mult, None, [a], [iq])
            cp(P, "dve", fv, iv, [iq], [fq])
            stt(P, "dve", av, fv, -TWO_PI, av, ALU.mult, ALU.add, [fq, a], [a])
            ts(P, "dve", fv, av, math.pi, TWO_PI, ALU.is_gt, ALU.mult, [a], [fq])
            tt(P, "dve", av, av, fv, ALU.subtract, [a, fq], [a])
            act(P, out_ap, av, AF.Sin, [a], [outbuf])

        s2_r = Ring(ph.sbufs(2, [128, 512], F32, "s2"))
        for ch in range(N // 512):
            c0 = ch * 512
            zt = z_r.next()
            P.dma("sp", zt.ap[:], zext[:, c0:c0 + 512], writes=[zt])
            p1 = ps_r.next()
            mm(P, p1.ap[0:64, :], w1t.ap[:], zt.ap[:], True, True, [w1t, zt], [p1])
            h1 = h1_r.next()
            sin_layer(p1, 64, f1, b1, h1.ap[:], h1)
            p2 = ps_r.next()
            mm(P, p2.ap[:], w2t.ap[:], h1.ap[:], True, True, [w2t, h1], [p2])
            s2 = s2_r.next()
            sin_layer(p2, 128, f2, b2, s2.ap[:], s2)
            ho = ho_r.next()
            lo = 0 if c0 < L else 64
            P.op("pool", (lambda ho=ho: lambda e: e.memset(ho.ap[:], 0.0))(), writes=[ho])
            cp(P, "pool", ho.ap[lo:lo + 64, :], s2.ap[lo:lo + 64, :], [s2], [ho])
            P.dma("pool", H2D[:, c0:c0 + 512], ho.ap[:], reads=[ho])


def fft_s1(P, N1, Kin, A_scrs, F1d, make_src):
    NK = N1 // 2 + 1
    ns = len(A_scrs)
    with Phase(P) as ph:
        F1t = Buf(ph.sb([128, 2, N1], BF16, "F1"))
        P.dma("sp", F1t.ap[0:Kin, :, :], F1d[0:Kin, :, :], writes=[F1t])
        chunk = make_src(ph)
        ps_r = Ring(ph.psbufs(4, [128, 512], F32, "s1p"))
        At_r = [Ring(ph.sbufs(2, [128, 2, 8, 512], BF16, "At")) for _ in range(ns)]
        ec = ["act", "dve"] if ns == 1 else ["act", "act", "act", "dve"]
        for n2c in range(16):
            Ats = [r.next() for r in At_r]
            srcs = chunk(n2c)
            for j in range(8):
                for si in range(ns):
                    rhs, rb = srcs[j][si]
                    At = Ats[si]
                    for ri in range(2):
                        p = ps_r.next()
                        mm(P, p.ap[0:NK, :], F1t.ap[0:Kin, ri, 0:NK], rhs, True, True, [F1t, rb], [p])
                        cp(P, ec[0], At.ap[0:NK, ri, j, :], p.ap[0:NK, :], [p], [At])
                        ec.append(ec.pop(0))
            for si in range(ns):
                for ri in range(2):
                    P.dma("act" if ri == 0 else "pool", A_scrs[si][ri, n2c * 8:(n2c + 1) * 8, :, :].rearrange("n k c -> k n c"),
                          Ats[si].ap[0:NK, ri, :, :], reads=[Ats[si]])


def src_from_dram(P, U, L, c0):
    Kin = L // 128
    Uv = U.rearrange("(n1 n2) c -> n1 n2 c", n2=128)

    def make(ph):
        d_r = Ring(ph.sbufs(2, [128, 8, 512], BF16, "D"))

        def chunk(n2c):
            d = d_r.next()
            P.dma("sp", d.ap[0:Kin, :, :], Uv[:, n2c * 8:(n2c + 1) * 8, c0:c0 + 512], writes=[d])
            return [[(d.ap[0:Kin, j, :], d)] for j in range(8)]

        return chunk

    return make


def src_filter(P, L, H2D, w3s, tau, negd, hyb):
    N1 = 2 * L // 128

    def make(ph):
        H2 = Buf(ph.sb([128, 2 * L], BF16, "H2"))
        P.dma("sp", H2.ap[:], H2D, writes=[H2])
        W3 = Buf(ph.sb([128, 2, 512], BF16, "W3"))
        w3f = Buf(ph.sb([128, 2, 512], F32, "w3f"))
        P.dma("sp", w3f.ap[:], w3s, writes=[w3f])
        cp(P, "dve", W3.ap[:], w3f.ap[:], [w3f], [W3])
        taut = Buf(ph.sb([128, 128], F32, "tau"))
        P.dma("sp", taut.ap[:], tau, writes=[taut])
        ndt = Buf(ph.sb([128, 512], F32, "negd"))
        P.dma("sp", ndt.ap[:], negd, writes=[ndt])
        bia = Buf(ph.sb([1, 2, 512], F32, "hyb"))
        P.dma("sp", bia.ap[:], hyb.rearrange("(o f) c -> o f c", o=1), writes=[bia])
        kp_r = Ring(ph.psbufs(3, [128, 512], F32, "kp"))
        dec_r = Ring(ph.sbufs(3, [128, 512], F32, "dec"))
        kf_r = Ring(ph.sbufs(2, [128, 512], F32, "kf32"))
        kb_r = Ring(ph.sbufs(20, [128, 512], BF16, "kbf"))
        H2v = H2.ap[:].rearrange("p (n1 n2) -> p n2 n1", n2=128)

        def chunk(n2c):
            out = []
            for j in range(8):
                n2 = n2c * 8 + j
                dec = dec_r.next()
                act(P, dec.ap[0:N1, :], ndt.ap[0:N1, :], AF.Exp, [ndt, taut], [dec], scale=taut.ap[0:N1, n2:n2 + 1])
                row = []
                for f in range(2):
                    kp = kp_r.next()
                    mm(P, kp.ap[0:N1, :], H2v[:, n2, :], W3.ap[:, f, :], True, True, [H2, W3], [kp])
                    kb = kb_r.next()
                    if n2 == 0:
                        kf = kf_r.next()
                        tt(P, "dve", kf.ap[0:N1, :], kp.ap[0:N1, :], dec.ap[0:N1, :], ALU.mult, [kp, dec], [kf])
                        tt(P, "dve", kf.ap[0:1, :], kf.ap[0:1, :], bia.ap[:, f, :], ALU.add, [kf, bia], [kf])
                        cp(P, "dve", kb.ap[0:N1, :], kf.ap[0:N1, :], [kf], [kb])
                    else:
                        tt(P, "dve", kb.ap[0:N1, :], kp.ap[0:N1, :], dec.ap[0:N1, :], ALU.mult, [kp, dec], [kb])
                    row.append((kb.ap[0:N1, :], kb))
                out.append(row)
            return out

        return chunk

    return make


def fft_s2(P, N1, A_scr, Gd, KFf, mode, Gid=None, B_scr=None):
    KC = 4
    NK = N1 // 2 + 1
    chunks = [(k0, min(KC, NK - k0)) for k0 in range(0, NK, KC)]
    with Phase(P) as ph:
        At_r = Ring(ph.sbufs(2, [128, 2, KC, 512], BF16, "At"))
        Gt_r = Ring(ph.sbufs(2, [128, KC, 3, 128], BF16, "Gt"))
        Kt_r = Ring(ph.sbufs(2, [128, 2, KC, 512], BF16, "Kt"))
        X_r = Ring(ph.psbufs(2, [128, 2, 512], F32, "X"))
        if mode == "conv":
            Gi_r = Ring(ph.sbufs(2, [128, KC, 3, 128], BF16, "Git"))
            Yt_r = Ring(ph.sbufs(2, [128, 2, 512], BF16, "Yt"))
            Bt_r = Ring(ph.sbufs(2, [128, 2, KC, 512], BF16, "Bt"))
            tm_r = Ring(ph.sbufs(8, [128, 512], F32, "tm"))
            B_r = Ring(ph.psbufs(2, [128, 2, 512], F32, "B"))
        ctx = {}

        def chunk_ctx(ci):
            if ci not in ctx:
                k0, n = chunks[ci]
                At = At_r.next()
                for ri in range(2):
                    P.dma("sp", At.ap[:, ri, 0:n, :], A_scr[ri, :, k0:k0 + n, :], writes=[At])
                Gt = Gt_r.next()
                P.dma("sp", Gt.ap[:, 0:n, :, :], Gd[:, k0:k0 + n, :, :], writes=[Gt])
                Kt = Kt_r.next()
                Git = Bt = None
                if mode == "conv":
                    for ri in range(2):
                        P.dma("sp", Kt.ap[:, ri, 0:n, :], KFf[ri, :, k0:k0 + n, :], writes=[Kt])
                    Git = Gi_r.next()
                    P.dma("sp", Git.ap[:, 0:n, :, :], Gid[:, k0:k0 + n, :, :], writes=[Git])
                    Bt = Bt_r.next()
                ctx[ci] = (At, Gt, Kt, Git, Bt)
            return ctx[ci]

        planes = [(ci, j) for ci, (k0, n) in enumerate(chunks) for j in range(n)]
        Xs = {}

        def emit_X(g):
            ci, j = planes[g]
            At, Gt, Kt, Git, Bt = chunk_ctx(ci)
            X = X_r.next()
            rd = [Gt, At]
            mm(P, X.ap[:, 0, :], Gt.ap[:, j, 0, :], At.ap[:, 0, j, :], True, False, rd, [X], inc=False)
            mm(P, X.ap[:, 0, :], Gt.ap[:, j, 2, :], At.ap[:, 1, j, :], False, True, rd, [X], inc=False)
            mm(P, X.ap[:, 1, :], Gt.ap[:, j, 1, :], At.ap[:, 0, j, :], True, False, rd, [X], inc=False)
            mm(P, X.ap[:, 1, :], Gt.ap[:, j, 0, :], At.ap[:, 1, j, :], False, True, rd, [X])
            Xs[g] = X

        emit_X(0)
        for g in range(len(planes)):
            ci, j = planes[g]
            k0, n = chunks[ci]
            At, Gt, Kt, Git, Bt = chunk_ctx(ci)
            if g + 1 < len(planes):
                emit_X(g + 1)
            X = Xs.pop(g)
            if mode == "filter":
                cp(P, "act" if g % 2 == 0 else "dve", Kt.ap[:, :, j, :], X.ap[:], [X], [Kt])
                if j == n - 1:
                    for ri in range(2):
                        P.dma("pool", KFf[ri, :, k0:k0 + n, :], Kt.ap[:, ri, 0:n, :], reads=[Kt])
                    ctx.pop(ci)
                continue
            t1, t2, t3, t4 = tm_r.next(), tm_r.next(), tm_r.next(), tm_r.next()
            tt(P, "dve", t1.ap[:], X.ap[:, 0, :], Kt.ap[:, 0, j, :], ALU.mult, [X, Kt], [t1])
            tt(P, "dve", t2.ap[:], X.ap[:, 1, :], Kt.ap[:, 1, j, :], ALU.mult, [X, Kt], [t2])
            tt(P, "dve", t3.ap[:], X.ap[:, 0, :], Kt.ap[:, 1, j, :], ALU.mult, [X, Kt], [t3])
            tt(P, "dve", t4.ap[:], X.ap[:, 1, :], Kt.ap[:, 0, j, :], ALU.mult, [X, Kt], [t4])
            Yt = Yt_r.next()
            tt(P, "pool", Yt.ap[:, 0, :], t1.ap[:], t2.ap[:], ALU.subtract, [t1, t2], [Yt])
            tt(P, "dve", Yt.ap[:, 1, :], t3.ap[:], t4.ap[:], ALU.add, [t3, t4], [Yt])
            B = B_r.next()
            rd = [Git, Yt]
            mm(P, B.ap[:, 0, :], Git.ap[:, j, 0, :], Yt.ap[:, 0, :], True, False, rd, [B], inc=False)
            mm(P, B.ap[:, 0, :], Git.ap[:, j, 2, :], Yt.ap[:, 1, :], False, True, rd, [B], inc=False)
            mm(P, B.ap[:, 1, :], Git.ap[:, j, 1, :], Yt.ap[:, 0, :], True, False, rd, [B], inc=False)
            mm(P, B.ap[:, 1, :], Git.ap[:, j, 0, :], Yt.ap[:, 1, :], False, True, rd, [B])
            cp(P, "act", Bt.ap[:, :, j, :], B.ap[:], [B], [Bt])
            if j == n - 1:
                for ri in range(2):
                    P.dma("act", B_scr[ri, k0:k0 + n, :, :].rearrange("k n c -> n k c"), Bt.ap[:, ri, 0:n, :], reads=[Bt])
                ctx.pop(ci)


def fft_s4(P, N1, L, B_scr, F4d, gate_src, gc0, dst, dc0):
    Kh = N1 // 2
    NK = N1 // 2 + 1
    gv = gate_src.rearrange("(n1 n2) c -> n1 n2 c", n2=128)
    dv = dst.rearrange("(n1 n2) c -> n1 n2 c", n2=128)
    with Phase(P) as ph:
        F4t = Buf(ph.sb([128, 2, Kh], BF16, "F4"))
        P.dma("sp", F4t.ap[0:NK, :, :], F4d, writes=[F4t])
        Bt_r = Ring(ph.sbufs(2, [128, 2, 8, 512], BF16, "Bt4"))
        g_r = Ring(ph.sbufs(2, [128, 8, 512], BF16, "gate"))
        z_r = Ring(ph.sbufs(2, [128, 8, 512], BF16, "zt"))
        ps_r = Ring(ph.psbufs(4, [128, 512], F32, "s4p"))
        for n2c in range(16):
            Bt = Bt_r.next()
            for ri in range(2):
                P.dma("sp", Bt.ap[0:NK, ri, :, :], B_scr[ri, :, n2c * 8:(n2c + 1) * 8, :], writes=[Bt])
            g = g_r.next()
            P.dma("sp", g.ap[0:Kh, :, :], gv[:, n2c * 8:(n2c + 1) * 8, gc0:gc0 + 512], writes=[g])
            z = z_r.next()
            for j in range(8):
                p = ps_r.next()
                mm(P, p.ap[0:Kh, :], F4t.ap[0:NK, 0, :], Bt.ap[0:NK, 0, j, :], True, False, [F4t, Bt], [p], inc=False)
                mm(P, p.ap[0:Kh, :], F4t.ap[0:NK, 1, :], Bt.ap[0:NK, 1, j, :], False, True, [F4t, Bt], [p])
                tt(P, "dve", z.ap[0:Kh, j, :], p.ap[0:Kh, :], g.ap[0:Kh, j, :], ALU.mult, [p, g], [z])
            P.dma("act", dv[:, n2c * 8:(n2c + 1) * 8, dc0:dc0 + 512], z.ap[0:Kh, :, :], reads=[z])


def phase_hyena(P, s0, L, PHYs, hw, ZH, scr, cL):
    N1 = 2 * L // 128
    Kin = N1 // 2
    hy_filter_hidden(P, L, cL["zext"], hw["w1"], hw["f1c"], hw["b1c"], hw["w2d"], hw["f2c"], hw["b2c"], scr["H2D"])
    fft_s1(P, N1, N1, [scr["A"], scr["A2"]], cL["F1"], src_filter(P, L, scr["H2D"], hw["w3s"], cL["tau"], cL["negd"], hw["hyb"]))
    fft_s2(P, N1, scr["A"], cL["G"], scr["KF"][0], "filter")
    fft_s2(P, N1, scr["A2"], cL["G"], scr["KF"][1], "filter")
    fft_s1(P, N1, Kin, [scr["A"]], cL["F1"], src_from_dram(P, scr["UC"], L, 1024))
    fft_s2(P, N1, scr["A"], cL["G"], scr["KF"][0], "conv", cL["Gi"], scr["B"])
    fft_s4(P, N1, L, scr["B"], cL["F4"], scr["UC"], 0, scr["U1"], 0)
    fft_s1(P, N1, Kin, [scr["A"]], cL["F1"], src_from_dram(P, scr["U1"], L, 0))
    fft_s2(P, N1, scr["A"], cL["G"], scr["KF"][1], "conv", cL["Gi"], scr["B"])
    fft_s4(P, N1, L, scr["B"], cL["F4"], scr["UC"], 512, ZH[s0:s0 + L, :], 0)


_CONST_CACHE = {}


def make_consts(LP, LS):
    key = (LP, LS)
    if key in _CONST_CACHE:
        return _CONST_CACHE[key]
    c = {}
    c["ident"] = np.eye(128, dtype=np.float32)
    Lm = max(LP, LS)
    inv = (np.float32(1.0) / (np.float32(10000.0) ** (np.arange(0, 32, 2, dtype=np.float32) / np.float32(32)))).astype(np.float32)
    ang = (np.arange(Lm, dtype=np.float32)[:, None] * inv[None, :]).astype(np.float32)
    cos = np.cos(ang).astype(np.float32).T
    sin = np.sin(ang).astype(np.float32).T
    cos32 = np.concatenate([cos, cos], 0)
    sin32 = np.concatenate([-sin, sin], 0)
    sc = np.float32(96.0 ** -0.5)
    c["rope"] = np.ascontiguousarray(np.stack([cos32 * sc, sin32 * sc, cos32, sin32], 0).astype(np.float32))
    Lt = 384
    M = np.zeros((4, 5, 128, 128), np.float64)
    for g, w in enumerate((2, 4, 8, 16)):
        Mf = np.zeros((Lt, Lt))
        for t in range(Lt):
            lo = min(max(t - w // 2, 0), Lt)
            hi = min(max(t + w // 2, 0), Lt)
            Mf[t, lo:hi] = 1.0 / (hi - lo)
            Mf[t, t] -= 1.0
        M[g, 0] = Mf[128:256, 0:128].T
        M[g, 1] = Mf[128:256, 128:256].T
        M[g, 2] = Mf[128:256, 256:384].T
        M[g, 3] = Mf[0:128, 0:128].T
        M[g, 4] = Mf[256:384, 256:384].T
    c["poolM"] = np.ascontiguousarray(M.transpose(2, 0, 1, 3)).astype(NPBF)
    for L in sorted(set((LP, LS))):
        N = 2 * L
        N1 = N // 128
        n1 = np.arange(N1)
        th = 2 * np.pi * ((n1[:, None] * n1[None, :]) % N1) / N1
        c["F1_%d" % L] = np.stack([np.cos(th), -np.sin(th)], 1).astype(NPBF)
        NK = N1 // 2 + 1
        thi = th[:NK, :N1 // 2]
        wg = np.full((NK, 1), 2.0)
        wg[0] = 1.0
        wg[N1 // 2] = 1.0
        c["F4_%d" % L] = np.stack([wg * np.cos(thi) / N, -wg * np.sin(thi) / N], 1).astype(NPBF)
        n2 = np.arange(128)[:, None, None]
        k1 = np.arange(N1)[None, :, None]
        k2 = np.arange(128)[None, None, :]
        tg = 2 * np.pi * ((n2 * (k1 + N1 * k2)) % N) / N
        cg, sg = np.cos(tg), np.sin(tg)
        c["G_%d" % L] = np.stack([cg, -sg, sg], 2).astype(NPBF)
        cgt, sgt = cg.transpose(2, 1, 0), sg.transpose(2, 1, 0)
        c["Gi_%d" % L] = np.stack([cgt, sgt, -sgt], 2).astype(NPBF)
        t = np.linspace(0.0, 1.0, L)
        wpos = (2.0 * np.pi / L) * np.arange(L)
        bands = np.linspace(1e-4, 15.0, 16)
        z = np.concatenate([t[:, None], np.cos(bands[None, :] * wpos[:, None]), -np.sin(bands[None, :] * wpos[:, None])], -1)
        idx = np.arange(N)
        src = np.where(idx < L, idx, (N - idx) % L)
        c["zext_%d" % L] = np.ascontiguousarray(z[src].T).astype(np.float32)
        tau = np.where(idx < L, idx, N - idx).astype(np.float64)
        tau[L] = 1e9
        taut = np.zeros((128, 128), np.float32)
        taut[:N1] = tau.reshape(N1, 128)
        c["tau_%d" % L] = taut
        deltas = np.abs(np.linspace(math.log(1e-2) / 1.5, math.log(1e-2) / 0.3, 512))
        c["negd_%d" % L] = np.ascontiguousarray(np.broadcast_to((-deltas / (L - 1)).astype(np.float32), (128, 512)))
    _CONST_CACHE[key] = c
    return c


def layout_weights(w):
    f = np.float32
    o = {}

    def colT(v, k):
        v = np.asarray(v, f)
        return np.ascontiguousarray(v.reshape(v.shape[0], k, 128).transpose(0, 2, 1))

    def rep(v):
        v = np.asarray(v, f)
        return np.ascontiguousarray(np.broadcast_to(v[..., None, :], v.shape[:-1] + (128, v.shape[-1])))

    o["mix_norm_T"] = colT(w["mix_norm"], 8)
    o["mlp_norm_T"] = colT(w["mlp_norm"], 8)
    o["q_norm_T"] = colT(w["q_norm"], 3)
    o["kv_norm_T"] = colT(w["kv_norm"], 2)
    o["final_b"] = rep(np.asarray(w["final_norm"], f))
    o["pscale_b"] = rep(w["pool_scale"])
    for k in ("w_in", "w_uq", "w_ukv", "w_out", "pool_w", "mlp_w1", "mlp_w2", "hf_w1"):
        o[k] = np.ascontiguousarray(np.asarray(w[k], f))
    cw = np.asarray(w["hy_conv_w"], f)
    o["cwb"] = np.ascontiguousarray(np.broadcast_to(cw[:, None], (cw.shape[0], 128, 3, HYC)))
    o["cbb"] = rep(w["hy_conv_b"])
    o["f1c"] = np.ascontiguousarray(np.asarray(w["hf_freq"], f)[:, :, None])
    o["b1c"] = np.ascontiguousarray(np.asarray(w["hf_b1"], f)[:, :, None])
    o["f2c"] = np.ascontiguousarray(np.concatenate([np.asarray(w["hf_freq"], f)] * 2, 1)[:, :, None])
    o["b2c"] = np.ascontiguousarray(np.concatenate([np.asarray(w["hf_b2"], f)] * 2, 1)[:, :, None])
    o["w2d"] = np.ascontiguousarray(np.concatenate([np.asarray(w["hf_w2"], f)] * 2, 2))
    w3 = np.asarray(w["hf_w3"], f).reshape(-1, 64, 2, 2, 512)
    o["w3s"] = np.ascontiguousarray(w3.transpose(0, 2, 1, 3, 4).reshape(-1, 128, 2, 512))
    o["hyb"] = np.ascontiguousarray(np.asarray(w["hy_bias"], f))
    return o


WSHAPES = {
    "mix_norm_T": [4, 128, 8], "mlp_norm_T": [4, 128, 8], "q_norm_T": [2, 128, 3], "kv_norm_T": [2, 128, 2],
    "final_b": [128, D], "pscale_b": [2, 128, D], "w_in": [2, D, INC], "w_uq": [2, QL, 768], "w_ukv": [2, KVL, 1024],
    "w_out": [2, D, D], "pool_w": [2, 4, 256, 256], "mlp_w1": [4, D, DFF], "mlp_w2": [4, DFF, D], "hf_w1": [2, 33, 64],
    "cwb": [2, 128, 3, HYC], "cbb": [2, 128, HYC], "f1c": [2, 64, 1], "b1c": [2, 64, 1], "f2c": [2, 128, 1],
    "b2c": [2, 128, 1], "w2d": [2, 64, 128], "w3s": [2, 128, 2, 512], "hyb": [2, 2, 512],
}


def build(LP, LS, layers=(0, 1, 2, 3), final=True, consts_np=None):
    nc = bass.Bass("TRN2", target_bir_lowering=False)
    T = LP + LS
    seqs = [(0, LP), (LP, LS)]

    def din(name, shape, dt=F32):
        return nc.dram_tensor(name, list(shape), dt, kind="ExternalInput").ap()

    def dscr(name, shape, dt):
        return nc.dram_tensor(name, list(shape), dt, kind="Internal").ap()

    xP = din("xP", [LP, D])
    xS = din("xS", [LS, D])
    W = {k: din(k, v) for k, v in WSHAPES.items()}
    C = {}
    for k, v in consts_np.items():
        C[k] = din("c_" + k, v.shape, BF16 if v.dtype == NPBF else F32)
    yP = nc.dram_tensor("yP", [LP, D], F32, kind="ExternalOutput").ap()
    yS = nc.dram_tensor("yS", [LS, D], F32, kind="ExternalOutput").ap()
    X = dscr("X", [T, D], F32)
    PHY = [dscr("PHY%d" % i, [L + 2, HYC], F32) for i, (s0, L) in enumerate(seqs)]
    QT = dscr("QT", [NH, 96, T], BF16)
    KT = dscr("KT", [NH, 96, T], BF16)
    KR = dscr("KR", [32, T], BF16)
    V = dscr("V", [T, NH * 65], BF16)
    ZH = dscr("ZH", [T, 512], BF16)
    OT = dscr("OT", [512, T], BF16)
    RCD = dscr("RCD", [8, 512], F32)
    hscr = []
    for i, (s0, L) in enumerate(seqs):
        N1 = 2 * L // 128
        hscr.append({
            "A": dscr("hA%d" % i, [2, 128, N1 // 2 + 1, 512], BF16), "A2": dscr("hA2%d" % i, [2, 128, N1 // 2 + 1, 512], BF16),
            "B": dscr("hB%d" % i, [2, N1 // 2 + 1, 128, 512], BF16),
            "KF": [dscr("hKF%d_%d" % (i, f), [2, 128, N1 // 2 + 1, 512], BF16) for f in range(2)],
            "UC": dscr("hUC%d" % i, [L, HYC], BF16), "U1": dscr("hU1%d" % i, [L, 512], BF16),
            "H2D": dscr("hH2%d" % i, [128, 2 * L], BF16)})
    P = Prog(nc)
    with Phase(P) as ph:
        RB = 1024
        for (s0, L), src in zip(seqs, (xP, xS)):
            for r in range(0, L, RB):
                n = min(RB, L - r)
                P.dma("sp", X[s0 + r:s0 + r + n, :], src[r:r + n, :])
    outs = [(0, LP, yP), (LP, LS, yS)]
    for li, i in enumerate(layers):
        last = (li == len(layers) - 1)
        if i % 2 == 0:
            e = i // 2
            phase_proj(P, X, seqs, W["w_in"][e], W["w_uq"][e], W["w_ukv"][e], W["mix_norm_T"][i], W["q_norm_T"][e],
                       W["kv_norm_T"][e], PHY, QT, KT, KR, V, C)
            hw = {"cwb": W["cwb"][e], "cbb": W["cbb"][e], "w1": W["hf_w1"][e], "f1c": W["f1c"][e], "b1c": W["b1c"][e],
                  "w2d": W["w2d"][e], "f2c": W["f2c"][e], "b2c": W["b2c"][e], "w3s": W["w3s"][e], "hyb": W["hyb"][e]}
            jobs = [(PHY[si], L, hscr[si]["UC"]) for si, (s0, L) in enumerate(seqs)]
            phase_attn(P, seqs, QT, KT, KR, V, OT, RCD, side=lambda ph: shortconv_gen(P, ph, jobs, hw["cwb"], hw["cbb"]))
            for si, (s0, L) in enumerate(seqs):
                cL = {k: C["%s_%d" % (k, L)] for k in ("F1", "F4", "G", "Gi", "zext", "tau", "negd")}
                phase_hyena(P, s0, L, PHY[si], hw, ZH, hscr[si], cL)
            phase_wout(P, X, T, ZH, OT, W["w_out"][e], C)
        else:
            o = i // 2
            phase_pool(P, X, seqs, W["pool_w"][o], W["mix_norm_T"][i], W["pscale_b"][o], C)
        fin = (W["final_b"], outs) if (last and final) else None
        phase_mlp(P, X, T, W["mlp_w1"][i], W["mlp_w2"][i], W["mlp_norm_T"][i], C, final=fin)
    if not final:
        with Phase(P) as ph:
            for (r0, nr, oap) in outs:
                for r in range(0, nr, 1024):
                    n = min(1024, nr - r)
                    P.dma("sp", oap[r:r + n, :], X[r0 + r:r0 + r + n, :])
    P.flush(final=True)
    return nc


_NC_CACHE = {}


def kernel(**inputs):
    LP, LS = 8192, 4096
    consts = make_consts(LP, LS)
    key = (LP, LS)
    if key not in _NC_CACHE:
        _NC_CACHE[key] = build(LP, LS, consts_np=consts)
    nc = _NC_CACHE[key]
    wl = layout_weights(inputs)
    xp = np.asarray(inputs["x_prompt"], np.float32)
    xs = np.asarray(inputs["x_sample"], np.float32)
    base = dict(wl)
    for k, v in consts.items():
        base["c_" + k] = v
    in_maps = []
    for c in range(8):
        m = dict(base)
        m["xP"] = np.ascontiguousarray(xp[c])
        m["xS"] = np.ascontiguousarray(xs[c // 2])
        in_maps.append(m)
    res = run_bass_kernel_spmd(nc, in_maps, core_ids=list(range(8)))
    yp = np.stack([np.asarray(res.results[c]["yP"], np.float32) for c in range(8)], 0)
    hl = LS // 2
    ys = np.stack([np.concatenate([np.asarray(res.results[2 * j]["yS"], np.float32)[:hl],
                                   np.asarray(res.results[2 * j + 1]["yS"], np.float32)[hl:]], 0) for j in range(4)], 0)
    return (yp, ys)
```

```python
import math
from contextlib import ExitStack

import numpy as np
import ml_dtypes
import concourse.bass as bass
import concourse.mybir as mybir
from concourse.bass_utils import run_bass_kernel_spmd

F32 = mybir.dt.float32
BF16 = mybir.dt.bfloat16
I32 = mybir.dt.int32
AF = mybir.ActivationFunctionType
ALU = mybir.AluOpType
AX = mybir.AxisListType
NPBF = ml_dtypes.bfloat16

D = 1024
DFF = 4096
DHY = 512
HYC = 1536
QL = 384
KVL = 256
INC = 2208
NH = 8
EPS = 1e-6
ENGS = ("sp", "act", "dve", "pool", "pe")
TWO_PI = 2.0 * math.pi


class Sem:
    def __init__(self, h):
        self.h = h
        self.n = 0


class Buf:
    def __init__(self, ap=None):
        self.ap = ap
        self.w = {}
        self.r = {}


def _merge(dst, src):
    for s, v in src.items():
        if dst.get(s, 0) < v:
            dst[s] = v


class Prog:
    def __init__(self, nc, n_dma_sems=16):
        self.nc = nc
        self.stack = ExitStack()
        self.esem = {e: Sem(self.stack.enter_context(nc.semaphore("es_" + e))) for e in ("act", "dve", "pool", "pe")}
        self.dsem = {e: [Sem(self.stack.enter_context(nc.semaphore("ds_%s%d" % (e, i)))) for i in range(n_dma_sems)]
                     for e in ("sp", "act", "pool")}
        self.dcnt = {e: 0 for e in self.dsem}
        self.q = {e: [] for e in ENGS}
        self.pending = {e: {} for e in ENGS}
        self.pend_reads = {e: [] for e in ENGS}
        self.seen = {e: {} for e in ENGS}
        self.uid = 0

    def name(self, p):
        self.uid += 1
        return "%s_%d" % (p, self.uid)

    def all_sems(self):
        out = list(self.esem.values())
        for l in self.dsem.values():
            out.extend(l)
        return out

    def op(self, eng, fn, reads=(), writes=(), dma=False, inc=True):
        waits = dict(self.pending[eng])
        self.pending[eng] = {}
        for b in reads:
            _merge(waits, b.w)
        for b in writes:
            _merge(waits, b.w)
            _merge(waits, b.r)
        tok = None
        incinfo = None
        if dma:
            sems = self.dsem[eng]
            s = sems[self.dcnt[eng] % len(sems)]
            self.dcnt[eng] += 1
            if s.n > 0 and waits.get(s, 0) < s.n:
                waits[s] = s.n
            s.n += 16
            tok = {s: s.n}
            incinfo = (s, 16)
        elif inc:
            s = self.esem[eng]
            s.n += 1
            tok = {s: s.n}
            incinfo = (s, 1)
        self.q[eng].append((fn, waits, incinfo))
        if tok is None:
            self.pend_reads[eng].extend(reads)
            self.pend_reads[eng].extend(writes)
        else:
            for b in list(reads) + self.pend_reads[eng]:
                _merge(b.r, tok)
            self.pend_reads[eng] = []
            for b in writes:
                b.w = dict(tok)
                b.r = {}
        return tok

    def dma(self, eng, out_ap, in_ap, reads=(), writes=()):
        return self.op(eng, lambda e: e.dma_start(out=out_ap, in_=in_ap), reads=reads, writes=writes, dma=True)

    def barrier(self):
        allv = {s: s.n for s in self.all_sems() if s.n > 0}
        for e in ENGS:
            _merge(self.pending[e], allv)

    def flush(self, final=False):
        nc = self.nc
        q = self.q
        if final:
            allv = {s: s.n for s in self.all_sems() if s.n > 0}
        with nc.Block() as block:
            def run(eng, e):
                seen = self.seen[eng]
                own = self.esem.get(eng)
                for fn, waits, incinfo in q[eng]:
                    for s, v in waits.items():
                        if eng == "pe" and s is own:
                            continue
                        if seen.get(s, 0) >= v:
                            continue
                        e.wait_ge(s.h, v)
                        seen[s] = v
                    ins = fn(e)
                    if incinfo is not None:
                        ins.then_inc(incinfo[0].h, incinfo[1])
                if final and eng == "sp":
                    for s, v in allv.items():
                        if seen.get(s, 0) < v:
                            e.wait_ge(s.h, v)
                            seen[s] = v

            @block.sync
            def _(e):
                run("sp", e)

            @block.scalar
            def _(e):
                run("act", e)

            @block.vector
            def _(e):
                run("dve", e)

            @block.gpsimd
            def _(e):
                run("pool", e)

            @block.tensor
            def _(e):
                run("pe", e)
        self.q = {e: [] for e in ENGS}


class Phase:
    def __init__(self, P):
        self.P = P
        self.nc = P.nc
        self.es = ExitStack()

    def __enter__(self):
        self.es.__enter__()
        return self

    def __exit__(self, *a):
        if a[0] is None:
            self.P.barrier()
            self.P.flush()
        return self.es.__exit__(*a)

    def sb(self, shape, dt, name="t"):
        t = self.es.enter_context(self.nc.sbuf_tensor(self.P.name(name), list(shape), dt))
        return t

    def ps(self, shape, dt=F32, name="p"):
        t = self.es.enter_context(self.nc.psum_tensor(self.P.name(name), list(shape), dt))
        return t

    def sbufs(self, n, shape, dt, name="r"):
        return [Buf(self.sb(shape, dt, name)) for _ in range(n)]

    def psbufs(self, n, shape, dt=F32, name="pr"):
        return [Buf(self.ps(shape, dt, name)) for _ in range(n)]


class Ring:
    def __init__(self, bufs):
        self.bufs = bufs
        self.i = -1

    def next(self):
        self.i += 1
        return self.bufs[self.i % len(self.bufs)]


def load_cast(P, ph, dst_ap, src_ap, shape, stg_ring, dstbuf, eng_cycle, scale_ap=None, scale_buf=None):
    stg = stg_ring.next()
    sap = stg.ap
    idx = tuple([slice(0, shape[0])] + [slice(0, s) for s in shape[1:]])
    view = sap[idx]
    P.dma("sp", view, src_ap, writes=[stg])
    eng = eng_cycle[0]
    eng_cycle.append(eng_cycle.pop(0))
    rd = [stg] + ([scale_buf] if scale_buf is not None else [])
    if eng == "act":
        if scale_ap is None:
            P.op("act", lambda e: e.activation(out=dst_ap, in_=view, func=AF.Copy), reads=rd, writes=[dstbuf])
        else:
            P.op("act", lambda e: e.activation(out=dst_ap, in_=view, func=AF.Copy, scale=scale_ap), reads=rd,
                 writes=[dstbuf])
    else:
        if scale_ap is None:
            P.op(eng, lambda e: e.tensor_copy(out=dst_ap, in_=view), reads=rd, writes=[dstbuf])
        else:
            P.op(eng, lambda e: e.tensor_scalar(out=dst_ap, in0=view, scalar1=scale_ap, scalar2=None, op0=ALU.mult),
                 reads=rd, writes=[dstbuf])


class StatRing:
    def __init__(self, ph, n=8):
        self.ss = ph.sb([128, n], F32, "ss")
        self.rs = ph.sb([128, n], F32, "rs")
        self.sb_ = [Buf(self.ss[:, i:i + 1]) for i in range(n)]
        self.rb_ = [Buf(self.rs[:, i:i + 1]) for i in range(n)]
        self.i = -1
        self.n = n

    def next(self):
        self.i += 1
        k = self.i % self.n
        return self.sb_[k], self.rb_[k]


def rms_norm_to(P, out_ap, outbuf, x_ap, xbuf, n, junk, stat, eps_b, post=None):
    ssb, rsb = stat.next()
    npart = x_ap.shape[0]
    ss = ssb.ap[0:npart, :]
    rs = rsb.ap[0:npart, :]
    P.op("act", lambda e: e.activation(out=junk.ap[0:npart, 0:n], in_=x_ap, func=AF.Square, accum_out=ss),
         reads=[xbuf], writes=[junk, ssb])
    P.op("act", lambda e: e.activation(out=rs, in_=ss, func=AF.Sqrt, scale=1.0 / n, bias=eps_b.ap[0:npart, :]),
         reads=[ssb, eps_b], writes=[rsb])
    P.op("dve", lambda e: e.reciprocal(out=rs, in_=rs), reads=[rsb], writes=[rsb])
    if post is not None:
        pap, pbuf = post
        P.op("dve", lambda e: e.scalar_tensor_tensor(out=out_ap, in0=x_ap, scalar=rs, in1=pap, op0=ALU.mult, op1=ALU.mult),
             reads=[xbuf, rsb, pbuf], writes=[outbuf])
        return
    P.op("dve", lambda e: e.tensor_scalar(out=out_ap, in0=x_ap, scalar1=rs, scalar2=None, op0=ALU.mult),
         reads=[xbuf, rsb], writes=[outbuf])


def make_eps(P, ph):
    eb = Buf(ph.sb([128, 1], F32, "eps"))
    P.op("pool", lambda e: e.memset(eb.ap[:], EPS), writes=[eb])
    return eb


def load_ident(P, ph, consts):
    idf = Buf(ph.sb([128, 128], F32, "idf"))
    idb = Buf(ph.sb([128, 128], BF16, "idb"))
    P.dma("sp", idf.ap[:], consts["ident"][:, :], writes=[idf])
    P.op("dve", lambda e: e.tensor_copy(out=idb.ap[:], in_=idf.ap[:]), reads=[idf], writes=[idb])
    return idf, idb


def mm(P, out_ap, lhsT_ap, rhs_ap, start, stop, reads, writes, inc=True):
    return P.op("pe", lambda e: e.matmul(out_ap, lhsT_ap, rhs_ap, start=start, stop=stop), reads=reads,
                writes=writes, inc=inc)


def tr(P, out_ap, in_ap, ident_ap, reads, writes, inc=True):
    return P.op("pe", lambda e: e.transpose(out_ap, in_ap, ident_ap), reads=reads, writes=writes, inc=inc)


def act(P, out_ap, in_ap, func, reads, writes, scale=None, bias=None):
    kw = {}
    if scale is not None:
        kw["scale"] = scale
    if bias is not None:
        kw["bias"] = bias
    return P.op("act", lambda e: e.activation(out=out_ap, in_=in_ap, func=func, **kw), reads=reads, writes=writes)


def cp(P, eng, out_ap, in_ap, reads, writes):
    if eng == "act":
        return act(P, out_ap, in_ap, AF.Copy, reads, writes)
    return P.op(eng, lambda e: e.tensor_copy(out=out_ap, in_=in_ap), reads=reads, writes=writes)


def tt(P, eng, out_ap, in0, in1, op, reads, writes):
    return P.op(eng, lambda e: e.tensor_tensor(out=out_ap, in0=in0, in1=in1, op=op), reads=reads, writes=writes)


def ts(P, eng, out_ap, in0, s1, s2, op0, op1, reads, writes):
    if s2 is None:
        return P.op(eng, lambda e: e.tensor_scalar(out=out_ap, in0=in0, scalar1=s1, scalar2=None, op0=op0),
                    reads=reads, writes=writes)
    return P.op(eng, lambda e: e.tensor_scalar(out=out_ap, in0=in0, scalar1=s1, scalar2=s2, op0=op0, op1=op1),
                reads=reads, writes=writes)


def stt(P, eng, out_ap, in0, scalar, in1, op0, op1, reads, writes):
    return P.op(eng, lambda e: e.scalar_tensor_tensor(out=out_ap, in0=in0, scalar=scalar, in1=in1, op0=op0, op1=op1),
                reads=reads, writes=writes)


def phase_mlp(P, X, T, w1, w2, gT, consts, final=None):
    TT = 256
    NS = TT // 128
    with Phase(P) as ph:
        W1s = ph.sb([128, 8, DFF], BF16, "w1s")
        W2s = ph.sb([128, 32, D], BF16, "w2s")
        W1b = [Buf() for _ in range(8)]
        W2b = [Buf() for _ in range(16)]
        gtb = Buf(ph.sb([128, 8], F32, "gT"))
        P.dma("sp", gtb.ap[:], gT, writes=[gtb])
        with Phase(P) as ph2:
            stg = Ring(ph2.sbufs(3, [128, 2048], F32, "stg"))
            cyc = ["act", "dve"]
            for kc in range(8):
                for hf in range(2):
                    load_cast(P, ph2, W1s[:, kc, hf * 2048:(hf + 1) * 2048], w1[kc * 128:(kc + 1) * 128, hf * 2048:(hf + 1) * 2048],
                              [128, 2048], stg, W1b[kc], cyc, scale_ap=gtb.ap[:, kc:kc + 1], scale_buf=gtb)
            w2v = w2.rearrange("(fc p) d -> p fc d", p=128)
            for f2 in range(16):
                st = stg.next()
                v = st.ap[:].rearrange("p (a d) -> p a d", a=2)
                P.dma("sp", v, w2v[:, 2 * f2:2 * f2 + 2, :], writes=[st])
                eng = cyc[0]
                cyc.append(cyc.pop(0))
                cp(P, eng, W2s[:, 2 * f2:2 * f2 + 2, :], v, [st], [W2b[f2]])
        idf, idb = load_ident(P, ph, consts)
        epsb = make_eps(P, ph)
        xt_r = Ring(ph.sbufs(3 * NS, [128, D], F32, "xt"))
        hb_r = Ring(ph.sbufs(2, [128, D], BF16, "hb"))
        junk = Buf(ph.sb([128, D], F32, "junk"))
        stat = StatRing(ph)
        hT_r = Ring(ph.sbufs(2, [128, 8, TT], BF16, "hT"))
        aT = ph.sb([128, 32, TT], BF16, "aT")
        aTb = [Buf() for _ in range(16)]
        tmp_r = Ring(ph.sbufs(3, [128, 2, TT], BF16, "rl"))
        xo_r = Ring(ph.sbufs(2, [128, D], F32, "xo"))
        tp_r = Ring(ph.psbufs(2, [128, 8, 128], BF16, "tp"))
        up_r = Ring(ph.psbufs(2, [128, 2, TT], F32, "up"))
        dn_r = Ring(ph.psbufs(4, [128, 512], F32, "dn"))
        if final is not None:
            gfb = Buf(ph.sb([128, D], F32, "gfin"))
            P.dma("sp", gfb.ap[:], final[0], writes=[gfb])
            yo_r = Ring(ph.sbufs(1, [128, D], F32, "yo"))
        ntile = T // TT
        xts = {}

        def issue_load(j):
            for s in range(NS):
                xb = xt_r.next()
                t0 = j * TT + s * 128
                P.dma("sp", xb.ap[:], X[t0:t0 + 128, :], writes=[xb])
                xts[(j, s)] = xb

        hTs = {}
        hbs = {}

        def prep_norm(j):
            for s in range(NS):
                xb = xts[(j, s)]
                hb = hb_r.next()
                rms_norm_to(P, hb.ap[:], hb, xb.ap[:], xb, D, junk, stat, epsb)
                hbs[(j, s)] = hb

        def prep_tr(j):
            hT = hT_r.next()
            hTs[j] = hT
            for s in range(NS):
                hb = hbs.pop((j, s))
                tp = tp_r.next()
                for kc in range(8):
                    tr(P, tp.ap[:, kc, :], hb.ap[:, kc * 128:(kc + 1) * 128], idb.ap[:], [hb, idb], [tp], inc=(kc == 7))
                cp(P, "act" if s == 0 else "dve", hT.ap[:, :, s * 128:(s + 1) * 128], tp.ap[:], [tp], [hT])

        def prep(j):
            prep_norm(j)
            prep_tr(j)

        issue_load(0)
        if ntile > 1:
            issue_load(1)
        prep(0)
        for j in range(ntile):
            hT = hTs.pop(j)
            for mp in range(16):
                up = up_r.next()
                for mi in range(2):
                    m = 2 * mp + mi
                    for kc in range(8):
                        mm(P, up.ap[:, mi, :], W1s[:, kc, m * 128:(m + 1) * 128], hT.ap[:, kc, :], kc == 0, kc == 7,
                           [W1b[kc], hT], [up], inc=(kc == 7 and mi == 1))
                tmp = tmp_r.next()
                act(P, tmp.ap[:], up.ap[:], AF.Relu, [up], [tmp])
                tt(P, "pool", aT[:, 2 * mp:2 * mp + 2, :], tmp.ap[:], tmp.ap[:], ALU.mult, [tmp], [aTb[mp]])
            if j + 2 < ntile:
                issue_load(j + 2)
            if j + 1 < ntile:
                prep_norm(j + 1)
            for s in range(NS):
                if s == 1 and j + 1 < ntile:
                    prep_tr(j + 1)
                xb = xts.pop((j, s))
                xo = xo_r.next()
                t0 = j * TT + s * 128
                for cc in range(2):
                    dn = dn_r.next()
                    for fc in range(32):
                        mm(P, dn.ap[:], aT[:, fc, s * 128:(s + 1) * 128], W2s[:, fc, cc * 512:(cc + 1) * 512],
                           fc == 0, fc == 31, [aTb[fc // 2], W2b[fc // 2]], [dn], inc=(fc == 31))
                    tt(P, "dve", xo.ap[:, cc * 512:(cc + 1) * 512], dn.ap[:], xb.ap[:, cc * 512:(cc + 1) * 512], ALU.add,
                       [dn, xb], [xo])
                if final is None:
                    P.dma("pool", X[t0:t0 + 128, :], xo.ap[:], reads=[xo])
                else:
                    yo = yo_r.next()
                    rms_norm_to(P, yo.ap[:], yo, xo.ap[:], xo, D, junk, stat, epsb, post=(gfb.ap[:], gfb))
                    for (r0, nr, oap) in final[1]:
                        if r0 <= t0 < r0 + nr:
                            P.dma("pool", oap[t0 - r0:t0 - r0 + 128, :], yo.ap[:], reads=[yo])


def dram_ap(base, offset_elems, dims):
    return bass.AP(tensor=base.tensor, offset=base.offset + offset_elems, ap=[list(d) for d in dims])


def phase_pool(P, X, seqs, pw, gT, psb, consts):
    with Phase(P) as ph:
        Wp = ph.sb([128, 8, 256], BF16, "wp")
        Wpb = [Buf() for _ in range(8)]
        gtb = Buf(ph.sb([128, 8], F32, "gT"))
        psc = Buf(ph.sb([128, D], F32, "psc"))
        MT = Buf(ph.sb([128, 4, 5, 128], BF16, "MT"))
        P.dma("sp", gtb.ap[:], gT, writes=[gtb])
        P.dma("sp", psc.ap[:], psb, writes=[psc])
        P.dma("sp", MT.ap[:], consts["poolM"], writes=[MT])
        stg = Ring(ph.sbufs(3, [128, 256], F32, "stg"))
        for cc in range(8):
            g, hf = cc // 2, cc % 2
            st = stg.next()
            P.dma("sp", st.ap[:], pw[g, hf * 128:(hf + 1) * 128, :], writes=[st])
            stt(P, "dve", Wp[:, cc, :], st.ap[:], gtb.ap[:, cc:cc + 1], psc.ap[:, g * 256:(g + 1) * 256], ALU.mult, ALU.mult,
                [st, gtb, psc], [Wpb[cc]])
        epsb = make_eps(P, ph)
        xt_r = Ring(ph.sbufs(5, [128, D], F32, "xt"))
        xr_r = Ring(ph.sbufs(5, [128, D], BF16, "xr"))
        junk = Buf(ph.sb([128, D], F32, "junk"))
        stat = StatRing(ph)
        dTs_t = [ph.sb([128, 8, 128], BF16, "dTs") for _ in range(2)]
        dTs_b = [[Buf(), Buf()] for _ in range(2)]
        xo_r = Ring(ph.sbufs(2, [128, D], F32, "xo"))
        dT_r = Ring(ph.psbufs(4, [128, 4, 128], F32, "dT"))
        po_r = Ring(ph.psbufs(4, [128, 512], F32, "po"))
        it = 0
        for (s0, L) in seqs:
            nt = L // 128
            xts, xrs = {}, {}
            def pnorm(i):
                xb = xt_r.next()
                P.dma("sp", xb.ap[:], X[s0 + i * 128:s0 + (i + 1) * 128, :], writes=[xb])
                xr = xr_r.next()
                rms_norm_to(P, xr.ap[:], xr, xb.ap[:], xb, D, junk, stat, epsb)
                xts[i], xrs[i] = xb, xr

            pnorm(0)
            for i in range(nt + 1):
                if i + 1 < nt:
                    pnorm(i + 1)
                if i >= 1:
                    o = i - 1
                    srcs = []
                    if o >= 1:
                        srcs.append((xrs[o - 1], 0))
                    srcs.append((xrs[o], 3 if o == 0 else (4 if o == nt - 1 else 1)))
                    if o + 1 < nt:
                        srcs.append((xrs[o + 1], 2))
                    k = it % 2
                    it += 1
                    dTs = dTs_t[k]
                    for b in range(2):
                        dT = dT_r.next()
                        for ci in range(4):
                            cc = b * 4 + ci
                            g = cc // 2
                            for si, (xr, var) in enumerate(srcs):
                                mm(P, dT.ap[:, ci, :], xr.ap[:, cc * 128:(cc + 1) * 128], MT.ap[:, g, var, :], si == 0,
                                   si == len(srcs) - 1, [xr, MT], [dT], inc=(ci == 3 and si == len(srcs) - 1))
                        cp(P, "act" if b == 0 else "dve", dTs[:, b * 4:(b + 1) * 4, :], dT.ap[:], [dT], [dTs_b[k][b]])
                    xo = xo_r.next()
                    xb = xts.pop(o)
                    for pb in range(2):
                        po = po_r.next()
                        for gi in range(2):
                            g = pb * 2 + gi
                            for hf in range(2):
                                cc = g * 2 + hf
                                mm(P, po.ap[:, gi * 256:(gi + 1) * 256], dTs[:, cc, :], Wp[:, cc, :], hf == 0, hf == 1,
                                   [dTs_b[k][cc // 4], Wpb[cc]], [po], inc=(gi == 1 and hf == 1))
                        tt(P, "dve", xo.ap[:, pb * 512:(pb + 1) * 512], po.ap[:], xb.ap[:, pb * 512:(pb + 1) * 512], ALU.add,
                           [po, xb], [xo])
                    P.dma("pool", X[s0 + o * 128:s0 + (o + 1) * 128, :], xo.ap[:], reads=[xo])
                    if o >= 1:
                        xrs.pop(o - 1)


def phase_wout(P, X, T, ZH, OT, wout, consts):
    TT = 512
    with Phase(P) as ph:
        Wo = ph.sb([128, 8, D], BF16, "wo")
        Wob = [Buf() for _ in range(8)]
        with Phase(P) as ph2:
            stg = Ring(ph2.sbufs(3, [128, D], F32, "stg"))
            cyc = ["act", "dve"]
            for kc in range(8):
                load_cast(P, ph2, Wo[:, kc, :], wout[kc * 128:(kc + 1) * 128, :], [128, D], stg, Wob[kc], cyc)
        idf, idb = load_ident(P, ph, consts)
        mix_r = Ring(ph.sbufs(2, [128, 8, TT], BF16, "mixT"))
        zh_r = Ring(ph.sbufs(3, [128, 512], BF16, "zh"))
        xt_r = Ring(ph.sbufs(3, [128, D], F32, "xt"))
        xo_r = Ring(ph.sbufs(2, [128, D], F32, "xo"))
        tp_r = Ring(ph.psbufs(2, [128, 4, 128], BF16, "tp"))
        dn_r = Ring(ph.psbufs(4, [128, 512], F32, "dn"))
        OTv = OT.rearrange("(kc p) t -> p kc t", p=128)
        for j in range(T // TT):
            t0 = j * TT
            mix = mix_r.next()
            P.dma("sp", mix.ap[:, 4:8, :], OTv[:, :, t0:t0 + TT], writes=[mix])
            for s in range(4):
                zh = zh_r.next()
                P.dma("sp", zh.ap[:], ZH[t0 + s * 128:t0 + (s + 1) * 128, :], writes=[zh])
                tp = tp_r.next()
                for kc in range(4):
                    tr(P, tp.ap[:, kc, :], zh.ap[:, kc * 128:(kc + 1) * 128], idb.ap[:], [zh, idb], [tp], inc=(kc == 3))
                cp(P, "act", mix.ap[:, 0:4, s * 128:(s + 1) * 128], tp.ap[:], [tp], [mix])
            for s in range(4):
                r0 = t0 + s * 128
                xb = xt_r.next()
                P.dma("sp", xb.ap[:], X[r0:r0 + 128, :], writes=[xb])
                xo = xo_r.next()
                for cc in range(2):
                    dn = dn_r.next()
                    for kc in range(8):
                        mm(P, dn.ap[:], mix.ap[:, kc, s * 128:(s + 1) * 128], Wo[:, kc, cc * 512:(cc + 1) * 512], kc == 0,
                           kc == 7, [mix, Wob[kc]], [dn], inc=(kc == 7))
                    tt(P, "dve", xo.ap[:, cc * 512:(cc + 1) * 512], dn.ap[:], xb.ap[:, cc * 512:(cc + 1) * 512], ALU.add,
                       [dn, xb], [xo])
                P.dma("pool", X[r0:r0 + 128, :], xo.ap[:], reads=[xo])


def phase_attn(P, seqs, QT, KT, KR, V, OT, RCD, side=None):
    rcd_r = Ring([Buf(RCD[i:i + 1, :]) for i in range(RCD.shape[0])])
    for qi_, (s0, L) in enumerate(seqs):
        nkb = L // 128
        npair = nkb // 2
        with Phase(P) as ph:
            gen = side(ph) if (side is not None and qi_ == 0) else None
            VCH = min(8, nkb)
            Vall = ph.sb([128, nkb, NH * 65], BF16, "vall")
            Vb = [Buf() for _ in range((nkb + VCH - 1) // VCH)]
            Vv = V[s0:s0 + L, :].rearrange("(kb p) c -> p kb c", p=128)
            for i in range(len(Vb)):
                P.dma("sp", Vall[:, i * VCH:(i + 1) * VCH, :], Vv[:, i * VCH:(i + 1) * VCH, :], writes=[Vb[i]])
            kt_r = Ring(ph.sbufs(2, [96, L], BF16, "kth"))
            q_r = Ring(ph.sbufs(3, [96, 512], BF16, "qt"))
            pt_r = Ring(ph.sbufs(3, [128, 2, 512], BF16, "pt"))
            rc_r = Ring(ph.sbufs(2, [128, 512], F32, "rc"))
            bcs_r = Ring(ph.sbufs(2, [64, 512], F32, "bcs"))
            o_r = Ring(ph.sbufs(2, [64, 512], BF16, "oT"))
            ones = Buf(ph.sb([128, 64], F32, "ones"))
            P.op("dve", lambda e: e.memset(ones.ap[:], 1.0), writes=[ones])
            st_r = Ring(ph.psbufs(3, [128, 2, 512], F32, "st"))
            ot_r = Ring(ph.psbufs(2, [128, 512], F32, "ot"))
            steps = []
            tiles = []
            for h in range(NH):
                for qi in range(L // 512):
                    tiles.append((h, qi))
            state = {}

            def tile_ctx(ti):
                if ti not in state:
                    h, qi = tiles[ti]
                    if qi == 0:
                        kt = kt_r.next()
                        P.dma("sp", kt.ap[0:64, :], KT[h, 0:64, s0:s0 + L], writes=[kt])
                        P.dma("sp", kt.ap[64:96, :], KR[:, s0:s0 + L], writes=[kt])
                        state[("kt", h)] = kt
                    q = q_r.next()
                    P.dma("sp", q.ap[:], QT[h, :, s0 + qi * 512:s0 + (qi + 1) * 512], writes=[q])
                    state[ti] = (state[("kt", h)], q)
                return state[ti]

            nst = len(tiles) * npair
            sts = {}

            def emit_s(g):
                ti, i = divmod(g, npair)
                kt, q = tile_ctx(ti)
                st = st_r.next()
                for u in range(2):
                    kb = 2 * i + u
                    mm(P, st.ap[:, u, :], kt.ap[:, kb * 128:(kb + 1) * 128], q.ap[:], True, True, [kt, q], [st], inc=(u == 1))
                sts[g] = st

            AHEAD = 2
            for g in range(min(AHEAD, nst)):
                emit_s(g)
            ot = None
            for g in range(nst):
                if gen is not None and g % 40 == 20:
                    if next(gen, "done") == "done":
                        gen = None
                ti, i = divmod(g, npair)
                h, qi = tiles[ti]
                if i == 0:
                    ot = ot_r.next()
                if g + AHEAD < nst:
                    emit_s(g + AHEAD)
                st = sts.pop(g)
                pt = pt_r.next()
                act(P, pt.ap[:], st.ap[:], AF.Exp, [st], [pt])
                for u in range(2):
                    kb = 2 * i + u
                    mm(P, ot.ap[0:65, :], Vall[:, kb, h * 65:(h + 1) * 65], pt.ap[:, u, :], kb == 0, kb == nkb - 1,
                       [Vb[kb // VCH], pt], [ot], inc=(u == 1))
                if i == npair - 1:
                    rc = rc_r.next()
                    P.op("dve", (lambda rc=rc, ot=ot: lambda e: e.reciprocal(out=rc.ap[64:65, :], in_=ot.ap[64:65, :]))(),
                         reads=[ot], writes=[rc])
                    bcs = bcs_r.next()
                    rd = rcd_r.next()
                    P.dma("pool", rd.ap, rc.ap[64:65, :], reads=[rc], writes=[rd])
                    P.dma("pool", bcs.ap[:], rd.ap.partition_broadcast(64), reads=[rd], writes=[bcs])
                    oT = o_r.next()
                    tt(P, "dve", oT.ap[:], ot.ap[0:64, :], bcs.ap[:], ALU.mult, [ot, bcs], [oT])
                    P.dma("pool", OT[h * 64:(h + 1) * 64, s0 + qi * 512:s0 + (qi + 1) * 512], oT.ap[:], reads=[oT])
                    state.pop(ti, None)
            if gen is not None:
                for _ in gen:
                    pass


def phase_proj(P, X, seqs, w_in, w_uq, w_ukv, gT, qnT, kvnT, PHY, QT, KT, KR, V, consts):
    SC = 96.0 ** -0.5
    with Phase(P) as ph:
        Win = ph.sb([128, 8, 2176], BF16, "win")
        Winb = [Buf() for _ in range(8)]
        Wkr = Buf(ph.sb([128, 8, 2, 96], BF16, "wkr"))
        Wuq = Buf(ph.sb([128, 3, 2, 768], BF16, "wuq"))
        Wkn = Buf(ph.sb([128, 2, 8, 64], BF16, "wkn"))
        Wv = Buf(ph.sb([128, 2, 8, 64], BF16, "wv"))
        gtb = Buf(ph.sb([128, 8], F32, "gT"))
        qnb = Buf(ph.sb([128, 3], F32, "qn"))
        kvnb = Buf(ph.sb([128, 2], F32, "kvn"))
        P.dma("sp", gtb.ap[:], gT, writes=[gtb])
        P.dma("sp", qnb.ap[:], qnT, writes=[qnb])
        P.dma("sp", kvnb.ap[:], kvnT, writes=[kvnb])
        P.op("pool", lambda e: e.memset(Wkr.ap[:], 0.0), writes=[Wkr])
        P.op("pool", lambda e: e.memset(Wuq.ap[:, :, 1, :], 0.0), writes=[Wuq])
        with Phase(P) as ph2:
            stg = Ring(ph2.sbufs(3, [128, 1104], F32, "stg"))
            for kc in range(8):
                g = gtb.ap[:, kc:kc + 1]
                st = stg.next()
                P.dma("sp", st.ap[:], w_in[kc * 128:(kc + 1) * 128, 0:1104], writes=[st])
                ts(P, "dve", Win[:, kc, 0:1104], st.ap[:], g, None, ALU.mult, None, [st, gtb], [Winb[kc]])
                st = stg.next()
                P.dma("sp", st.ap[:], w_in[kc * 128:(kc + 1) * 128, 1104:2208], writes=[st])
                act(P, Win[:, kc, 1104:2176], st.ap[:, 0:1072], AF.Copy, [st, gtb], [Winb[kc]], scale=g)
                ts(P, "dve", Wkr.ap[:, kc, 0, 64:96], st.ap[:, 1072:1104], g, None, ALU.mult, None, [st, gtb], [Wkr])
                ts(P, "dve", Wkr.ap[:, kc, 1, 64:80], st.ap[:, 1088:1104], g, None, ALU.mult, None, [st, gtb], [Wkr])
                ts(P, "dve", Wkr.ap[:, kc, 1, 80:96], st.ap[:, 1072:1088], g, None, ALU.mult, None, [st, gtb], [Wkr])
            for kc in range(3):
                g = qnb.ap[:, kc:kc + 1]
                st = stg.next()
                P.dma("sp", st.ap[:, 0:768], w_uq[kc * 128:(kc + 1) * 128, :], writes=[st])
                sv = st.ap[:, 0:768].rearrange("p (h c) -> p h c", c=96)
                dv = Wuq.ap[:, kc, 1, :].rearrange("p (h c) -> p h c", c=96)
                act(P, Wuq.ap[:, kc, 0, :], st.ap[:, 0:768], AF.Copy, [st, qnb], [Wuq], scale=g)
                ts(P, "dve", dv[:, :, 64:80], sv[:, :, 80:96], g, None, ALU.mult, None, [st, qnb], [Wuq])
                ts(P, "dve", dv[:, :, 80:96], sv[:, :, 64:80], g, None, ALU.mult, None, [st, qnb], [Wuq])
            for kc in range(2):
                g = kvnb.ap[:, kc:kc + 1]
                st = stg.next()
                P.dma("sp", st.ap[:, 0:1024], w_ukv[kc * 128:(kc + 1) * 128, :], writes=[st])
                sv = st.ap[:, 0:1024].rearrange("p (h c) -> p h c", c=128)
                ts(P, "dve", Wkn.ap[:, kc, :, :], sv[:, :, 0:64], g, None, ALU.mult, None, [st, kvnb], [Wkn])
                ts(P, "pool", Wv.ap[:, kc, :, :], sv[:, :, 64:128], g, None, ALU.mult, None, [st, kvnb], [Wv])
        idf, idb = load_ident(P, ph, consts)
        zero = Buf(ph.sb([1, HYC], F32, "zero"))
        P.op("pool", lambda e: e.memset(zero.ap[:], 0.0), writes=[zero])
        for si, (s0, L) in enumerate(seqs):
            P.dma("sp", PHY[si][0:1, :], zero.ap[:], reads=[zero])
            P.dma("sp", PHY[si][L + 1:L + 2, :], zero.ap[:], reads=[zero])
        epsb = make_eps(P, ph)
        xt_r = Ring(ph.sbufs(5, [128, D], F32, "xt"))
        hb_r = Ring(ph.sbufs(3, [128, D], BF16, "hb"))
        junk = Buf(ph.sb([128, D], F32, "junk"))
        stat = StatRing(ph, 12)
        hT_r = Ring(ph.sbufs(3, [128, 8, 512], BF16, "hT"))
        phy_r = Ring(ph.sbufs(2, [128, HYC], F32, "phy"))
        cl_r = Ring(ph.sbufs(2, [128, 640], F32, "cl"))
        cln_r = Ring(ph.sbufs(2, [128, 640], BF16, "cln"))
        clT_r = Ring(ph.sbufs(3, [128, 5, 512], BF16, "clT"))
        vt_t = [ph.sb([128, 8, 65], BF16, "vt") for _ in range(2)]
        vt_b = [Buf(t) for t in vt_t]
        for t in vt_b:
            P.op("pool", (lambda t=t: lambda e: e.memset(t.ap[:], 1.0))(), writes=[t])
        vt_r = Ring(vt_b)
        rt_r = Ring(ph.sbufs(3, [96, 4, 512], F32, "rope"))
        tA_r = Ring(ph.sbufs(2, [96, 512], F32, "tA"))
        tB_r = Ring(ph.sbufs(2, [96, 512], F32, "tB"))
        kro_r = Ring(ph.sbufs(2, [96, 512], BF16, "krot"))
        qts_r = Ring(ph.sbufs(2, [96, 2, 512], BF16, "qts"))
        kts_r = Ring(ph.sbufs(2, [64, 2, 512], BF16, "kts"))
        tp_r = Ring(ph.psbufs(2, [128, 8, 128], BF16, "tp"))
        pj_r = Ring(ph.psbufs(2, [128, 512], F32, "pj"))
        ctp_r = Ring(ph.psbufs(1, [128, 5, 128], BF16, "ctp"))
        fm_r = Ring(ph.psbufs(3, [128, 512], F32, "fm"))
        rope = consts["rope"]
        ecyc = ["act", "dve"]

        def rope_mix(dst_ap, dstbuf, a, b, rt, ic, isn):
            tA = tA_r.next()
            tB = tB_r.next()
            tt(P, "dve", tA.ap[64:96, :], a.ap[64:96, :], rt.ap[64:96, ic, :], ALU.mult, [a, rt], [tA])
            tt(P, "dve", tB.ap[64:96, :], b.ap[64:96, :], rt.ap[64:96, isn, :], ALU.mult, [b, rt], [tB])
            tt(P, "pool", dst_ap, tA.ap[64:96, :], tB.ap[64:96, :], ALU.add, [tA, tB], [dstbuf])

        units = []
        for si, (s0, L) in enumerate(seqs):
            for j in range(L // 512):
                for s_ in range(4):
                    units.append((si, s0, L, j, s_))
        tctx = {}
        uctx = {}

        def tile_ctx(si, j):
            if (si, j) not in tctx:
                rt = rt_r.next()
                P.dma("sp", rt.ap[64:96, :, :], rope[:, :, j * 512:(j + 1) * 512].rearrange("f r t -> r f t"), writes=[rt])
                tctx[(si, j)] = (hT_r.next(), clT_r.next(), rt)
            return tctx[(si, j)]

        hbs = {}

        def prep_norm(u):
            xb = xload.pop(u)
            hb = hb_r.next()
            rms_norm_to(P, hb.ap[:], hb, xb.ap[:], xb, D, junk, stat, epsb)
            hbs[u] = hb

        def prep_tr(u):
            si, s0, L, j, s = units[u]
            hT, clT, rt = tile_ctx(si, j)
            hb = hbs.pop(u)
            tp = tp_r.next()
            for kc in range(8):
                tr(P, tp.ap[:, kc, :], hb.ap[:, kc * 128:(kc + 1) * 128], idb.ap[:], [hb, idb], [tp], inc=(kc == 7))
            cp(P, "dve", hT.ap[:, :, s * 128:(s + 1) * 128], tp.ap[:], [tp], [hT])

        xload = {}

        def load_x(u):
            si, s0, L, j, s = units[u]
            r0 = s0 + j * 512 + s * 128
            xb = xt_r.next()
            P.dma("sp", xb.ap[:], X[r0:r0 + 128, :], writes=[xb])
            xload[u] = xb

        def body(u):
            si, s0, L, j, s = units[u]
            hT, clT, rt = tile_ctx(si, j)
            p0 = j * 512
            r0 = s0 + p0 + s * 128
            phy = phy_r.next()
            cl = cl_r.next()
            for (c0, c1) in ((0, 512), (512, 1024), (1024, 1536), (1536, 2048), (2048, 2176)):
                pj = pj_r.next()
                w = c1 - c0
                for kc in range(8):
                    mm(P, pj.ap[:, 0:w], hT.ap[:, kc, s * 128:(s + 1) * 128], Win[:, kc, c0:c1], kc == 0, kc == 7,
                       [hT, Winb[kc]], [pj], inc=(kc == 7))
                eng = ecyc[0]
                ecyc.append(ecyc.pop(0))
                if c0 < 1536:
                    cp(P, eng, phy.ap[:, c0:c1], pj.ap[:, 0:w], [pj], [phy])
                else:
                    cp(P, eng, cl.ap[:, c0 - 1536:c1 - 1536], pj.ap[:, 0:w], [pj], [cl])
            P.dma("pool", PHY[si][1 + p0 + s * 128:1 + p0 + (s + 1) * 128, :], phy.ap[:], reads=[phy])
            cln = cln_r.next()
            rms_norm_to(P, cln.ap[:, 0:QL], cln, cl.ap[:, 0:QL], cl, QL, junk, stat, epsb)
            rms_norm_to(P, cln.ap[:, QL:640], cln, cl.ap[:, QL:640], cl, KVL, junk, stat, epsb)
            uctx[u] = cln

        def body_b(u):
            si, s0, L, j, s = units[u]
            hT, clT, rt = tile_ctx(si, j)
            r0 = s0 + j * 512 + s * 128
            cln = uctx.pop(u)
            ctp = ctp_r.next()
            for kc in range(5):
                tr(P, ctp.ap[:, kc, :], cln.ap[:, kc * 128:(kc + 1) * 128], idb.ap[:], [cln, idb], [ctp], inc=(kc == 4))
            cp(P, "dve", clT.ap[:, :, s * 128:(s + 1) * 128], ctp.ap[:], [ctp], [clT])
            vps = pj_r.next()
            for kc in range(2):
                mm(P, vps.ap[:], clT.ap[:, 3 + kc, s * 128:(s + 1) * 128], Wv.ap[:, kc, :, :].rearrange("p h c -> p (h c)"),
                   kc == 0, kc == 1, [clT, Wv], [vps], inc=(kc == 1))
            vt = vt_r.next()
            cp(P, "act", vt.ap[:, :, 0:64], vps.ap[:].rearrange("p (h c) -> p h c", c=64), [vps], [vt])
            P.dma("sp", V[r0:r0 + 128, :], vt.ap[:].rearrange("p h c -> p (h c)"), reads=[vt])

        def fm_part(si, s0, j, part):
            hT, clT, rt = tctx[(si, j)]
            g0 = s0 + j * 512
            if part == 0:
                kra = fm_r.next()
                for kc in range(8):
                    mm(P, kra.ap[0:96, :], Wkr.ap[:, kc, 0, :], hT.ap[:, kc, :], kc == 0, kc == 7, [Wkr, hT], [kra], inc=(kc == 7))
                krb = fm_r.next()
                for kc in range(8):
                    mm(P, krb.ap[0:96, :], Wkr.ap[:, kc, 1, :], hT.ap[:, kc, :], kc == 0, kc == 7, [Wkr, hT], [krb], inc=(kc == 7))
                kro = kro_r.next()
                rope_mix(kro.ap[64:96, :], kro, kra, krb, rt, 2, 3)
                P.dma("sp", KR[:, g0:g0 + 512], kro.ap[64:96, :], reads=[kro])
            qts = qts_r.next()
            kts = kts_r.next()
            for hi, h in enumerate((2 * part, 2 * part + 1)):
                qa = fm_r.next()
                for kc in range(3):
                    mm(P, qa.ap[0:96, :], Wuq.ap[:, kc, 0, h * 96:(h + 1) * 96], clT.ap[:, kc, :], kc == 0, kc == 2,
                       [Wuq, clT], [qa], inc=(kc == 2))
                qb = fm_r.next()
                for kc in range(3):
                    mm(P, qb.ap[0:96, :], Wuq.ap[:, kc, 1, h * 96:(h + 1) * 96], clT.ap[:, kc, :], kc == 0, kc == 2,
                       [Wuq, clT], [qb], inc=(kc == 2))
                act(P, qts.ap[0:64, hi, :], qa.ap[0:64, :], AF.Copy, [qa], [qts], scale=SC)
                rope_mix(qts.ap[64:96, hi, :], qts, qa, qb, rt, 0, 1)
                kn = fm_r.next()
                for kc in range(2):
                    mm(P, kn.ap[0:64, :], Wkn.ap[:, kc, h, :], clT.ap[:, 3 + kc, :], kc == 0, kc == 1, [Wkn, clT], [kn],
                       inc=(kc == 1))
                cp(P, "act", kts.ap[:, hi, :], kn.ap[0:64, :], [kn], [kts])
            h0 = 2 * part
            P.dma("sp", QT[h0:h0 + 2, :, g0:g0 + 512].rearrange("h r t -> r h t"), qts.ap[:], reads=[qts])
            P.dma("sp", KT[h0:h0 + 2, 0:64, g0:g0 + 512].rearrange("h r t -> r h t"), kts.ap[:], reads=[kts])
            if part == 3:
                tctx.pop((si, j))

        nu = len(units)
        for u in range(min(3, nu)):
            load_x(u)
        prep_norm(0)
        if nu > 1:
            prep_norm(1)
        prep_tr(0)
        for u in range(nu + 4):
            if u + 3 < nu:
                load_x(u + 3)
            if u + 2 < nu:
                prep_norm(u + 2)
            if u < nu:
                body(u)
            if 1 <= u <= nu:
                body_b(u - 1)
            k = u - 4
            if 0 <= k < nu:
                si, s0, L, j, s_ = units[k]
                fm_part(si, s0, j, s_)
            if u + 1 < nu:
                prep_tr(u + 1)


def shortconv_gen(P, ph, jobs, cwb, cbb):
    Rc = 4
    wt = Buf(ph.sb([128, 3, 512], F32, "cw"))
    bt = Buf(ph.sb([128, 512], F32, "cb"))
    up_r = Ring(ph.sbufs(2, [128, Rc + 2, 512], F32, "up"))
    acc_r = Ring(ph.sbufs(2, [128, Rc, 512], F32, "acc"))
    tmp_r = Ring(ph.sbufs(2, [128, Rc, 512], F32, "tmp"))
    out_r = Ring(ph.sbufs(2, [128, Rc, 512], BF16, "uc"))
    k = 0
    for g in range(3):
        P.dma("sp", wt.ap[:], cwb[:, :, g * 512:(g + 1) * 512], writes=[wt])
        P.dma("sp", bt.ap[:], cbb[:, g * 512:(g + 1) * 512], writes=[bt])
        for (PHYs, L, UC) in jobs:
            R = L // 128
            for rc in range(R // Rc):
                up = up_r.next()
                src = dram_ap(PHYs, (rc * Rc) * HYC + g * 512, [[R * HYC, 128], [HYC, Rc + 2], [1, 512]])
                P.dma("sp", up.ap[:], src, writes=[up])

                def wb(i):
                    return wt.ap[:, i, :].unsqueeze(1).broadcast_to([128, Rc, 512])

                bb = bt.ap[:].unsqueeze(1).broadcast_to([128, Rc, 512])
                e1 = "dve" if k % 3 != 2 else "pool"
                k += 1
                acc = acc_r.next()
                tmp = tmp_r.next()
                out = out_r.next()
                tt(P, e1, acc.ap[:], up.ap[:, 0:Rc, :], wb(0), ALU.mult, [up, wt], [acc])
                tt(P, e1, tmp.ap[:], up.ap[:, 1:Rc + 1, :], wb(1), ALU.mult, [up, wt], [tmp])
                tt(P, e1, acc.ap[:], acc.ap[:], tmp.ap[:], ALU.add, [acc, tmp], [acc])
                tt(P, e1, tmp.ap[:], up.ap[:, 2:Rc + 2, :], wb(2), ALU.mult, [up, wt], [tmp])
                tt(P, e1, acc.ap[:], acc.ap[:], tmp.ap[:], ALU.add, [acc, tmp], [acc])
                tt(P, e1, out.ap[:], acc.ap[:], bb, ALU.add, [acc, bt], [out])
                dst = dram_ap(UC, (rc * Rc) * HYC + g * 512, [[R * HYC, 128], [HYC, Rc], [1, 512]])
                P.dma("pool", dst, out.ap[:], reads=[out])
                yield


def hy_filter_hidden(P, L, zext, w1, f1c, b1c, w2d, f2c, b2c, H2D):
    N = 2 * L
    OFF = 16.0 * math.pi
    with Phase(P) as ph:
        w1t = Buf(ph.sb([33, 64], F32, "w1"))
        w2t = Buf(ph.sb([64, 128], F32, "w2"))
        f1 = Buf(ph.sb([64, 1], F32, "f1"))
        b1 = Buf(ph.sb([64, 1], F32, "b1"))
        f2 = Buf(ph.sb([128, 1], F32, "f2"))
        b2 = Buf(ph.sb([128, 1], F32, "b2"))
        for (b, a) in ((w1t, w1), (w2t, w2d), (f1, f1c), (b1, b1c), (f2, f2c), (b2, b2c)):
            P.dma("sp", b.ap[:], a, writes=[b])
        ts(P, "dve", b1.ap[:], b1.ap[:], f1.ap[:, 0:1], OFF, ALU.mult, ALU.add, [b1, f1], [b1])
        ts(P, "dve", b2.ap[:], b2.ap[:], f2.ap[:, 0:1], OFF, ALU.mult, ALU.add, [b2, f2], [b2])
        z_r = Ring(ph.sbufs(2, [33, 512], F32, "z"))
        a_r = Ring(ph.sbufs(3, [128, 512], F32, "arg"))
        i_r = Ring(ph.sbufs(3, [128, 512], I32, "iq"))
        q_r = Ring(ph.sbufs(3, [128, 512], F32, "fq"))
        h1_r = Ring(ph.sbufs(3, [64, 512], F32, "h1"))
        ho_r = Ring(ph.sbufs(2, [128, 512], BF16, "h2"))
        ps_r = Ring(ph.psbufs(4, [128, 512], F32, "ps"))
        def sin_layer(ps, npart, fr, bs, out_ap, outbuf):
            a = a_r.next()
            iq = i_r.next()
            fq = q_r.next()
            av, iv, fv = a.ap[0:npart, :], iq.ap[0:npart, :], fq.ap[0:npart, :]
            ts(P, "dve", av, ps.ap[0:npart, :], fr.ap[:, 0:1], bs.ap[:, 0:1], ALU.mult, ALU.add, [ps, fr, bs], [a])
            ts(P, "dve", iv, av, 1.0 / TWO_PI, None, ALU.mult, None, [a], [iq])
            cp(P, "dve", fv, iv, [iq], [fq])
            stt(P, "dve", av, fv, -TWO_PI, av, ALU.mult, ALU.add, [fq, a], [a])
            ts(P, "dve", fv, av, math.pi, TWO_PI, ALU.is_gt, ALU.mult, [a], [fq])
            tt(P, "dve", av, av, fv, ALU.subtract, [a, fq], [a])
            act(P, out_ap, av, AF.Sin, [a], [outbuf])

        s2_r = Ring(ph.sbufs(2, [128, 512], F32, "s2"))
        nch = N // 512
        h1s = {}

        def layer1(ch):
            c0 = ch * 512
            zt = z_r.next()
            P.dma("sp", zt.ap[:], zext[:, c0:c0 + 512], writes=[zt])
            p1 = ps_r.next()
            mm(P, p1.ap[0:64, :], w1t.ap[:], zt.ap[:], True, True, [w1t, zt], [p1])
            h1 = h1_r.next()
            sin_layer(p1, 64, f1, b1, h1.ap[:], h1)
            h1s[ch] = h1

        def layer2(ch):
            c0 = ch * 512
            h1 = h1s.pop(ch)
            p2 = ps_r.next()
            mm(P, p2.ap[:], w2t.ap[:], h1.ap[:], True, True, [w2t, h1], [p2])
            s2 = s2_r.next()
            sin_layer(p2, 128, f2, b2, s2.ap[:], s2)
            ho = ho_r.next()
            lo = 0 if c0 < L else 64
            P.op("pool", (lambda ho=ho: lambda e: e.memset(ho.ap[:], 0.0))(), writes=[ho])
            cp(P, "pool", ho.ap[lo:lo + 64, :], s2.ap[lo:lo + 64, :], [s2], [ho])
            P.dma("pool", H2D[:, c0:c0 + 512], ho.ap[:], reads=[ho])

        layer1(0)
        for ch in range(nch):
            if ch + 1 < nch:
                layer1(ch + 1)
            layer2(ch)


def fft_s1(P, N1, Kin, A_scrs, F1d, make_src):
    NK = N1 // 2 + 1
    ns = len(A_scrs)
    with Phase(P) as ph:
        F1t = Buf(ph.sb([128, 2, N1], BF16, "F1"))
        P.dma("sp", F1t.ap[0:Kin, :, :], F1d[0:Kin, :, :], writes=[F1t])
        chunk = make_src(ph)
        ps_r = Ring(ph.psbufs(4, [128, 512], F32, "s1p"))
        At_r = [Ring(ph.sbufs(2, [128, 2, 8, 512], BF16, "At")) for _ in range(ns)]
        ec = ["act", "dve"]
        for n2c in range(16):
            Ats = [r.next() for r in At_r]
            srcs = chunk(n2c)
            for j in range(8):
                for si in range(ns):
                    rhs, rb = srcs[j][si]
                    At = Ats[si]
                    for ri in range(2):
                        p = ps_r.next()
                        mm(P, p.ap[0:NK, :], F1t.ap[0:Kin, ri, 0:NK], rhs, True, True, [F1t, rb], [p])
                        cp(P, ec[0], At.ap[0:NK, ri, j, :], p.ap[0:NK, :], [p], [At])
                        ec.append(ec.pop(0))
            for si in range(ns):
                for ri in range(2):
                    P.dma("act" if ri == 0 else "pool", A_scrs[si][ri, n2c * 8:(n2c + 1) * 8, :, :].rearrange("n k c -> k n c"),
                          Ats[si].ap[0:NK, ri, :, :], reads=[Ats[si]])


def src_from_dram(P, U, L, c0):
    Kin = L // 128
    Uv = U.rearrange("(n1 n2) c -> n1 n2 c", n2=128)

    def make(ph):
        d_r = Ring(ph.sbufs(2, [128, 8, 512], BF16, "D"))

        def chunk(n2c):
            d = d_r.next()
            P.dma("sp", d.ap[0:Kin, :, :], Uv[:, n2c * 8:(n2c + 1) * 8, c0:c0 + 512], writes=[d])
            return [[(d.ap[0:Kin, j, :], d)] for j in range(8)]

        return chunk

    return make


def src_filter(P, L, H2D, w3s, tau, negd, hyb):
    N1 = 2 * L // 128

    def make(ph):
        H2 = Buf(ph.sb([128, 2 * L], BF16, "H2"))
        P.dma("sp", H2.ap[:], H2D, writes=[H2])
        W3 = Buf(ph.sb([128, 2, 512], BF16, "W3"))
        w3f = Buf(ph.sb([128, 2, 512], F32, "w3f"))
        P.dma("sp", w3f.ap[:], w3s, writes=[w3f])
        cp(P, "dve", W3.ap[:], w3f.ap[:], [w3f], [W3])
        taut = Buf(ph.sb([128, 128], F32, "tau"))
        P.dma("sp", taut.ap[:], tau, writes=[taut])
        ndt = Buf(ph.sb([128, 512], F32, "negd"))
        P.dma("sp", ndt.ap[:], negd, writes=[ndt])
        bia = Buf(ph.sb([1, 2, 512], F32, "hyb"))
        P.dma("sp", bia.ap[:], hyb.rearrange("(o f) c -> o f c", o=1), writes=[bia])
        kp_r = Ring(ph.psbufs(3, [128, 512], F32, "kp"))
        dec_r = Ring(ph.sbufs(3, [128, 512], F32, "dec"))
        kf_r = Ring(ph.sbufs(2, [128, 512], F32, "kf32"))
        kb_r = Ring(ph.sbufs(20, [128, 512], BF16, "kbf"))
        H2v = H2.ap[:].rearrange("p (n1 n2) -> p n2 n1", n2=128)

        def chunk(n2c):
            out = []
            for j in range(8):
                n2 = n2c * 8 + j
                dec = dec_r.next()
                act(P, dec.ap[0:N1, :], ndt.ap[0:N1, :], AF.Exp, [ndt, taut], [dec], scale=taut.ap[0:N1, n2:n2 + 1])
                row = []
                for f in range(2):
                    kp = kp_r.next()
                    mm(P, kp.ap[0:N1, :], H2v[:, n2, :], W3.ap[:, f, :], True, True, [H2, W3], [kp])
                    kb = kb_r.next()
                    if n2 == 0:
                        kf = kf_r.next()
                        tt(P, "dve", kf.ap[0:N1, :], kp.ap[0:N1, :], dec.ap[0:N1, :], ALU.mult, [kp, dec], [kf])
                        tt(P, "dve", kf.ap[0:1, :], kf.ap[0:1, :], bia.ap[:, f, :], ALU.add, [kf, bia], [kf])
                        cp(P, "dve", kb.ap[0:N1, :], kf.ap[0:N1, :], [kf], [kb])
                    else:
                        tt(P, "dve", kb.ap[0:N1, :], kp.ap[0:N1, :], dec.ap[0:N1, :], ALU.mult, [kp, dec], [kb])
                    row.append((kb.ap[0:N1, :], kb))
                out.append(row)
            return out

        return chunk

    return make


def fft_s2(P, N1, A_scr, Gd, KFf, mode, Gid=None, B_scr=None):
    KC = 4
    NK = N1 // 2 + 1
    chunks = [(k0, min(KC, NK - k0)) for k0 in range(0, NK, KC)]
    with Phase(P) as ph:
        At_r = Ring(ph.sbufs(2, [128, 2, KC, 512], BF16, "At"))
        Gt_r = Ring(ph.sbufs(2, [128, KC, 3, 128], BF16, "Gt"))
        Kt_r = Ring(ph.sbufs(2, [128, 2, KC, 512], BF16, "Kt"))
        X_r = Ring(ph.psbufs(2, [128, 2, 512], F32, "X"))
        if mode == "conv":
            Gi_r = Ring(ph.sbufs(2, [128, KC, 3, 128], BF16, "Git"))
            Yt_r = Ring(ph.sbufs(2, [128, 2, 512], BF16, "Yt"))
            Bt_r = Ring(ph.sbufs(2, [128, 2, KC, 512], BF16, "Bt"))
            tm_r = Ring(ph.sbufs(8, [128, 512], F32, "tm"))
            B_r = Ring(ph.psbufs(2, [128, 2, 512], F32, "B"))
        ctx = {}

        def chunk_ctx(ci):
            if ci not in ctx:
                k0, n = chunks[ci]
                At = At_r.next()
                for ri in range(2):
                    P.dma("sp", At.ap[:, ri, 0:n, :], A_scr[ri, :, k0:k0 + n, :], writes=[At])
                Gt = Gt_r.next()
                P.dma("sp", Gt.ap[:, 0:n, :, :], Gd[:, k0:k0 + n, :, :], writes=[Gt])
                Kt = Kt_r.next()
                Git = Bt = None
                if mode == "conv":
                    for ri in range(2):
                        P.dma("sp", Kt.ap[:, ri, 0:n, :], KFf[ri, :, k0:k0 + n, :], writes=[Kt])
                    Git = Gi_r.next()
                    P.dma("sp", Git.ap[:, 0:n, :, :], Gid[:, k0:k0 + n, :, :], writes=[Git])
                    Bt = Bt_r.next()
                ctx[ci] = (At, Gt, Kt, Git, Bt)
            return ctx[ci]

        planes = [(ci, j) for ci, (k0, n) in enumerate(chunks) for j in range(n)]
        Xs = {}

        def emit_X(g):
            ci, j = planes[g]
            At, Gt, Kt, Git, Bt = chunk_ctx(ci)
            X = X_r.next()
            rd = [Gt, At]
            mm(P, X.ap[:, 0, :], Gt.ap[:, j, 0, :], At.ap[:, 0, j, :], True, False, rd, [X], inc=False)
            mm(P, X.ap[:, 0, :], Gt.ap[:, j, 2, :], At.ap[:, 1, j, :], False, True, rd, [X], inc=False)
            mm(P, X.ap[:, 1, :], Gt.ap[:, j, 1, :], At.ap[:, 0, j, :], True, False, rd, [X], inc=False)
            mm(P, X.ap[:, 1, :], Gt.ap[:, j, 0, :], At.ap[:, 1, j, :], False, True, rd, [X])
            Xs[g] = X

        emit_X(0)
        for g in range(len(planes)):
            ci, j = planes[g]
            k0, n = chunks[ci]
            At, Gt, Kt, Git, Bt = chunk_ctx(ci)
            if g + 1 < len(planes):
                emit_X(g + 1)
            X = Xs.pop(g)
            if mode == "filter":
                cp(P, "act" if g % 2 == 0 else "dve", Kt.ap[:, :, j, :], X.ap[:], [X], [Kt])
                if j == n - 1:
                    for ri in range(2):
                        P.dma("pool", KFf[ri, :, k0:k0 + n, :], Kt.ap[:, ri, 0:n, :], reads=[Kt])
                    ctx.pop(ci)
                continue
            t1, t2, t3, t4 = tm_r.next(), tm_r.next(), tm_r.next(), tm_r.next()
            tt(P, "dve", t1.ap[:], X.ap[:, 0, :], Kt.ap[:, 0, j, :], ALU.mult, [X, Kt], [t1])
            tt(P, "dve", t2.ap[:], X.ap[:, 1, :], Kt.ap[:, 1, j, :], ALU.mult, [X, Kt], [t2])
            tt(P, "dve", t3.ap[:], X.ap[:, 0, :], Kt.ap[:, 1, j, :], ALU.mult, [X, Kt], [t3])
            tt(P, "dve", t4.ap[:], X.ap[:, 1, :], Kt.ap[:, 0, j, :], ALU.mult, [X, Kt], [t4])
            Yt = Yt_r.next()
            tt(P, "pool", Yt.ap[:, 0, :], t1.ap[:], t2.ap[:], ALU.subtract, [t1, t2], [Yt])
            tt(P, "dve", Yt.ap[:, 1, :], t3.ap[:], t4.ap[:], ALU.add, [t3, t4], [Yt])
            B = B_r.next()
            rd = [Git, Yt]
            mm(P, B.ap[:, 0, :], Git.ap[:, j, 0, :], Yt.ap[:, 0, :], True, False, rd, [B], inc=False)
            mm(P, B.ap[:, 0, :], Git.ap[:, j, 2, :], Yt.ap[:, 1, :], False, True, rd, [B], inc=False)
            mm(P, B.ap[:, 1, :], Git.ap[:, j, 1, :], Yt.ap[:, 0, :], True, False, rd, [B], inc=False)
            mm(P, B.ap[:, 1, :], Git.ap[:, j, 0, :], Yt.ap[:, 1, :], False, True, rd, [B])
            cp(P, "act", Bt.ap[:, :, j, :], B.ap[:], [B], [Bt])
            if j == n - 1:
                for ri in range(2):
                    P.dma("act", B_scr[ri, k0:k0 + n, :, :].rearrange("k n c -> n k c"), Bt.ap[:, ri, 0:n, :], reads=[Bt])
                ctx.pop(ci)


def fft_s4(P, N1, L, B_scr, F4d, gate_src, gc0, dst, dc0):
    Kh = N1 // 2
    NK = N1 // 2 + 1
    gv = gate_src.rearrange("(n1 n2) c -> n1 n2 c", n2=128)
    dv = dst.rearrange("(n1 n2) c -> n1 n2 c", n2=128)
    with Phase(P) as ph:
        F4t = Buf(ph.sb([128, 2, Kh], BF16, "F4"))
        P.dma("sp", F4t.ap[0:NK, :, :], F4d, writes=[F4t])
        Bt_r = Ring(ph.sbufs(2, [128, 2, 8, 512], BF16, "Bt4"))
        g_r = Ring(ph.sbufs(2, [128, 8, 512], BF16, "gate"))
        z_r = Ring(ph.sbufs(2, [128, 8, 512], BF16, "zt"))
        ps_r = Ring(ph.psbufs(4, [128, 512], F32, "s4p"))
        for n2c in range(16):
            Bt = Bt_r.next()
            for ri in range(2):
                P.dma("sp", Bt.ap[0:NK, ri, :, :], B_scr[ri, :, n2c * 8:(n2c + 1) * 8, :], writes=[Bt])
            g = g_r.next()
            P.dma("sp", g.ap[0:Kh, :, :], gv[:, n2c * 8:(n2c + 1) * 8, gc0:gc0 + 512], writes=[g])
            z = z_r.next()
            for j in range(8):
                p = ps_r.next()
                mm(P, p.ap[0:Kh, :], F4t.ap[0:NK, 0, :], Bt.ap[0:NK, 0, j, :], True, False, [F4t, Bt], [p], inc=False)
                mm(P, p.ap[0:Kh, :], F4t.ap[0:NK, 1, :], Bt.ap[0:NK, 1, j, :], False, True, [F4t, Bt], [p])
                tt(P, "dve", z.ap[0:Kh, j, :], p.ap[0:Kh, :], g.ap[0:Kh, j, :], ALU.mult, [p, g], [z])
            P.dma("act", dv[:, n2c * 8:(n2c + 1) * 8, dc0:dc0 + 512], z.ap[0:Kh, :, :], reads=[z])


def phase_hyena(P, s0, L, PHYs, hw, ZH, scr, cL):
    N1 = 2 * L // 128
    Kin = N1 // 2
    hy_filter_hidden(P, L, cL["zext"], hw["w1"], hw["f1c"], hw["b1c"], hw["w2d"], hw["f2c"], hw["b2c"], scr["H2D"])
    fft_s1(P, N1, N1, [scr["A"], scr["A2"]], cL["F1"], src_filter(P, L, scr["H2D"], hw["w3s"], cL["tau"], cL["negd"], hw["hyb"]))
    fft_s2(P, N1, scr["A"], cL["G"], scr["KF"][0], "filter")
    fft_s2(P, N1, scr["A2"], cL["G"], scr["KF"][1], "filter")
    fft_s1(P, N1, Kin, [scr["A"]], cL["F1"], src_from_dram(P, scr["UC"], L, 1024))
    fft_s2(P, N1, scr["A"], cL["G"], scr["KF"][0], "conv", cL["Gi"], scr["B"])
    fft_s4(P, N1, L, scr["B"], cL["F4"], scr["UC"], 0, scr["U1"], 0)
    fft_s1(P, N1, Kin, [scr["A"]], cL["F1"], src_from_dram(P, scr["U1"], L, 0))
    fft_s2(P, N1, scr["A"], cL["G"], scr["KF"][1], "conv", cL["Gi"], scr["B"])
    fft_s4(P, N1, L, scr["B"], cL["F4"], scr["UC"], 512, ZH[s0:s0 + L, :], 0)


_CONST_CACHE = {}


def make_consts(LP, LS):
    key = (LP, LS)
    if key in _CONST_CACHE:
        return _CONST_CACHE[key]
    c = {}
    c["ident"] = np.eye(128, dtype=np.float32)
    Lm = max(LP, LS)
    inv = (np.float32(1.0) / (np.float32(10000.0) ** (np.arange(0, 32, 2, dtype=np.float32) / np.float32(32)))).astype(np.float32)
    ang = (np.arange(Lm, dtype=np.float32)[:, None] * inv[None, :]).astype(np.float32)
    cos = np.cos(ang).astype(np.float32).T
    sin = np.sin(ang).astype(np.float32).T
    cos32 = np.concatenate([cos, cos], 0)
    sin32 = np.concatenate([-sin, sin], 0)
    sc = np.float32(96.0 ** -0.5)
    c["rope"] = np.ascontiguousarray(np.stack([cos32 * sc, sin32 * sc, cos32, sin32], 0).astype(np.float32))
    Lt = 384
    M = np.zeros((4, 5, 128, 128), np.float64)
    for g, w in enumerate((2, 4, 8, 16)):
        Mf = np.zeros((Lt, Lt))
        for t in range(Lt):
            lo = min(max(t - w // 2, 0), Lt)
            hi = min(max(t + w // 2, 0), Lt)
            Mf[t, lo:hi] = 1.0 / (hi - lo)
            Mf[t, t] -= 1.0
        M[g, 0] = Mf[128:256, 0:128].T
        M[g, 1] = Mf[128:256, 128:256].T
        M[g, 2] = Mf[128:256, 256:384].T
        M[g, 3] = Mf[0:128, 0:128].T
        M[g, 4] = Mf[256:384, 256:384].T
    c["poolM"] = np.ascontiguousarray(M.transpose(2, 0, 1, 3)).astype(NPBF)
    for L in sorted(set((LP, LS))):
        N = 2 * L
        N1 = N // 128
        n1 = np.arange(N1)
        th = 2 * np.pi * ((n1[:, None] * n1[None, :]) % N1) / N1
        c["F1_%d" % L] = np.stack([np.cos(th), -np.sin(th)], 1).astype(NPBF)
        NK = N1 // 2 + 1
        thi = th[:NK, :N1 // 2]
        wg = np.full((NK, 1), 2.0)
        wg[0] = 1.0
        wg[N1 // 2] = 1.0
        c["F4_%d" % L] = np.stack([wg * np.cos(thi) / N, -wg * np.sin(thi) / N], 1).astype(NPBF)
        n2 = np.arange(128)[:, None, None]
        k1 = np.arange(N1)[None, :, None]
        k2 = np.arange(128)[None, None, :]
        tg = 2 * np.pi * ((n2 * (k1 + N1 * k2)) % N) / N
        cg, sg = np.cos(tg), np.sin(tg)
        c["G_%d" % L] = np.stack([cg, -sg, sg], 2).astype(NPBF)
        cgt, sgt = cg.transpose(2, 1, 0), sg.transpose(2, 1, 0)
        c["Gi_%d" % L] = np.stack([cgt, sgt, -sgt], 2).astype(NPBF)
        t = np.linspace(0.0, 1.0, L)
        wpos = (2.0 * np.pi / L) * np.arange(L)
        bands = np.linspace(1e-4, 15.0, 16)
        z = np.concatenate([t[:, None], np.cos(bands[None, :] * wpos[:, None]), -np.sin(bands[None, :] * wpos[:, None])], -1)
        idx = np.arange(N)
        src = np.where(idx < L, idx, (N - idx) % L)
        c["zext_%d" % L] = np.ascontiguousarray(z[src].T).astype(np.float32)
        tau = np.where(idx < L, idx, N - idx).astype(np.float64)
        tau[L] = 1e9
        taut = np.zeros((128, 128), np.float32)
        taut[:N1] = tau.reshape(N1, 128)
        c["tau_%d" % L] = taut
        deltas = np.abs(np.linspace(math.log(1e-2) / 1.5, math.log(1e-2) / 0.3, 512))
        c["negd_%d" % L] = np.ascontiguousarray(np.broadcast_to((-deltas / (L - 1)).astype(np.float32), (128, 512)))
    _CONST_CACHE[key] = c
    return c


def layout_weights(w):
    f = np.float32
    o = {}

    def colT(v, k):
        v = np.asarray(v, f)
        return np.ascontiguousarray(v.reshape(v.shape[0], k, 128).transpose(0, 2, 1))

    def rep(v):
        v = np.asarray(v, f)
        return np.ascontiguousarray(np.broadcast_to(v[..., None, :], v.shape[:-1] + (128, v.shape[-1])))

    o["mix_norm_T"] = colT(w["mix_norm"], 8)
    o["mlp_norm_T"] = colT(w["mlp_norm"], 8)
    o["q_norm_T"] = colT(w["q_norm"], 3)
    o["kv_norm_T"] = colT(w["kv_norm"], 2)
    o["final_b"] = rep(np.asarray(w["final_norm"], f))
    o["pscale_b"] = rep(w["pool_scale"])
    for k in ("w_in", "w_uq", "w_ukv", "w_out", "pool_w", "mlp_w1", "mlp_w2", "hf_w1"):
        o[k] = np.ascontiguousarray(np.asarray(w[k], f))
    cw = np.asarray(w["hy_conv_w"], f)
    o["cwb"] = np.ascontiguousarray(np.broadcast_to(cw[:, None], (cw.shape[0], 128, 3, HYC)))
    o["cbb"] = rep(w["hy_conv_b"])
    o["f1c"] = np.ascontiguousarray(np.asarray(w["hf_freq"], f)[:, :, None])
    o["b1c"] = np.ascontiguousarray(np.asarray(w["hf_b1"], f)[:, :, None])
    o["f2c"] = np.ascontiguousarray(np.concatenate([np.asarray(w["hf_freq"], f)] * 2, 1)[:, :, None])
    o["b2c"] = np.ascontiguousarray(np.concatenate([np.asarray(w["hf_b2"], f)] * 2, 1)[:, :, None])
    o["w2d"] = np.ascontiguousarray(np.concatenate([np.asarray(w["hf_w2"], f)] * 2, 2))
    w3 = np.asarray(w["hf_w3"], f).reshape(-1, 64, 2, 2, 512)
    o["w3s"] = np.ascontiguousarray(w3.transpose(0, 2, 1, 3, 4).reshape(-1, 128, 2, 512))
    o["hyb"] = np.ascontiguousarray(np.asarray(w["hy_bias"], f))
    return o


WSHAPES = {
    "mix_norm_T": [4, 128, 8], "mlp_norm_T": [4, 128, 8], "q_norm_T": [2, 128, 3], "kv_norm_T": [2, 128, 2],
    "final_b": [128, D], "pscale_b": [2, 128, D], "w_in": [2, D, INC], "w_uq": [2, QL, 768], "w_ukv": [2, KVL, 1024],
    "w_out": [2, D, D], "pool_w": [2, 4, 256, 256], "mlp_w1": [4, D, DFF], "mlp_w2": [4, DFF, D], "hf_w1": [2, 33, 64],
    "cwb": [2, 128, 3, HYC], "cbb": [2, 128, HYC], "f1c": [2, 64, 1], "b1c": [2, 64, 1], "f2c": [2, 128, 1],
    "b2c": [2, 128, 1], "w2d": [2, 64, 128], "w3s": [2, 128, 2, 512], "hyb": [2, 2, 512],
}


def build(LP, LS, layers=(0, 1, 2, 3), final=True, consts_np=None):
    nc = bass.Bass("TRN2", target_bir_lowering=False)
    T = LP + LS
    seqs = [(0, LP), (LP, LS)]

    def din(name, shape, dt=F32):
        return nc.dram_tensor(name, list(shape), dt, kind="ExternalInput").ap()

    def dscr(name, shape, dt):
        return nc.dram_tensor(name, list(shape), dt, kind="Internal").ap()

    xP = din("xP", [LP, D])
    xS = din("xS", [LS, D])
    W = {k: din(k, v) for k, v in WSHAPES.items()}
    C = {}
    for k, v in consts_np.items():
        C[k] = din("c_" + k, v.shape, BF16 if v.dtype == NPBF else F32)
    yP = nc.dram_tensor("yP", [LP, D], F32, kind="ExternalOutput").ap()
    yS = nc.dram_tensor("yS", [LS, D], F32, kind="ExternalOutput").ap()
    X = dscr("X", [T, D], F32)
    PHY = [dscr("PHY%d" % i, [L + 2, HYC], F32) for i, (s0, L) in enumerate(seqs)]
    QT = dscr("QT", [NH, 96, T], BF16)
    KT = dscr("KT", [NH, 96, T], BF16)
    KR = dscr("KR", [32, T], BF16)
    V = dscr("V", [T, NH * 65], BF16)
    ZH = dscr("ZH", [T, 512], BF16)
    OT = dscr("OT", [512, T], BF16)
    RCD = dscr("RCD", [8, 512], F32)
    hscr = []
    for i, (s0, L) in enumerate(seqs):
        N1 = 2 * L // 128
        hscr.append({
            "A": dscr("hA%d" % i, [2, 128, N1 // 2 + 1, 512], BF16), "A2": dscr("hA2%d" % i, [2, 128, N1 // 2 + 1, 512], BF16),
            "B": dscr("hB%d" % i, [2, N1 // 2 + 1, 128, 512], BF16),
            "KF": [dscr("hKF%d_%d" % (i, f), [2, 128, N1 // 2 + 1, 512], BF16) for f in range(2)],
            "UC": dscr("hUC%d" % i, [L, HYC], BF16), "U1": dscr("hU1%d" % i, [L, 512], BF16),
            "H2D": dscr("hH2%d" % i, [128, 2 * L], BF16)})
    P = Prog(nc)
    with Phase(P) as ph:
        RB = 1024
        for (s0, L), src in zip(seqs, (xP, xS)):
            for r in range(0, L, RB):
                n = min(RB, L - r)
                P.dma("sp", X[s0 + r:s0 + r + n, :], src[r:r + n, :])
    outs = [(0, LP, yP), (LP, LS, yS)]
    for li, i in enumerate(layers):
        last = (li == len(layers) - 1)
        if i % 2 == 0:
            e = i // 2
            phase_proj(P, X, seqs, W["w_in"][e], W["w_uq"][e], W["w_ukv"][e], W["mix_norm_T"][i], W["q_norm_T"][e],
                       W["kv_norm_T"][e], PHY, QT, KT, KR, V, C)
            hw = {"cwb": W["cwb"][e], "cbb": W["cbb"][e], "w1": W["hf_w1"][e], "f1c": W["f1c"][e], "b1c": W["b1c"][e],
                  "w2d": W["w2d"][e], "f2c": W["f2c"][e], "b2c": W["b2c"][e], "w3s": W["w3s"][e], "hyb": W["hyb"][e]}
            jobs = [(PHY[si], L, hscr[si]["UC"]) for si, (s0, L) in enumerate(seqs)]
            phase_attn(P, seqs, QT, KT, KR, V, OT, RCD, side=lambda ph: shortconv_gen(P, ph, jobs, hw["cwb"], hw["cbb"]))
            for si, (s0, L) in enumerate(seqs):
                cL = {k: C["%s_%d" % (k, L)] for k in ("F1", "F4", "G", "Gi", "zext", "tau", "negd")}
                phase_hyena(P, s0, L, PHY[si], hw, ZH, hscr[si], cL)
            phase_wout(P, X, T, ZH, OT, W["w_out"][e], C)
        else:
            o = i // 2
            phase_pool(P, X, seqs, W["pool_w"][o], W["mix_norm_T"][i], W["pscale_b"][o], C)
        fin = (W["final_b"], outs) if (last and final) else None
        phase_mlp(P, X, T, W["mlp_w1"][i], W["mlp_w2"][i], W["mlp_norm_T"][i], C, final=fin)
    if not final:
        with Phase(P) as ph:
            for (r0, nr, oap) in outs:
                for r in range(0, nr, 1024):
                    n = min(1024, nr - r)
                    P.dma("sp", oap[r:r + n, :], X[r0 + r:r0 + r + n, :])
    P.flush(final=True)
    return nc


_NC_CACHE = {}


def kernel(**inputs):
    LP, LS = 8192, 4096
    consts = make_consts(LP, LS)
    key = (LP, LS)
    if key not in _NC_CACHE:
        _NC_CACHE[key] = build(LP, LS, consts_np=consts)
    nc = _NC_CACHE[key]
    wl = layout_weights(inputs)
    xp = np.asarray(inputs["x_prompt"], np.float32)
    xs = np.asarray(inputs["x_sample"], np.float32)
    base = dict(wl)
    for k, v in consts.items():
        base["c_" + k] = v
    in_maps = []
    for c in range(8):
        m = dict(base)
        m["xP"] = np.ascontiguousarray(xp[c])
        m["xS"] = np.ascontiguousarray(xs[c // 2])
        in_maps.append(m)
    res = run_bass_kernel_spmd(nc, in_maps, core_ids=list(range(8)))
    yp = np.stack([np.asarray(res.results[c]["yP"], np.float32) for c in range(8)], 0)
    hl = LS // 2
    ys = np.stack([np.concatenate([np.asarray(res.results[2 * j]["yS"], np.float32)[:hl],
                                   np.asarray(res.results[2 * j + 1]["yS"], np.float32)[hl:]], 0) for j in range(4)], 0)
    return (yp, ys)
```
